# Optimizing a Trainium2 kernel written in Bass

```python
import jax, jax.numpy as jnp
from jax import lax
import numpy as np

D_MODEL = 1024
BATCH = 8
SEQ = 2048
DEPTH = 4

MEM_LEN = 256
MIX_WIDTH = D_MODEL
HGRN_WIDTH = MIX_WIDTH // 2
MLSTM_WIDTH = MIX_WIDTH - HGRN_WIDTH
HGRN_HEADS = 4
HGRN_HEAD_DIM = HGRN_WIDTH // HGRN_HEADS
MLSTM_HEADS = 4
MLSTM_HEAD_DIM = MLSTM_WIDTH // MLSTM_HEADS
CONV_WIDTH = 5
XATTN_HEADS = 4
XATTN_HEAD_DIM = D_MODEL // XATTN_HEADS
D_FF = 4 * D_MODEL
CHUNK = 64
NORM_EPS = 1e-6
MLSTM_FGATE_BIAS = 3.0
IN_SPLITS = (HGRN_WIDTH, HGRN_WIDTH, HGRN_WIDTH, HGRN_WIDTH, HGRN_WIDTH,
             MLSTM_WIDTH, MLSTM_WIDTH, MLSTM_WIDTH, MLSTM_WIDTH,
             MLSTM_HEADS, MLSTM_HEADS, MLSTM_HEADS, MLSTM_HEADS)
D_IN = 5 * HGRN_WIDTH + 4 * MLSTM_WIDTH + 4 * MLSTM_HEADS

kernel_name = "bidir_hgrn2_mlstm_hybrid_encoder"


def rmsnorm(x, g):
    x32 = x.astype(jnp.float32)
    y = x32 * lax.rsqrt(jnp.mean(x32 * x32, axis=-1, keepdims=True) + NORM_EPS)
    return (y * g.astype(jnp.float32)).astype(x.dtype)


def head_rmsnorm(h, g, n_heads):
    b, s, w = h.shape
    h32 = h.astype(jnp.float32).reshape(b, s, n_heads, w // n_heads)
    h32 = h32 * lax.rsqrt(jnp.mean(h32 * h32, axis=-1, keepdims=True) + NORM_EPS)
    return (h32.reshape(b, s, w) * g.astype(jnp.float32)).astype(h.dtype)


def split_cols(t, sizes):
    idx = np.cumsum(np.array(sizes))[:-1].tolist()
    return jnp.split(t, idx, axis=-1)


def to_heads(t, n_heads):
    b, s, w = t.shape
    return t.reshape(b, s, n_heads, w // n_heads).transpose(0, 2, 1, 3)


def from_heads(t):
    b, h, s, d = t.shape
    return t.transpose(0, 2, 1, 3).reshape(b, s, h * d)


def to_chunks(t):
    b, h, s = t.shape[:3]
    t = t.reshape(b, h, s // CHUNK, CHUNK, *t.shape[3:])
    return jnp.moveaxis(t, 2, 0)


def from_chunks(t):
    t = jnp.moveaxis(t, 0, 2)
    b, h, n, c = t.shape[:4]
    return t.reshape(b, h, n * c, *t.shape[4:])


def flip_seq(t):
    return jnp.flip(t, axis=2)


def hgrn2_scan(q, k, v, log_f):
    b, h, _, dk = q.shape
    dv = v.shape[-1]
    tril = jnp.tril(jnp.ones((CHUNK, CHUNK), dtype=bool))

    def step(state, xs):
        qc, kc, vc, lfc = xs
        cum = jnp.cumsum(lfc, axis=-2)
        o_inter = jnp.einsum('bhtd,bhde->bhte', qc * jnp.exp(cum), state)
        diff = cum[:, :, :, None, :] - cum[:, :, None, :, :]
        decay = jnp.exp(jnp.where(tril[:, :, None], diff, -jnp.inf))
        scores = jnp.einsum('bhtd,bhsd,bhtsd->bhts', qc, kc, decay)
        o_intra = jnp.einsum('bhts,bhse->bhte', scores, vc)
        cum_last = cum[:, :, -1:, :]
        k_dec = kc * jnp.exp(cum_last - cum)
        state = (jnp.exp(cum_last[:, :, 0, :])[..., None] * state
                 + jnp.einsum('bhsd,bhse->bhde', k_dec, vc))
        return state, o_inter + o_intra

    s0 = jnp.zeros((b, h, dk, dv), jnp.float32)
    xs = tuple(to_chunks(t.astype(jnp.float32)) for t in (q, k, v, log_f))
    _, o = lax.scan(step, s0, xs)
    return from_chunks(o)


def mlstm_scan(q, k, v, ig, log_fg):
    b, h, _, d = q.shape
    tril = jnp.tril(jnp.ones((CHUNK, CHUNK), dtype=bool))

    def step(carry, xs):
        c_st, n_st, m = carry
        qc, kc, vc, igc, lfc = xs
        cum = jnp.cumsum(lfc, axis=-1)
        log_d = cum[..., :, None] - cum[..., None, :] + igc[..., None, :]
        log_d = jnp.where(tril, log_d, -jnp.inf)
        log_inter = cum + m[..., None]
        m_t = jnp.maximum(jnp.max(log_d, axis=-1), log_inter)
        d_mat = jnp.exp(log_d - m_t[..., None])
        w_inter = jnp.exp(log_inter - m_t)
        scores = jnp.einsum('bhtd,bhsd->bhts', qc, kc) * d_mat
        num = (jnp.einsum('bhts,bhse->bhte', scores, vc)
               + w_inter[..., None] * jnp.einsum('bhtd,bhde->bhte', qc, c_st))
        den = jnp.sum(scores, axis=-1) + w_inter * jnp.einsum('bhtd,bhd->bht', qc, n_st)
        h_out = num / jnp.maximum(jnp.abs(den), jnp.exp(-m_t))[..., None]
        cum_last = cum[..., -1]
        log_w = cum_last[..., None] - cum + igc
        m_new = jnp.maximum(cum_last + m, jnp.max(log_w, axis=-1))
        carry_w = jnp.exp(cum_last + m - m_new)
        k_w = kc * jnp.exp(log_w - m_new[..., None])[..., None]
        c_st = carry_w[..., None, None] * c_st + jnp.einsum('bhsd,bhse->bhde', k_w, vc)
        n_st = carry_w[..., None] * n_st + jnp.sum(k_w, axis=-2)
        return (c_st, n_st, m_new), h_out

    init = (jnp.zeros((b, h, d, d), jnp.float32),
            jnp.zeros((b, h, d), jnp.float32),
            jnp.zeros((b, h), jnp.float32))
    xs = tuple(to_chunks(t.astype(jnp.float32)) for t in (q, k, v, ig, log_fg))
    _, hs = lax.scan(step, init, xs)
    return from_chunks(hs)


def layer_lower_bounds(logits):
    p = jax.nn.softmax(logits.astype(jnp.float32), axis=1)
    c = jnp.cumsum(p, axis=1)
    return c - c[:, :1]


def lower_bounded_log_forget(z, lb):
    z = z.astype(jnp.float32)
    return jnp.logaddexp(jnp.log(lb), jnp.log1p(-lb) + jax.nn.log_sigmoid(z))


def centred_dwconv(x, w, b):
    y = lax.conv_general_dilated(
        x, w[:, None, :], window_strides=(1,),
        padding=[(CONV_WIDTH // 2, CONV_WIDTH // 2)],
        dimension_numbers=('NWC', 'WIO', 'NWC'),
        feature_group_count=x.shape[-1])
    return y + b


def token_mixer(xn, w_in, b_in, lb_fwd, lb_bwd, conv_w, conv_b, hgrn_g, mlstm_g, w_out):
    proj = xn @ w_in + b_in
    (h_q, h_f_fwd, h_f_bwd, h_i, h_g,
     m_q, m_k, m_v, m_o,
     m_ig_fwd, m_ig_bwd, m_fg_fwd, m_fg_bwd) = split_cols(proj, IN_SPLITS)

    q_h = to_heads(jax.nn.silu(h_q), HGRN_HEADS)
    v_h = to_heads(h_i, HGRN_HEADS)
    lf_fwd = to_heads(lower_bounded_log_forget(h_f_fwd, lb_fwd), HGRN_HEADS)
    lf_bwd = to_heads(lower_bounded_log_forget(h_f_bwd, lb_bwd), HGRN_HEADS)
    k_fwd = -jnp.expm1(lf_fwd)
    k_bwd = -jnp.expm1(lf_bwd)
    o_h = (hgrn2_scan(q_h, k_fwd, v_h, lf_fwd)
           + flip_seq(hgrn2_scan(flip_seq(q_h), flip_seq(k_bwd), flip_seq(v_h), flip_seq(lf_bwd))))
    hgrn_out = head_rmsnorm(from_heads(o_h).astype(xn.dtype), hgrn_g, HGRN_HEADS) * jax.nn.silu(h_g)

    qk = jax.nn.silu(centred_dwconv(jnp.concatenate([m_q, m_k], axis=-1), conv_w, conv_b))
    m_q, m_k = jnp.split(qk, 2, axis=-1)
    q_m = to_heads(m_q, MLSTM_HEADS)
    k_m = to_heads(m_k, MLSTM_HEADS) * (MLSTM_HEAD_DIM ** -0.5)
    v_m = to_heads(m_v, MLSTM_HEADS)
    ig_fwd = jnp.swapaxes(m_ig_fwd, 1, 2).astype(jnp.float32)
    ig_bwd = jnp.swapaxes(m_ig_bwd, 1, 2).astype(jnp.float32)
    lfg_fwd = jax.nn.log_sigmoid(jnp.swapaxes(m_fg_fwd, 1, 2).astype(jnp.float32))
    lfg_bwd = jax.nn.log_sigmoid(jnp.swapaxes(m_fg_bwd, 1, 2).astype(jnp.float32))
    h_m = (mlstm_scan(q_m, k_m, v_m, ig_fwd, lfg_fwd)
           + flip_seq(mlstm_scan(flip_seq(q_m), flip_seq(k_m), flip_seq(v_m),
                                 flip_seq(ig_bwd), flip_seq(lfg_bwd))))
    mlstm_out = head_rmsnorm(from_heads(h_m).astype(xn.dtype), mlstm_g, MLSTM_HEADS) * jax.nn.sigmoid(m_o)

    return jnp.concatenate([hgrn_out, mlstm_out], axis=-1) @ w_out


def cross_attention(xn, memn, w_q, w_kv, w_o):
    b, s, _ = xn.shape
    q = (xn @ w_q).reshape(b, s, XATTN_HEADS, XATTN_HEAD_DIM)
    k, v = jnp.split(memn @ w_kv, 2, axis=-1)
    k = k.reshape(b, -1, XATTN_HEADS, XATTN_HEAD_DIM)
    v = v.reshape(b, -1, XATTN_HEADS, XATTN_HEAD_DIM)
    scores = jnp.einsum('bqhd,bkhd->bhqk', q, k).astype(jnp.float32) * (XATTN_HEAD_DIM ** -0.5)
    p = jax.nn.softmax(scores, axis=-1).astype(v.dtype)
    o = jnp.einsum('bhqk,bkhd->bqhd', p, v).reshape(b, s, D_MODEL)
    return o @ w_o


def squared_relu_mlp(xn, w_up, w_down):
    return jnp.square(jax.nn.relu(xn @ w_up)) @ w_down


def setup_inputs(seed: int = 0) -> dict:
    key = jax.random.key(seed)
    ks = jax.random.split(key, 20)

    def nrm(k, shape, scale):
        return jax.random.normal(k, shape, jnp.float32) * scale

    def gain(k, shape):
        return 1.0 + 0.05 * jax.random.normal(k, shape, jnp.float32)

    fgate_offset = jnp.concatenate([jnp.zeros((D_IN - 2 * MLSTM_HEADS,), jnp.float32),
                                    jnp.full((2 * MLSTM_HEADS,), MLSTM_FGATE_BIAS, jnp.float32)])
    return {
        "x": nrm(ks[0], (BATCH, SEQ, D_MODEL), 1.0),
        "mem": nrm(ks[1], (BATCH, MEM_LEN, D_MODEL), 1.0),
        "norm_mix": gain(ks[2], (DEPTH, D_MODEL)),
        "norm_xattn": gain(ks[3], (DEPTH, D_MODEL)),
        "norm_mem": gain(ks[4], (DEPTH, D_MODEL)),
        "norm_mlp": gain(ks[5], (DEPTH, D_MODEL)),
        "norm_final": gain(ks[6], (D_MODEL,)),
        "w_in": nrm(ks[7], (DEPTH, D_MODEL, D_IN), D_MODEL ** -0.5),
        "b_in": nrm(ks[8], (DEPTH, D_IN), 0.02) + fgate_offset,
        "hgrn_lb_logits": nrm(ks[9], (2, DEPTH, HGRN_WIDTH), 0.5),
        "hgrn_norm": gain(ks[10], (DEPTH, HGRN_WIDTH)),
        "mlstm_conv_w": nrm(ks[11], (DEPTH, CONV_WIDTH, 2 * MLSTM_WIDTH), CONV_WIDTH ** -0.5),
        "mlstm_conv_b": nrm(ks[12], (DEPTH, 2 * MLSTM_WIDTH), 0.02),
        "mlstm_norm": gain(ks[13], (DEPTH, MLSTM_WIDTH)),
        "w_out": nrm(ks[14], (DEPTH, MIX_WIDTH, D_MODEL), MIX_WIDTH ** -0.5),
        "w_xq": nrm(ks[15], (DEPTH, D_MODEL, D_MODEL), D_MODEL ** -0.5),
        "w_xkv": nrm(ks[16], (DEPTH, D_MODEL, 2 * D_MODEL), D_MODEL ** -0.5),
        "w_xo": nrm(ks[17], (DEPTH, D_MODEL, D_MODEL), D_MODEL ** -0.5),
        "w_up": nrm(ks[18], (DEPTH, D_MODEL, D_FF), D_MODEL ** -0.5),
        "w_down": nrm(ks[19], (DEPTH, D_FF, D_MODEL), D_FF ** -0.5),
    }


def reference(x, mem, norm_mix, norm_xattn, norm_mem, norm_mlp, norm_final, w_in, b_in,
              hgrn_lb_logits, hgrn_norm, mlstm_conv_w, mlstm_conv_b, mlstm_norm, w_out,
              w_xq, w_xkv, w_xo, w_up, w_down):
    lb = layer_lower_bounds(hgrn_lb_logits)
    for l in range(DEPTH):
        x = x + token_mixer(rmsnorm(x, norm_mix[l]), w_in[l], b_in[l], lb[0, l], lb[1, l],
                            mlstm_conv_w[l], mlstm_conv_b[l], hgrn_norm[l], mlstm_norm[l], w_out[l])
        x = x + cross_attention(rmsnorm(x, norm_xattn[l]), rmsnorm(mem, norm_mem[l]),
                                w_xq[l], w_xkv[l], w_xo[l])
        x = x + squared_relu_mlp(rmsnorm(x, norm_mlp[l]), w_up[l], w_down[l])
    return rmsnorm(x, norm_final)
```

```python
import numpy as np
from contextlib import ExitStack
import concourse.bass as bass
import concourse.mybir as mybir
from concourse.bass_utils import run_bass_kernel_spmd

F32 = mybir.dt.float32
BF16 = mybir.dt.bfloat16
AF = mybir.ActivationFunctionType
ALU = mybir.AluOpType
AX = mybir.AxisListType

P = 128
T = 2048
D = 1024
KC = 8
NG = 4
GS = 512
NCH = 32
CH = 64
L = 4
MEM = 256
DFF = 4096
EPS = 1e-6
NCORES = 8

PF_GN = 0
PF_BIN = PF_GN + L * 32 + 8
PF_LBL = PF_BIN + L * 36
PF_HGN = PF_LBL + 32
PF_MLN = PF_HGN + L * 4
PF_CW = PF_MLN + L * 4
PF_CB = PF_CW + L * 40
NPF = PF_CB + L * 8


class Tok:
    __slots__ = ("w", "r")

    def __init__(self):
        self.w = None
        self.r = {}


class Eng:
    def __init__(self, k, eng, name, ordered=False):
        self.k = k
        self.eng = eng
        self.name = name
        self.ordered = ordered
        self.sem = None
        self.epoch = 0
        self.cnt = 0
        self.seen = {}
        self.pending = False


EPOCH_LIMIT = 30000


class K:
    def __init__(self, nc, es):
        self.nc = nc
        self.es = es
        self.es_stack = [es]
        self.pe = Eng(self, nc.tensor, "pe", ordered=True)
        self.act = Eng(self, nc.scalar, "act")
        self.dve = Eng(self, nc.vector, "dve")
        self.pool = Eng(self, nc.gpsimd, "pool")
        self.sp = Eng(self, nc.sync, "sp")
        self.engs = [self.pe, self.act, self.dve, self.pool, self.sp]
        self.nsem = 0
        for e in self.engs:
            e.sem = self.newsem(e.name)
        self.dmas = []
        self.dmas_hw = []
        self.dmas_sw = []
        for i in range(48):
            d = Eng(self, None, "dma%d" % i)
            d.sem = self.newsem(d.name)
            self.dmas.append(d)
            (self.dmas_hw if i < 24 else self.dmas_sw).append(d)
        self.dma_rr = 0
        self.dma_rr_hw = 0
        self.dma_rr_sw = 0
        self.uid = 0
        self.log = {e: [] for e in self.engs}

    def newsem(self, name):
        self.nsem += 1
        return self.es.enter_context(self.nc.semaphore("s_%s_%d" % (name, self.nsem)))

    def name(self, base):
        self.uid += 1
        return "%s_%d" % (base, self.uid)

    def sb(self, base, shape, dt):
        return self.es_stack[-1].enter_context(self.nc.sbuf_tensor(self.name(base), shape, dt))

    def scope(self):
        k = self

        class _S:
            def __enter__(s2):
                s2.es = ExitStack()
                k.es_stack.append(s2.es)
                return s2

            def __exit__(s2, *a):
                k.barrier()
                k.es_stack.pop()
                s2.es.close()
                return False
        return _S()

    def ps(self, base, shape, dt):
        return self.es.enter_context(self.nc.psum_tensor(self.name(base), shape, dt))

    def _deps(self, reads, writes):
        deps = {}

        def add(st):
            if st is None:
                return
            key = (st[0], st[1])
            if deps.get(key, 0) < st[2]:
                deps[key] = st[2]

        for t in reads:
            add(t.w)
        for t in writes:
            add(t.w)
            for st in t.r.values():
                add(st)
        return deps

    def _wait(self, e, deps):
        for (e2, ep), c in deps.items():
            if e2 is e and e.ordered:
                continue
            if e.seen.get((e2, ep), 0) >= c:
                continue
            sem = e2.sem if ep == e2.epoch else e2.old_sems[ep]
            e.eng.wait_ge(sem, c)
            e.seen[(e2, ep)] = c
            self.log[e].append(("w", e2, ep, c))

    def op(self, e, fn, reads=(), writes=(), inc=True):
        deps = self._deps(reads, writes)
        self._wait(e, deps)
        ins = fn()
        if inc:
            if e.cnt >= EPOCH_LIMIT:
                if not hasattr(e, "old_sems"):
                    e.old_sems = {}
                e.old_sems[e.epoch] = e.sem
                e.sem = self.newsem(e.name)
                e.epoch += 1
                e.cnt = 0
            ins.then_inc(e.sem, 1)
            e.cnt += 1
            self.log[e].append(("i", e, e.epoch, 1))
            st = (e, e.epoch, e.cnt)
            e.pending = False
        else:
            assert e.cnt + 1 < EPOCH_LIMIT
            st = (e, e.epoch, e.cnt + 1)
            e.pending = True
        for t in reads:
            t.r[e] = st
        for t in writes:
            t.w = st
            t.r = {}
        return ins

    def dma(self, q, out, in_, reads=(), writes=()):
        if q is self.pool:
            d = self.dmas_sw[self.dma_rr_sw % len(self.dmas_sw)]
            self.dma_rr_sw += 1
        else:
            d = self.dmas_hw[self.dma_rr_hw % len(self.dmas_hw)]
            self.dma_rr_hw += 1
        self.dma_rr += 1
        deps = self._deps(reads, writes)
        if d.cnt > 0:
            deps[(d, 0)] = max(deps.get((d, 0), 0), d.cnt)
        self._wait(q, deps)
        q.eng.dma_start(out=out, in_=in_).then_inc(d.sem, 16)
        d.cnt += 16
        self.log[q].append(("i", d, 0, 16))
        st = (d, 0, d.cnt)
        for t in reads:
            t.r[d] = st
        for t in writes:
            t.w = st
            t.r = {}

    def barrier(self):
        for e in self.engs:
            assert not e.pending
        for e in self.engs:
            deps = {}
            for e2 in self.engs + self.dmas:
                if e2 is e or e2.cnt == 0:
                    continue
                deps[(e2, e2.epoch)] = e2.cnt
            self._wait(e, deps)

    def simulate(self):
        val = {}
        pos = {e: 0 for e in self.engs}
        progress = True
        while progress:
            progress = False
            for e in self.engs:
                lg = self.log[e]
                while pos[e] < len(lg):
                    kind, e2, ep, c = lg[pos[e]]
                    if kind == "w":
                        if val.get((e2, ep), 0) < c:
                            break
                    else:
                        val[(e2, ep)] = val.get((e2, ep), 0) + c
                    pos[e] += 1
                    progress = True
        stuck = {e.name: (pos[e], len(self.log[e])) for e in self.engs if pos[e] < len(self.log[e])}
        for e in self.engs:
            if pos[e] < len(self.log[e]):
                kind, e2, ep, c = self.log[e][pos[e]]
                print("STUCK", e.name, "at", pos[e], "/", len(self.log[e]), "waiting", e2.name, ep, c, "have",
                      val.get((e2, ep), 0))
        return stuck

    def mm(self, out, lhsT, rhs, start, stop, reads, writes, inc=None, **kw):
        if inc is None:
            inc = stop
        return self.op(self.pe, lambda: self.nc.tensor.matmul(out, lhsT=lhsT, rhs=rhs, start=start, stop=stop, **kw),
                       reads, writes, inc=inc)

    def tr(self, out, in_, ident, reads, writes, inc=True):
        return self.op(self.pe, lambda: self.nc.tensor.transpose(out, in_, ident), reads, writes, inc=inc)

    def activation(self, out, in_, func, reads, writes, bias=None, scale=None, e=None):
        e = e or self.act
        kw = {}
        if bias is not None:
            kw["bias"] = bias
        if scale is not None:
            kw["scale"] = scale
        return self.op(e, lambda: self.nc.scalar.activation(out=out, in_=in_, func=func, **kw), reads, writes)

    def veng(self, e):
        return self.nc.vector if e is self.dve else self.nc.gpsimd

    def tt(self, out, in0, in1, op, reads, writes, e=None):
        e = e or self.dve
        return self.op(e, lambda: self.veng(e).tensor_tensor(out=out, in0=in0, in1=in1, op=op), reads, writes)

    def ts(self, out, in0, s1, s2, op0, op1, reads, writes, e=None):
        e = e or self.dve
        if op1 is None:
            return self.op(e, lambda: self.veng(e).tensor_single_scalar(out=out, in_=in0, scalar=s1, op=op0), reads, writes)
        return self.op(e, lambda: self.veng(e).tensor_scalar(out=out, in0=in0, scalar1=s1, scalar2=s2, op0=op0, op1=op1),
                       reads, writes)

    def stt(self, out, in0, scalar, in1, op0, op1, reads, writes):
        return self.op(self.dve, lambda: self.nc.vector.scalar_tensor_tensor(out=out, in0=in0, scalar=scalar, in1=in1,
                                                                             op0=op0, op1=op1), reads, writes)

    def copy(self, out, in_, reads, writes, e=None):
        e = e or self.dve
        if e is self.act:
            return self.op(e, lambda: self.nc.scalar.copy(out=out, in_=in_), reads, writes)
        return self.op(e, lambda: self.veng(e).tensor_copy(out=out, in_=in_), reads, writes)

    def memset(self, ap, val, writes, e=None):
        e = e or self.dve
        return self.op(e, lambda: self.veng(e).memset(ap, val), (), writes)


def toks(n):
    return [Tok() for _ in range(n)]


def toks2(n, m):
    return [[Tok() for _ in range(m)] for _ in range(n)]


class Prog:
    def __init__(self, n_layers=L, phases=("mix", "xattn", "mlp"), debug=(), nhg=4, nml=4, do_scan=True, do_post=True,
                 do_wout=True, do_prep=True, stop=99, hbar=0):
        self.hbar = hbar
        self.stop = stop
        self.nhg, self.nml, self.do_scan, self.do_post, self.do_wout, self.do_prep = nhg, nml, do_scan, do_post, do_wout, do_prep
        self.n_layers = n_layers
        self.phases = phases
        self.debug = set(debug)
        self.nc = bass.Bass("TRN2", target_bir_lowering=False)
        self.dbg_out = {}

    def declare(self):
        nc = self.nc

        def inp(name, shape):
            return nc.dram_tensor(name, list(shape), F32, kind="ExternalInput").ap()

        self.d_x = inp("x", [T, D])
        self.d_mem = inp("mem", [MEM, D])
        self.d_pfm = inp("pfm", [P, NPF])
        self.d_bin = inp("b_in", [L, 4624])
        self.d_whg = inp("w_hg", [L * 4 * P, KC * 512])
        self.d_whgg = inp("w_hgg", [L * 4 * P, KC * 128])
        self.d_wml = inp("w_ml", [L * 4 * P, KC * 384])
        self.d_wmlg = inp("w_mlg", [L * 4 * P, KC * 128])
        self.d_wgate = inp("w_gate", [L * P, KC * 16])
        self.d_wout = inp("w_out", [L * P, KC * D])
        self.d_wxq = inp("w_xq", [L * P, KC * D])
        self.d_wxo = inp("w_xo", [L * P, KC * D])
        self.d_wxkv = inp("w_xkv", [L * P, KC * 2 * D])
        self.d_wup = inp("w_up", [L * 8 * P, KC * 512])
        self.d_wdn = inp("w_down", [L * 8 * P, 32 * P])
        self.d_out = nc.dram_tensor("out", [T, D], F32, kind="ExternalOutput").ap()
        self.d_mix = nc.dram_tensor("mix_scr", [8 * P, T], BF16, kind="Internal").ap()

    def dbg(self, name, shape, dt=F32):
        ap = self.nc.dram_tensor("dbg_" + name, list(shape), dt, kind="ExternalOutput").ap()
        self.dbg_out[name] = "dbg_" + name
        return ap

    def build(self):
        self.declare()
        with ExitStack() as es:
            self.es = es
            self.k = K(self.nc, es)
            self.body()
            stuck = self.k.simulate()
            assert not stuck, stuck
            print("instr counts", {e.name: (e.epoch, e.cnt) for e in self.k.engs}, "dmas", self.k.dma_rr)
        return self.nc

    def body(self):
        k = self.k
        nc = self.nc
        self.xT = k.sb("xT", [P, KC * T], F32)
        self.xT_t = toks2(KC, NG)
        self.xn = k.sb("xn", [P, KC * T], BF16)
        self.xn_t = toks2(KC, NG)
        self.pfm = k.sb("pfm", [P, NPF], F32)
        self.pfm_t = Tok()
        self.mix_t = toks2(8, NG)
        self.setup_consts()
        self.pb = [k.ps("pb", [P, 512], F32) for _ in range(6)]
        self.pb_t = toks(6)
        self.pb_rr = 0
        self.ptr = [k.ps("ptr", [P, 1024], BF16) for _ in range(2)]
        self.ptr_t = toks(2)
        self.ptr_rr = 0

        self.load_x()
        for l in range(self.n_layers):
            self.layer(l)
        self.final_out()

    def big(self):
        i = self.pb_rr % 6
        self.pb_rr += 1
        return self.pb[i], self.pb_t[i]

    def trb(self):
        i = self.ptr_rr % 2
        self.ptr_rr += 1
        return self.ptr[i], self.ptr_t[i]

    def setup_consts(self):
        k = self.k
        nc = self.nc
        self.ct = Tok()
        self.identf = k.sb("identf", [P, P], F32)
        self.identb = k.sb("identb", [P, P], BF16)
        self.onesb = k.sb("onesb", [P, P], BF16)
        self.onesf = k.sb("onesf", [P, P], F32)
        self.maskF = k.sb("maskF", [CH, CH], F32)
        self.maskB = k.sb("maskB", [CH, CH], F32)
        self.scanm = k.sb("scanm", [P, GS], F32)
        ct = self.ct
        k.memset(self.onesf[:], 1.0, [ct], e=k.pool)
        k.op(k.pool, lambda: nc.gpsimd.affine_select(out=self.identf[:], in_=self.onesf[:], pattern=[[1, P]],
                                                     compare_op=ALU.is_equal, fill=0.0, base=0, channel_multiplier=-1),
             [ct], [ct])
        k.copy(self.identb[:], self.identf[:], [ct], [ct], e=k.pool)
        k.copy(self.onesb[:], self.onesf[:], [ct], [ct], e=k.pool)
        k.op(k.pool, lambda: nc.gpsimd.affine_select(out=self.maskF[:], in_=self.onesf[0:CH, 0:CH], pattern=[[1, CH]],
                                                     compare_op=ALU.is_ge, fill=0.0, base=0, channel_multiplier=-1),
             [ct], [ct])
        k.op(k.pool, lambda: nc.gpsimd.affine_select(out=self.maskB[:], in_=self.onesf[0:CH, 0:CH], pattern=[[-1, CH]],
                                                     compare_op=ALU.is_ge, fill=0.0, base=0, channel_multiplier=1),
             [ct], [ct])
        self.epsc = k.sb("epsc", [P, 1], F32)
        k.memset(self.epsc[:], EPS, [ct], e=k.pool)
        k.memset(self.scanm[:], 1.0, [ct], e=k.pool)
        k.memset(self.scanm[:].rearrange("p (c j) -> p c j", j=CH)[:, :, 0:1], 0.0, [ct], e=k.pool)
        k.dma(k.sp, self.pfm[:], self.d_pfm, [], [self.pfm_t])
        self.npfm = k.sb("npfm", [P, NPF], F32)
        k.ts(self.npfm[:], self.pfm[:], -1.0, None, ALU.mult, None, [self.pfm_t], [self.pfm_t])
        self.lb = k.sb("lb", [P, 32], F32)
        self.oml = k.sb("oml", [P, 32], F32)
        self.lb_t = Tok()
        ex = k.sb("lbex", [P, 32], F32)
        sm = k.sb("lbsm", [P, 8], F32)
        t = self.lb_t
        k.activation(ex[:], self.pfm[:, PF_LBL:PF_LBL + 32], AF.Exp, [self.pfm_t], [t])
        exv = ex[:].rearrange("p (d l h) -> p d l h", d=2, l=L)
        smv = sm[:].rearrange("p (d h) -> p d h", d=2)
        lbv = self.lb[:].rearrange("p (d l h) -> p d l h", d=2, l=L)
        k.tt(smv, exv[:, :, 0, :], exv[:, :, 1, :], ALU.add, [t], [t])
        k.tt(smv, smv, exv[:, :, 2, :], ALU.add, [t], [t])
        k.tt(smv, smv, exv[:, :, 3, :], ALU.add, [t], [t])
        k.op(k.dve, lambda: nc.vector.reciprocal(out=sm[:], in_=sm[:]), [t], [t])
        for l in range(L):
            k.tt(exv[:, :, l, :], exv[:, :, l, :], smv, ALU.mult, [t], [t])
        k.memset(lbv[:, :, 0, :], 0.0, [t])
        k.copy(lbv[:, :, 1, :], exv[:, :, 1, :], [t], [t])
        k.tt(lbv[:, :, 2, :], lbv[:, :, 1, :], exv[:, :, 2, :], ALU.add, [t], [t])
        k.tt(lbv[:, :, 3, :], lbv[:, :, 2, :], exv[:, :, 3, :], ALU.add, [t], [t])
        k.ts(self.oml[:], self.lb[:], -1.0, 1.0, ALU.mult, ALU.add, [t], [t])
        self.noml = k.sb("noml", [P, 32], F32)
        k.ts(self.noml[:], self.oml[:], -1.0, None, ALU.mult, None, [t], [t])

    def load_tm_to_fm(self, dram, ntile, dst, dst_cols, dst_tok_fn):
        k = self.k
        stg = [k.sb("ldstg", [P, D], F32) for _ in range(2)]
        stg_t = toks(2)
        for i in range(ntile):
            s, st = stg[i % 2], stg_t[i % 2]
            k.dma(k.sp, s[:], dram[i * P:(i + 1) * P, :], [], [st])
            for half in range(2):
                pb, pt = self.big()
                for cc in range(4):
                    c = half * 4 + cc
                    k.tr(pb[:, cc * P:(cc + 1) * P], s[:, c * P:(c + 1) * P], self.identf[:], [st, self.ct], [pt],
                         inc=(cc == 3))
                dv = dst[:].rearrange("p (c t) -> p c t", c=KC)[:, half * 4:(half + 1) * 4, i * P:(i + 1) * P]
                k.copy(dv, pb[:].rearrange("p (c t) -> p c t", c=4), [pt], dst_tok_fn(i, half),
                       e=(k.act if (i + half) % 2 else k.dve))

    def load_x(self):
        with self.k.scope():
            self._load_x()

    def _load_x(self):
        self.load_tm_to_fm(self.d_x, T // P, self.xT, T,
                           lambda i, half: [self.xT_t[c][i // 4] for c in range(half * 4, half * 4 + 4)])

    def rmsnorm_fm(self, src, src_t, ncols, gcol, dst, dst_t, out_dt_is_bf16=True):
        k = self.k
        nc = self.nc
        ng = (ncols + GS - 1) // GS
        gs = min(GS, ncols)
        self._nsq = [k.sb("nsq", [P, KC * gs], BF16) for _ in range(2)]
        self._nsq_t = toks(2)
        self._nrs = [k.sb("nrs", [P, gs], F32) for _ in range(2)]
        self._nrs_t = toks(2)
        self._n_rr = 0
        for g in range(ng):
            i = self._n_rr % 2
            self._n_rr += 1
            sq, sqt, rs, rst = self._nsq[i], self._nsq_t[i], self._nrs[i], self._nrs_t[i]
            srcv = src[:].rearrange("p (c t) -> p c t", c=KC)[:, :, g * gs:(g + 1) * gs]
            sqv = sq[:, 0:KC * gs].rearrange("p (c t) -> p c t", c=KC)
            k.activation(sqv, srcv, AF.Square, [src_t[c][g] for c in range(KC)], [sqt])
            pb, pt = self.big()
            for c in range(KC):
                k.mm(pb[:, 0:gs], self.onesb[:], sq[:, c * gs:(c + 1) * gs], c == 0, c == KC - 1, [sqt, self.ct], [pt])
            k.activation(rs[:, 0:gs], pb[:, 0:gs], AF.Ln, [pt, self.ct], [rst], bias=self.epsc[:, 0:1], scale=1.0 / D)
            k.activation(rs[:, 0:gs], rs[:, 0:gs], AF.Exp, [rst], [rst], scale=-0.5)
            for c in range(KC):
                k.stt(dst[:, c * ncols + g * gs: c * ncols + (g + 1) * gs],
                      src[:, c * ncols + g * gs: c * ncols + (g + 1) * gs],
                      self.pfm[:, gcol + c: gcol + c + 1], rs[:, 0:gs], ALU.mult, ALU.mult,
                      [src_t[c][g], rst, self.pfm_t], [dst_t[c][g]])

    def final_out(self):
        k = self.k
        with k.scope():
            self.rmsnorm_fm(self.xT, self.xT_t, T, PF_GN + L * 32, self.xT, self.xT_t)
        es2 = k.scope()
        es2.__enter__()
        stg = [k.sb("ostg", [P, D], F32) for _ in range(2)]
        stg_t = toks(2)
        xv = self.xT[:].rearrange("p (c t) -> p c t", c=KC)
        for i in range(T // P):
            s, st = stg[i % 2], stg_t[i % 2]
            for half in range(2):
                pb, pt = self.big()
                for cc in range(4):
                    c = half * 4 + cc
                    k.tr(pb[:, cc * P:(cc + 1) * P], xv[:, c, i * P:(i + 1) * P], self.identf[:],
                         [self.xT_t[c][i // 4], self.ct], [pt], inc=(cc == 3))
                k.copy(s[:, half * 512:(half + 1) * 512], pb[:], [pt], [st], e=(k.act if half else k.dve))
            k.dma(k.sp, self.d_out[i * P:(i + 1) * P, :], s[:], [st], [])
        es2.__exit__(None, None, None)

    def layer(self, l):
        k = self.k
        if "mix" in self.phases:
            with k.scope():
                self.rmsnorm_fm(self.xT, self.xT_t, T, PF_GN + l * 32 + 0, self.xn, self.xn_t)
            with k.scope():
                self.mixer(l)
        pre = ("xattn" in self.phases) and ("mix" in self.phases) and self.do_wout
        if pre:
            with k.scope():
                xa = self.xattn_alloc(l)
                with k.scope():
                    self.xattn_wkv(l, xa)
                    with k.scope():
                        self.wout_phase(l)
                    with k.scope():
                        self.rmsnorm_fm(self.xT, self.xT_t, T, PF_GN + l * 32 + 8, self.xn, self.xn_t)
                    self.xattn_kv(l, xa)
                self.xattn_main(l, xa)
        else:
            if "mix" in self.phases and self.do_wout:
                with k.scope():
                    self.wout_phase(l)
            if "xattn" in self.phases:
                with k.scope():
                    self.rmsnorm_fm(self.xT, self.xT_t, T, PF_GN + l * 32 + 8, self.xn, self.xn_t)
                with k.scope():
                    xa = self.xattn_alloc(l)
                    with k.scope():
                        self.xattn_wkv(l, xa)
                        self.xattn_kv(l, xa)
                    self.xattn_main(l, xa)
        if "mlp" in self.phases:
            with k.scope():
                self.rmsnorm_fm(self.xT, self.xT_t, T, PF_GN + l * 32 + 24, self.xn, self.xn_t)
            with k.scope():
                self.mlp_phase(l)

    def mixer(self, l):
        import types
        k = self.k
        m = self.m = types.SimpleNamespace()
        SW = self.SW = 130
        m.stage = [k.sb("stage", [CH, NCH * SW], F32) for _ in range(2)]
        m.stage_t = [toks(NCH) for _ in range(2)]
        m.ktm = [k.sb("ktm", [CH, NCH * P], BF16) for _ in range(2)]
        m.ktm_t = [toks(4) for _ in range(2)]
        m.vcm = k.sb("vcm", [CH, NCH * SW], BF16)
        m.vcm_t = toks(8)
        m.QT = [k.sb("QT", [P, T], BF16)]
        m.QT_t = [toks(NG)]
        m.KT = [k.sb("KT", [P, T], BF16)]
        m.KT_t = [toks(NG)]
        m.wh = k.sb("wh", [P, KC * 512], BF16)
        m.wh_t = Tok()
        m.whg = k.sb("whg", [P, KC * 128], BF16)
        m.whg_t = Tok()
        self.issue_main(l, 0)
        m.R = [k.sb("R", [P, SW], F32) for _ in range(2)]
        m.R_t = toks(2)
        m.S = [[k.sb("S", [P, SW], BF16) for _ in range(3)] for _ in range(2)]
        m.S_t = toks2(2, 3)
        m.sc = [k.sb("sc", [CH, CH], BF16) for _ in range(4)]
        m.sc_t = toks(4)
        m.sc_rr = 0
        m.bv = k.sb("bv", [CH, P], F32)
        m.bv_t = Tok()
        m.gt = [k.sb("gateg", [P, GS], F32)] * 2
        m.gt_t = [Tok()] * 2
        m.mixh = [k.sb("mixh", [P, GS], BF16) for _ in range(2)]
        m.mixh_t = toks(2)
        m.ms = k.sb("ms", [CH, NCH], F32)
        m.small_t = Tok()
        m.tA = k.sb("tA", [P, GS], F32)
        m.tA_t = Tok()
        vv = m.vcm[:].rearrange("p (c e) -> p c e", e=SW)
        k.memset(vv[:, :, P:P + 1], 1.0, m.vcm_t)
        with k.scope():
            m.QT.append(k.sb("QTb", [P, T], BF16))
            m.QT_t.append(toks(NG))
            m.KT.append(k.sb("KTb", [P, T], BF16))
            m.KT_t.append(toks(NG))
            m.q2 = [k.sb("q32", [P, GS], F32) for _ in range(2)]
            m.q2_t = toks(2)
            m.tB2 = [k.sb("tB", [P, GS], F32) for _ in range(2)]
            m.tB2_t = toks(2)
            m.atab = [k.sb("atab", [P, NCH], F32) for _ in range(2)]
            m.atab_t = toks(2)
            for h in range(self.nhg):
                self.hgrn_head(l, h)
        m.QT = m.QT[:1]
        m.QT_t = m.QT_t[:1]
        m.KT = m.KT[:1]
        m.KT_t = m.KT_t[:1]
        with k.scope():
            m.xpad = k.sb("xpad", [P, T + 4], F32)
            m.xpad_t = toks(NG)
            m.G = k.sb("G", [CH, NCH * 16], F32)
            m.lfg = k.sb("lfg", [CH, 256], F32)
            m.cum = k.sb("cum", [CH, 256], F32)
            m.e1 = k.sb("e1", [CH, 256], F32)
            m.e2 = k.sb("e2", [CH, 256], F32)
            m.abc = k.sb("abc", [P, 256], F32)
            m.wg = k.sb("wg", [P, KC * 16], BF16)
            m.bg = k.sb("bg", [CH, 16], F32)
            m.lnk = k.sb("lnk", [CH, 1], F32)
            m.fac = k.sb("fac", [CH, 2 * NCH], F32)
            m.gates_t = Tok()
            m.tE = k.sb("tE", [P, GS], F32)
            m.tE_t = Tok()
            k.memset(m.xpad[:, 0:2], 0.0, m.xpad_t)
            k.memset(m.xpad[:, T + 2:T + 4], 0.0, m.xpad_t)
            if self.nml:
                self.mlstm_gates(l)
            for h in range(self.nml):
                self.mlstm_head(l, h)

    def sig_inplace(self, ap, tok):
        k = self.k
        np_ = ap.partition_size() if callable(getattr(ap, "partition_size", None)) else P
        k.activation(ap, ap, AF.Ln, [tok, self.ct], [tok], bias=self.onesf[0:np_, 0:1])
        k.activation(ap, ap, AF.Exp, [tok], [tok], scale=-1.0)

    def issue_main(self, l, idx):
        k = self.k
        m = self.m
        if idx < 4:
            if idx >= self.nhg:
                return
            k.dma(k.pool, m.wh[:, 0:KC * 512], self.d_whg[(l * 4 + idx) * P:(l * 4 + idx + 1) * P, :], [], [m.wh_t])
        elif idx < 8:
            if idx - 4 >= self.nml:
                return
            h = idx - 4
            k.dma(k.pool, m.wh[:, 0:KC * 384], self.d_wml[(l * 4 + h) * P:(l * 4 + h + 1) * P, :], [], [m.wh_t])

    def issue_gate(self, l, idx):
        k = self.k
        m = self.m
        src = self.d_whgg if idx < 4 else self.d_wmlg
        h = idx % 4
        k.dma(k.pool, m.whg[:], src[(l * 4 + h) * P:(l * 4 + h + 1) * P, :], [], [m.whg_t])

    @staticmethod
    def interleave(gens, window=3):
        active = []
        gens = list(gens)
        while gens or active:
            while gens and len(active) < window:
                active.append(gens.pop(0))
            nxt = []
            for gen in active:
                try:
                    next(gen)
                    nxt.append(gen)
                except StopIteration:
                    pass
            active = nxt

    def proj_group(self, blk, stride, g, gate=False):
        k = self.k
        m = self.m
        w, wt = (m.whg, m.whg_t) if gate else (m.wh, m.wh_t)
        pb, pt = self.big()
        for kc in range(KC):
            k.mm(pb[:, :], w[:, kc * stride + blk * P: kc * stride + (blk + 1) * P],
                 self.xn[:, kc * T + g * GS: kc * T + (g + 1) * GS], kc == 0, kc == KC - 1,
                 [wt, self.xn_t[kc][g]], [pt])
        return pb, pt

    def v_proj(self, l, vblk, stride, bias_col0):
        k = self.k
        m = self.m
        SW = self.SW
        k.dma(k.sp, m.bv[:], self.d_bin[l, bias_col0:bias_col0 + P].partition_broadcast(CH), [], [m.bv_t])
        vv = m.vcm[:].rearrange("p (c e) -> p c e", e=SW)
        for c4 in range(8):
            pb, pt = self.big()
            for cc in range(4):
                c = c4 * 4 + cc
                for kc in range(KC):
                    k.mm(pb[0:CH, cc * P:(cc + 1) * P], self.xn[:, kc * T + c * CH: kc * T + (c + 1) * CH],
                         m.wh[:, kc * stride + vblk * P: kc * stride + (vblk + 1) * P], kc == 0, kc == KC - 1,
                         [self.xn_t[kc][c // 8], m.wh_t], [pt])
            k.tt(vv[:, c4 * 4:(c4 + 1) * 4, 0:P], pb[0:CH, :].rearrange("p (c e) -> p c e", e=P),
                 m.bv[:].unsqueeze(1).to_broadcast([CH, 4, P]), ALU.add, [pt, m.bv_t], [m.vcm_t[c4]])

    def k_transposes(self, d_src, scale_fn):
        k = self.k
        m = self.m
        for c8 in range(4):
            ptb, ptt = self.trb()
            for cc in range(8):
                c = c8 * 8 + cc
                k.tr(ptb[0:CH, cc * P:(cc + 1) * P], m.KT[d_src][:, c * CH:(c + 1) * CH], self.identb[:],
                     [m.KT_t[d_src][c // 8], self.ct], [ptt], inc=(cc == 7))
            yield c8, ptb, ptt

    def hgrn_head(self, l, h):
        k = self.k
        nc = self.nc
        m = self.m
        stride = 512
        if self.hbar:
            k.barrier()
        self.issue_gate(l, h)
        if self.stop <= 1:
            return
        self.v_proj(l, 3, stride, 1536 + h * P)
        if self.stop <= 2:
            return
        bq = PF_BIN + l * 36 + h

        def q_chain(g):
            q, qtk = m.q2[g % 2], m.q2_t[g % 2]
            pb, pt = self.proj_group(0, stride, g)
            yield
            k.activation(q[:], pb[:, :], AF.Identity, [pt, self.pfm_t], [qtk], bias=self.pfm[:, bq:bq + 1])
            k.activation(pb[:, :], pb[:, :], AF.Exp, [pt, self.pfm_t], [pt], bias=self.npfm[:, bq:bq + 1], scale=-1.0)
            yield
            k.activation(pb[:, :], pb[:, :], AF.Ln, [pt, self.ct], [pt], bias=self.onesf[:, 0:1])
            yield
            k.activation(pb[:, :], pb[:, :], AF.Exp, [pt], [pt], scale=-1.0)
            yield
            k.tt(q[:], q[:], pb[:, :], ALU.mult, [qtk, pt], [qtk])
            yield

        def d_chain(g, d):
            q, qtk = m.q2[g % 2], m.q2_t[g % 2]
            bcol = PF_BIN + l * 36 + 4 + 4 * d + h
            idx = d * 16 + l * 4 + h
            tB, tBt = m.tB2[d], m.tB2_t[d]
            pa, pat = self.proj_group(1 + d, stride, g)
            yield
            k.activation(pa[:, :], pa[:, :], AF.Exp, [pat, self.pfm_t], [pat], bias=self.npfm[:, bcol:bcol + 1],
                         scale=-1.0)
            yield
            k.activation(pa[:, :], pa[:, :], AF.Ln, [pat, self.ct], [pat], bias=self.onesf[:, 0:1])
            yield
            k.activation(pa[:, :], pa[:, :], AF.Exp, [pat], [pat], scale=-1.0)
            yield
            k.activation(tB[:], pa[:, :], AF.Ln, [pat, self.lb_t], [tBt], scale=self.oml[:, idx:idx + 1],
                         bias=self.lb[:, idx:idx + 1])
            k.activation(pa[:, :], pa[:, :], AF.Identity, [pat, self.lb_t], [pat], scale=self.noml[:, idx:idx + 1],
                         bias=self.oml[:, idx:idx + 1])
            yield
            pc, pct = self.big()
            if d == 0:
                k.op(k.dve, lambda: nc.vector.tensor_tensor_scan(out=pc[:, :], data0=self.scanm[:], data1=tB[:],
                                                                 initial=0.0, op0=ALU.mult, op1=ALU.add),
                     [tBt, self.ct], [pct])
            else:
                k.op(k.dve, lambda: nc.vector.tensor_tensor_scan(out=pc[:, ::-1], data0=self.scanm[:],
                                                                 data1=tB[:, ::-1], initial=0.0, op0=ALU.mult,
                                                                 op1=ALU.add),
                     [tBt, self.ct], [pct])
            yield
            cv = pc[:, :].rearrange("p (c j) -> p c j", j=CH)
            edge = CH - 1 if d == 0 else 0
            k.activation(m.atab[d][:, g * 8:(g + 1) * 8], cv[:, :, edge], AF.Exp, [pct], [m.atab_t[d]])
            k.activation(tB[:], pc[:, :], AF.Exp, [pct], [tBt])
            yield
            k.tt(m.QT[d][:, g * GS:(g + 1) * GS], q[:], tB[:], ALU.mult, [qtk, tBt], [m.QT_t[d][g]])
            yield
            k.activation(tB[:], pc[:, :], AF.Exp, [pct], [tBt], scale=-1.0)
            yield
            k.tt(m.KT[d][:, g * GS:(g + 1) * GS], pa[:, :], tB[:], ALU.mult, [pat, tBt], [m.KT_t[d][g]])
            yield

        self.interleave([q_chain(0)], window=1)
        for g in range(NG):
            wave = [d_chain(g, 0), d_chain(g, 1)]
            if g + 1 < NG:
                wave.append(q_chain(g + 1))
            self.interleave(wave, window=3)
        self.issue_main(l, h + 1)
        if self.stop <= 3:
            return
        for d in range(2):
            for c8, ptb, ptt in self.k_transposes(d, None):
                k.copy(m.ktm[d][0:CH, c8 * 1024:(c8 + 1) * 1024], ptb[0:CH, :], [ptt], [m.ktm_t[d][c8]],
                       e=(k.act if c8 % 2 else k.dve))
        if self.do_scan:
            self.scan_head(P, False, lambda d, c: (m.atab[d][:, c:c + 1], m.atab_t[d]), None)
        if self.do_post:
            self.post_head(l, h, h, False, 0, 128, AF.Silu, PF_BIN + l * 36 + 16 + h, PF_HGN + l * 4 + h)

    def mlstm_gates(self, l):
        k = self.k
        nc = self.nc
        m = self.m
        gt = m.gates_t
        k.dma(k.pool, m.wg[:], self.d_wgate[l * P:(l + 1) * P, :], [], [gt])
        k.dma(k.sp, m.bg[:], self.d_bin[l, 4608:4624].partition_broadcast(CH), [], [gt])
        k.memset(m.lnk[:], -0.5 * float(np.log(128.0)), [gt])
        pb, pt = self.big()
        for c in range(NCH):
            for kc in range(KC):
                k.mm(pb[0:CH, c * 16:(c + 1) * 16], self.xn[:, kc * T + c * CH: kc * T + (c + 1) * CH],
                     m.wg[:, kc * 16:(kc + 1) * 16], kc == 0, kc == KC - 1, [gt, self.xn_t[kc][c // 8]], [pt])
        Gv = m.G[:].rearrange("p (c j) -> p c j", j=16)
        k.tt(Gv, pb[0:CH, :].rearrange("p (c j) -> p c j", j=16), m.bg[:].unsqueeze(1).to_broadcast([CH, NCH, 16]),
             ALU.add, [pt, gt], [gt])
        lv = m.lfg[:].rearrange("p (d c h) -> p d c h", d=2, h=4)
        for d in range(2):
            k.activation(lv[:, d], Gv[:, :, 8 + 4 * d:12 + 4 * d], AF.Exp, [gt], [gt], scale=-1.0)
        k.ts(m.lfg[:], m.lfg[:], 1.0, None, ALU.add, None, [gt], [gt])
        k.activation(m.lfg[:], m.lfg[:], AF.Ln, [gt], [gt])
        k.ts(m.lfg[:], m.lfg[:], -1.0, None, ALU.mult, None, [gt], [gt])
        pb, pt = self.big()
        k.mm(pb[0:CH, 0:128], self.maskF[:], m.lfg[:, 0:128], True, True, [gt, self.ct], [pt])
        k.mm(pb[0:CH, 128:256], self.maskB[:], m.lfg[:, 128:256], True, True, [gt, self.ct], [pt])
        k.copy(m.cum[:], pb[0:CH, 0:256], [pt], [gt])
        k.activation(m.e1[:], m.cum[:], AF.Exp, [gt], [gt])
        e2v = m.e2[:].rearrange("p (d c h) -> p d c h", d=2, h=4)
        cumv = m.cum[:].rearrange("p (d c h) -> p d c h", d=2, h=4)
        for d in range(2):
            k.tt(e2v[:, d], Gv[:, :, 4 * d:4 * d + 4], cumv[:, d], ALU.subtract, [gt], [gt])
        k.activation(m.e2[:], m.e2[:], AF.Exp, [gt], [gt], bias=m.lnk[:, 0:1])
        pb, pt = self.big()
        k.mm(pb[:, 0:256], self.onesf[0:CH, :], m.lfg[:], True, True, [gt, self.ct], [pt])
        k.activation(m.abc[:], pb[:, 0:256], AF.Exp, [pt], [gt])

    def conv_proj(self, l, blk, stride, cidx, bias_col, dst, dst_t):
        k = self.k
        m = self.m
        for g in range(NG):
            pb, pt = self.proj_group(blk, stride, g)
            k.activation(m.xpad[:, 2 + g * GS: 2 + (g + 1) * GS], pb[:, :], AF.Identity, [pt, self.pfm_t], [m.xpad_t[g]],
                         bias=self.pfm[:, bias_col:bias_col + 1])
        wcol = PF_CW + l * 40 + cidx * 5
        cb = PF_CB + l * 8 + cidx

        def chain(g):
            rd = [m.xpad_t[gg] for gg in range(max(0, g - 1), min(NG, g + 2))]
            pa, pat = self.big()
            k.ts(pa[:, :], m.xpad[:, g * GS: g * GS + GS], self.pfm[:, wcol:wcol + 1], self.pfm[:, cb:cb + 1], ALU.mult,
                 ALU.add, rd + [self.pfm_t], [pat])
            yield
            for j in range(1, 5):
                k.stt(pa[:, :], m.xpad[:, g * GS + j: g * GS + j + GS], self.pfm[:, wcol + j:wcol + j + 1], pa[:, :],
                      ALU.mult, ALU.add, rd + [pat, self.pfm_t], [pat])
                yield
            k.activation(m.tE[:], pa[:, :], AF.Exp, [pat], [m.tE_t], scale=-1.0)
            k.activation(m.tE[:], m.tE[:], AF.Ln, [m.tE_t, self.ct], [m.tE_t], bias=self.onesf[:, 0:1])
            k.activation(m.tE[:], m.tE[:], AF.Exp, [m.tE_t], [m.tE_t], scale=-1.0)
            k.tt(dst[:, g * GS:(g + 1) * GS], pa[:, :], m.tE[:], ALU.mult, [pat, m.tE_t], [dst_t[g]])
            yield

        self.interleave([chain(g) for g in range(NG)], window=2)

    def mlstm_head(self, l, h):
        k = self.k
        m = self.m
        stride = 384
        if self.hbar:
            k.barrier()
        self.issue_gate(l, 4 + h)
        self.v_proj(l, 2, stride, 3584 + h * P)
        self.conv_proj(l, 0, stride, h, PF_BIN + l * 36 + 20 + h, m.QT[0], m.QT_t[0])
        self.conv_proj(l, 1, stride, 4 + h, PF_BIN + l * 36 + 24 + h, m.KT[0], m.KT_t[0])
        self.issue_main(l, 4 + h + 1)
        for c8, ptb, ptt in self.k_transposes(0, None):
            for d in range(2):
                e2d = m.e2[:, d * 128:(d + 1) * 128].rearrange("p (c h) -> p c h", h=4)[:, c8 * 8:(c8 + 1) * 8, h:h + 1]
                k.tt(m.ktm[d][0:CH, c8 * 1024:(c8 + 1) * 1024].rearrange("p (c e) -> p c e", e=P),
                     ptb[0:CH, :].rearrange("p (c e) -> p c e", e=P), e2d.to_broadcast([CH, 8, P]), ALU.mult,
                     [ptt, m.gates_t], [m.ktm_t[d][c8]])
        if self.do_scan:
            self.scan_head(P + 1, True,
                           lambda d, c: (m.abc[:, d * 128 + c * 4 + h: d * 128 + c * 4 + h + 1], m.gates_t),
                           lambda d, c: m.e2[0:CH, d * 128 + c * 4 + h: d * 128 + c * 4 + h + 1])
        if self.do_post:
            self.post_head(l, h, 4 + h, True, 0, 128, AF.Sigmoid, PF_BIN + l * 36 + 32 + h, PF_MLN + l * 4 + h)

    def scan_head(self, E, mlstm, a_fn, e2_fn):
        k = self.k
        m = self.m
        SW = self.SW

        def front(i, d):
            dq = 0 if mlstm else d
            c = i if d == 0 else NCH - 1 - i
            g = c // 8
            vslot = m.vcm[0:CH, c * SW: c * SW + E]
            vt = m.vcm_t[c // 4]
            qsl = m.QT[dq][:, c * CH:(c + 1) * CH]
            qt = m.QT_t[dq][g]
            pbk, pst = self.big()
            psl = pbk[0:CH, 0:CH]
            k.mm(psl, m.KT[dq][:, c * CH:(c + 1) * CH], qsl, True, True, [m.KT_t[dq][g], qt], [pst])
            sc, sct = m.sc[m.sc_rr % 4], m.sc_t[m.sc_rr % 4]
            m.sc_rr += 1
            mask = self.maskF if d == 0 else self.maskB
            if mlstm:
                k.stt(sc[:], psl, e2_fn(d, c), mask[:], ALU.mult, ALU.mult, [pst, m.gates_t, self.ct], [sct])
            else:
                k.tt(sc[:], psl, mask[:], ALU.mult, [pst, self.ct], [sct])
            pbk, put = self.big()
            pul = pbk[:, 0:E]
            k.mm(pul, m.ktm[d][0:CH, c * P:(c + 1) * P], vslot, True, True, [m.ktm_t[d][c // 8], vt], [put])
            if i == 0:
                k.copy(m.R[d][:, 0:E], pul, [put], [m.R_t[d]])
            else:
                cprev = c - 1 if d == 0 else c + 1
                ap_prev, atok = a_fn(d, cprev)
                k.stt(m.R[d][:, 0:E], m.R[d][:, 0:E], ap_prev, pul, ALU.mult, ALU.add, [put, m.R_t[d], atok],
                      [m.R_t[d]])
            if i < NCH - 1:
                ap_c, atok = a_fn(d, c)
                k.activation(m.S[d][(i + 1) % 3][:, 0:E], m.R[d][:, 0:E], AF.Identity, [m.R_t[d], atok],
                             [m.S_t[d][(i + 1) % 3]], scale=ap_c)
            return (c, g, vslot, vt, qsl, qt, sc, sct)

        def back(i, d, st):
            c, g, vslot, vt, qsl, qt, sc, sct = st
            pbk, pot = self.big()
            pol = pbk[0:CH, 0:E]
            if i > 0:
                k.mm(pol, qsl, m.S[d][i % 3][:, 0:E], True, False, [qt, m.S_t[d][i % 3]], [pot])
            k.mm(pol, sc[:], vslot, i == 0, True, [sct, vt], [pot])
            dst = m.stage[d][0:CH, c * SW: c * SW + E]
            k.copy(dst, pol, [pot], [m.stage_t[d][c]], e=k.act)

        pending = []
        for i in range(NCH):
            for d in range(2):
                pending.append((i, d, front(i, d)))
                if len(pending) > 2:
                    back(*pending.pop(0))
        while pending:
            back(*pending.pop(0))

    def post_head(self, l, h, hidx, mlstm, gblk, stride, gfunc, gbias_col, gain_col):
        k = self.k
        nc = self.nc
        m = self.m
        SW = self.SW
        sA = m.stage[0][:].rearrange("p (c e) -> p c e", e=SW)
        sB = m.stage[1][:].rearrange("p (c e) -> p c e", e=SW)
        tA_all = m.stage_t[0]
        tB_all = m.stage_t[1]
        st = m.small_t
        if mlstm:
            for d in range(2):
                sv = sA if d == 0 else sB
                tv = tA_all if d == 0 else tB_all
                e1d = m.e1[:, d * 128:(d + 1) * 128].rearrange("p (c h) -> p c h", h=4)[:, :, h]
                t1 = m.fac[:, d * NCH:(d + 1) * NCH]
                k.tt(t1, sv[:, :, P], e1d, ALU.mult, tv + [m.gates_t], [st])
                k.activation(t1, t1, AF.Abs, [st], [st])
                k.ts(t1, t1, 1.0, None, ALU.max, None, [st], [st])
                k.op(k.dve, lambda: nc.vector.reciprocal(out=t1, in_=t1), [st], [st])
                k.tt(t1, t1, e1d, ALU.mult, [st, m.gates_t], [st])
                k.tt(sv[:, :, 0:P], sv[:, :, 0:P], t1.unsqueeze(2).to_broadcast([CH, NCH, P]), ALU.mult, tv + [st], tv)
        k.tt(sA[:, :, 0:P], sA[:, :, 0:P], sB[:, :, 0:P], ALU.add, tA_all + tB_all, tA_all)
        k.tt(sB[:, :, 0:P], sA[:, :, 0:P], sA[:, :, 0:P], ALU.mult, tA_all, tB_all)
        k.op(k.dve, lambda: nc.vector.tensor_reduce(out=m.ms[:], in_=sB[:, :, 0:P], axis=AX.X, op=ALU.add), tB_all, [st])
        k.activation(m.ms[:], m.ms[:], AF.Ln, [st, self.ct], [st], scale=1.0 / P, bias=self.epsc[0:CH, 0:1])
        k.activation(m.ms[:], m.ms[:], AF.Exp, [st], [st], scale=-0.5)
        hn = m.ktm[0]
        k.tt(hn[:].rearrange("p (c e) -> p c e", e=P), sA[:, :, 0:P], m.ms[:].unsqueeze(2).to_broadcast([CH, NCH, P]),
             ALU.mult, tA_all + [st], m.ktm_t[0])
        for g in range(NG):
            pb, pt = self.proj_group(gblk, stride, g, gate=True)
            gt, gtt = m.gt[g % 2], m.gt_t[g % 2]
            k.activation(gt[:], pb[:, :], AF.Exp, [pt, self.pfm_t], [gtt], bias=self.npfm[:, gbias_col:gbias_col + 1],
                         scale=-1.0)
            if gfunc == AF.Silu:
                k.activation(m.tA[:], pb[:, :], AF.Identity, [pt, self.pfm_t], [m.tA_t],
                             bias=self.pfm[:, gbias_col:gbias_col + 1])
            self.sig_inplace(gt[:], gtt)
            if gfunc == AF.Silu:
                k.tt(gt[:], gt[:], m.tA[:], ALU.mult, [gtt, m.tA_t], [gtt])
            ptb, ptt = self.trb()
            for cc in range(8):
                c = g * 8 + cc
                k.tr(ptb[:, cc * CH:(cc + 1) * CH], hn[0:CH, c * P:(c + 1) * P], self.identb[0:CH, 0:CH],
                     [m.ktm_t[0][c // 8], self.ct], [ptt], inc=(cc == 7))
            mh, mht = m.mixh[g % 2], m.mixh_t[g % 2]
            k.stt(mh[:], ptb[:, 0:GS], self.pfm[:, gain_col:gain_col + 1], gt[:], ALU.mult, ALU.mult,
                  [ptt, gtt, self.pfm_t], [mht])
            k.dma(k.sp, self.d_mix[hidx * P:(hidx + 1) * P, g * GS:(g + 1) * GS], mh[:], [mht], [self.mix_t[hidx][g]])

    def resid_add(self, pb, pt, n, g):
        k = self.k
        sl = self.xT[:, n * T + g * GS: n * T + (g + 1) * GS]
        k.tt(sl, pb[:, 0:GS], sl, ALU.add, [pt, self.xT_t[n][g]], [self.xT_t[n][g]])

    def mlp_phase(self, l):
        k = self.k
        hT = k.sb("hT", [P, 32 * 1024], BF16)
        hT_t = toks2(32, 2)
        wup = [k.sb("wup", [P, KC * 512], BF16) for _ in range(2)]
        wup_t = toks(2)
        wdn = [k.sb("wdn", [P, 32 * P], BF16) for _ in range(2)]
        wdn_t = toks(2)
        rt = [k.sb("rtmp", [P, GS], F32) for _ in range(2)]
        rt_t = toks(2)
        rr = 0

        def ld_up(fg):
            k.dma(k.pool, wup[fg % 2][:], self.d_wup[(l * 8 + fg) * P:(l * 8 + fg + 1) * P, :], [], [wup_t[fg % 2]])

        def ld_dn(n):
            k.dma(k.pool, wdn[n % 2][:], self.d_wdn[(l * 8 + n) * P:(l * 8 + n + 1) * P, :], [], [wdn_t[n % 2]])

        for hf in range(2):
            ld_up(0)
            for fg in range(8):
                if fg + 1 < 8:
                    ld_up(fg + 1)
                else:
                    ld_dn(0)
                wu, wut = wup[fg % 2], wup_t[fg % 2]
                for fc in range(4):
                    for g2 in range(2):
                        g = hf * 2 + g2
                        pb, pt = self.big()
                        for kc in range(KC):
                            k.mm(pb[:, :], wu[:, kc * 512 + fc * P: kc * 512 + (fc + 1) * P],
                                 self.xn[:, kc * T + g * GS: kc * T + (g + 1) * GS], kc == 0, kc == KC - 1,
                                 [wut, self.xn_t[kc][g]], [pt])
                        r, rtk = rt[rr % 2], rt_t[rr % 2]
                        rr += 1
                        k.activation(r[:], pb[:], AF.Relu, [pt], [rtk])
                        f = fg * 4 + fc
                        k.tt(hT[:, f * 1024 + g2 * GS: f * 1024 + (g2 + 1) * GS], r[:], r[:], ALU.mult, [rtk],
                             [hT_t[f][g2]])
            for n in range(8):
                if n + 1 < 8:
                    ld_dn(n + 1)
                wd, wdt = wdn[n % 2], wdn_t[n % 2]
                for g2 in range(2):
                    g = hf * 2 + g2
                    pb, pt = self.big()
                    for f in range(32):
                        k.mm(pb[:, :], wd[:, f * P:(f + 1) * P], hT[:, f * 1024 + g2 * GS: f * 1024 + (g2 + 1) * GS],
                             f == 0, f == 31, [wdt, hT_t[f][g2]], [pt])
                    self.resid_add(pb, pt, n, g)

    def wout_phase(self, l):
        k = self.k
        wo = k.sb("wout", [P, KC * D], BF16)
        wo_t = Tok()
        mb = [k.sb("mixb", [P, KC * GS], BF16) for _ in range(2)]
        mb_t = toks(2)
        k.dma(k.pool, wo[:], self.d_wout[l * P:(l + 1) * P, :], [], [wo_t])
        dmv = self.d_mix.rearrange("(c p) t -> p c t", p=P)

        act = list(range(self.nhg)) + [4 + i for i in range(self.nml)]
        full = len(act) == 8

        def ld(g):
            if full:
                k.dma(k.sp, mb[g % 2][:].rearrange("p (c t) -> p c t", c=KC), dmv[:, :, g * GS:(g + 1) * GS],
                      [self.mix_t[c][g] for c in range(8)], [mb_t[g % 2]])
            else:
                for c in act:
                    k.dma(k.sp, mb[g % 2][:, c * GS:(c + 1) * GS], self.d_mix[c * P:(c + 1) * P, g * GS:(g + 1) * GS],
                          [self.mix_t[c][g]], [mb_t[g % 2]])

        ld(0)
        for g in range(NG):
            if g + 1 < NG:
                ld(g + 1)
            for n in range(8):
                pb, pt = self.big()
                for kc in act:
                    k.mm(pb[:, :], wo[:, kc * D + n * P: kc * D + (n + 1) * P], mb[g % 2][:, kc * GS:(kc + 1) * GS],
                         kc == act[0], kc == act[-1], [wo_t, mb_t[g % 2]], [pt])
                self.resid_add(pb, pt, n, g)

    def xattn_alloc(self, l):
        import types
        k = self.k
        xa = types.SimpleNamespace()
        xa.wq = k.sb("wq", [P, KC * D], BF16)
        xa.wq_t = Tok()
        k.dma(k.pool, xa.wq[:], self.d_wxq[l * P:(l + 1) * P, :], [], [xa.wq_t])
        xa.memn = k.sb("memn", [P, KC * MEM], BF16)
        xa.memn_t = toks2(KC, 1)
        xa.kT = k.sb("kT", [P, 8 * MEM], BF16)
        xa.kT_t = Tok()
        xa.vsb = k.sb("vsb", [P, 2 * D], BF16)
        xa.vsb_t = Tok()
        return xa

    def xattn_wkv(self, l, xa):
        k = self.k
        xa.wkv = k.sb("wkv", [P, KC * 2 * D], BF16)
        xa.wkv_t = Tok()
        k.dma(k.pool, xa.wkv[:], self.d_wxkv[l * P:(l + 1) * P, :], [], [xa.wkv_t])

    def xattn_kv(self, l, xa):
        k = self.k
        memn, memn_t, kT, kT_t, vsb, vsb_t, wkv, wkv_t = (xa.memn, xa.memn_t, xa.kT, xa.kT_t, xa.vsb, xa.vsb_t,
                                                          xa.wkv, xa.wkv_t)
        if True:
            memT = k.sb("memT", [P, KC * MEM], F32)
            memT_t = toks2(KC, 1)
            self.load_tm_to_fm(self.d_mem, MEM // P, memT, MEM,
                               lambda i, half: [memT_t[c][0] for c in range(half * 4, half * 4 + 4)])
            self.rmsnorm_fm(memT, memT_t, MEM, PF_GN + l * 32 + 16, memn, memn_t)
            for n in range(8):
                pb, pt = self.big()
                for kc in range(KC):
                    k.mm(pb[:, 0:MEM], wkv[:, kc * 2 * D + n * P: kc * 2 * D + (n + 1) * P],
                         memn[:, kc * MEM:(kc + 1) * MEM], kc == 0, kc == KC - 1, [wkv_t, memn_t[kc][0]], [pt])
                k.copy(kT[:, n * MEM:(n + 1) * MEM], pb[:, 0:MEM], [pt], [kT_t], e=k.act)
            for jc in range(2):
                for ng in range(2):
                    pb, pt = self.big()
                    for kc in range(KC):
                        k.mm(pb[:, :], memn[:, kc * MEM + jc * P: kc * MEM + (jc + 1) * P],
                             wkv[:, kc * 2 * D + D + ng * GS: kc * 2 * D + D + (ng + 1) * GS], kc == 0, kc == KC - 1,
                             [wkv_t, memn_t[kc][0]], [pt])
                    k.copy(vsb[:, jc * D + ng * GS: jc * D + (ng + 1) * GS], pb[:, :], [pt], [vsb_t], e=k.act)

    def xattn_main(self, l, xa):
        k = self.k
        nc = self.nc
        memn, memn_t, kT, kT_t, vsb, vsb_t = xa.memn, xa.memn_t, xa.kT, xa.kT_t, xa.vsb, xa.vsb_t
        wq, wq_t = xa.wq, xa.wq_t
        wo = k.sb("wxo", [P, KC * D], BF16)
        wo_t = Tok()
        k.dma(k.pool, wo[:], self.d_wxo[l * P:(l + 1) * P, :], [], [wo_t])
        qT = [k.sb("qT", [P, 8 * GS], BF16) for _ in range(2)]
        qT_t = toks(2)
        oT = [k.sb("oT", [P, 8 * GS], BF16) for _ in range(2)]
        oT_t = toks(2)
        pT = [k.sb("pT", [P, 2 * GS], BF16) for _ in range(2)]
        pT_t = toks(2)
        rc = [k.sb("rc", [P, GS], F32) for _ in range(2)]
        rc_t = toks(2)
        pr = 0
        for g in range(NG):
            q, qt = qT[g % 2], qT_t[g % 2]
            o, ot = oT[g % 2], oT_t[g % 2]
            for n in range(8):
                pb, pt = self.big()
                for kc in range(KC):
                    k.mm(pb[:, :], wq[:, kc * D + n * P: kc * D + (n + 1) * P],
                         self.xn[:, kc * T + g * GS: kc * T + (g + 1) * GS], kc == 0, kc == KC - 1,
                         [wq_t, self.xn_t[kc][g]], [pt])
                k.copy(q[:, n * GS:(n + 1) * GS], pb[:, :], [pt], [qt], e=k.act)
            for h in range(4):
                p_, ptk = pT[pr % 2], pT_t[pr % 2]
                r_, rtk = rc[pr % 2], rc_t[pr % 2]
                pr += 1
                for jc in range(2):
                    pb, pt = self.big()
                    for hc in range(2):
                        n = h * 2 + hc
                        k.mm(pb[:, :], kT[:, n * MEM + jc * P: n * MEM + (jc + 1) * P], q[:, n * GS:(n + 1) * GS],
                             hc == 0, hc == 1, [kT_t, qt], [pt])
                    k.activation(p_[:, jc * GS:(jc + 1) * GS], pb[:, :], AF.Exp, [pt], [ptk], scale=1.0 / 16.0)
                pb, pt = self.big()
                for jc in range(2):
                    k.mm(pb[:, :], self.onesb[:], p_[:, jc * GS:(jc + 1) * GS], jc == 0, jc == 1, [self.ct, ptk], [pt])
                k.activation(r_[:], pb[:, :], AF.Ln, [pt], [rtk])
                k.activation(r_[:], r_[:], AF.Exp, [rtk], [rtk], scale=-1.0)
                for hc in range(2):
                    n = h * 2 + hc
                    pb, pt = self.big()
                    for jc in range(2):
                        k.mm(pb[:, :], vsb[:, jc * D + n * P: jc * D + (n + 1) * P], p_[:, jc * GS:(jc + 1) * GS],
                             jc == 0, jc == 1, [vsb_t, ptk], [pt])
                    k.tt(o[:, n * GS:(n + 1) * GS], pb[:, :], r_[:], ALU.mult, [pt, rtk], [ot])
            for n in range(8):
                pb, pt = self.big()
                for kc in range(KC):
                    k.mm(pb[:, :], wo[:, kc * D + n * P: kc * D + (n + 1) * P], o[:, kc * GS:(kc + 1) * GS],
                         kc == 0, kc == KC - 1, [wo_t, ot], [pt])
                self.resid_add(pb, pt, n, g)


def fm(v):
    v = np.asarray(v, np.float32)
    n = v.shape[-1] // P
    lead = v.shape[:-1]
    return np.moveaxis(v.reshape(lead + (n, P)), -1, 0)


def wtile(w):
    kk, n = w.shape
    kc = kk // P
    return np.ascontiguousarray(w.reshape(kc, P, n).transpose(1, 0, 2).reshape(P, kc * n))


def prep_shared(inp):
    f32 = np.float32
    pfm = np.zeros((P, NPF), f32)
    norms = np.stack([inp["norm_mix"], inp["norm_xattn"], inp["norm_mem"], inp["norm_mlp"]], axis=1)
    pfm[:, PF_GN:PF_GN + L * 32] = fm(norms).reshape(P, L * 32)
    pfm[:, PF_GN + L * 32:PF_GN + L * 32 + 8] = fm(inp["norm_final"]).reshape(P, 8)
    pfm[:, PF_BIN:PF_BIN + L * 36] = fm(inp["b_in"][:, :36 * P]).reshape(P, L * 36)
    pfm[:, PF_LBL:PF_LBL + 32] = fm(inp["hgrn_lb_logits"]).reshape(P, 32)
    pfm[:, PF_HGN:PF_HGN + L * 4] = fm(inp["hgrn_norm"]).reshape(P, L * 4)
    pfm[:, PF_MLN:PF_MLN + L * 4] = fm(inp["mlstm_norm"]).reshape(P, L * 4)
    cw = fm(inp["mlstm_conv_w"])
    pfm[:, PF_CW:PF_CW + L * 40] = cw.transpose(0, 1, 3, 2).reshape(P, L * 40)
    pfm[:, PF_CB:PF_CB + L * 8] = fm(inp["mlstm_conv_b"]).reshape(P, L * 8)

    w_in = np.asarray(inp["w_in"], f32)
    whg = np.empty((L, 4, P, KC * 512), f32)
    whgg = np.empty((L, 4, P, KC * 128), f32)
    wml = np.empty((L, 4, P, KC * 384), f32)
    wmlg = np.empty((L, 4, P, KC * 128), f32)
    wgate = np.empty((L, P, KC * 16), f32)
    for l in range(L):
        for h in range(4):
            cols = np.concatenate([np.arange(b * 512 + h * P, b * 512 + (h + 1) * P) for b in (0, 1, 2, 3)])
            whg[l, h] = wtile(w_in[l][:, cols])
            whgg[l, h] = wtile(w_in[l][:, 4 * 512 + h * P:4 * 512 + (h + 1) * P])
            cols = np.concatenate([np.arange(2560 + b * 512 + h * P, 2560 + b * 512 + (h + 1) * P) for b in (0, 1, 2)])
            wml[l, h] = wtile(w_in[l][:, cols])
            wmlg[l, h] = wtile(w_in[l][:, 2560 + 3 * 512 + h * P:2560 + 3 * 512 + (h + 1) * P])
        wgate[l] = wtile(w_in[l][:, 4608:4624])
    w_up = np.asarray(inp["w_up"], f32)
    w_dn = np.asarray(inp["w_down"], f32)
    wup = np.empty((L, 8, P, KC * 512), f32)
    wdn = np.empty((L, 8, P, 32 * P), f32)
    for l in range(L):
        for g in range(8):
            wup[l, g] = wtile(w_up[l][:, g * 512:(g + 1) * 512])
            wdn[l, g] = wtile(w_dn[l][:, g * P:(g + 1) * P])
    sh = {
        "pfm": pfm,
        "b_in": np.ascontiguousarray(inp["b_in"], f32),
        "w_hg": whg.reshape(L * 4 * P, KC * 512),
        "w_hgg": whgg.reshape(L * 4 * P, KC * 128),
        "w_ml": wml.reshape(L * 4 * P, KC * 384),
        "w_mlg": wmlg.reshape(L * 4 * P, KC * 128),
        "w_gate": wgate.reshape(L * P, KC * 16),
        "w_out": np.stack([wtile(inp["w_out"][l]) for l in range(L)]).reshape(L * P, KC * D),
        "w_xq": np.stack([wtile(inp["w_xq"][l]) for l in range(L)]).reshape(L * P, KC * D),
        "w_xo": np.stack([wtile(inp["w_xo"][l]) for l in range(L)]).reshape(L * P, KC * D),
        "w_xkv": np.stack([wtile(inp["w_xkv"][l]) for l in range(L)]).reshape(L * P, KC * 2 * D),
        "w_up": wup.reshape(L * 8 * P, KC * 512),
        "w_down": wdn.reshape(L * 8 * P, 32 * P),
    }
    return sh


_CACHE = {}


def get_prog(**kw):
    key = repr(sorted(kw.items()))
    if key not in _CACHE:
        p = Prog(**kw)
        p.build()
        _CACHE[key] = p
    return _CACHE[key]


def run(inputs, cores=NCORES, **kw):
    prog = get_prog(**kw)
    sh = prep_shared(inputs)
    x = np.asarray(inputs["x"], np.float32)
    mem = np.asarray(inputs["mem"], np.float32)
    in_maps = []
    for b in range(cores):
        m = dict(sh)
        m["x"] = np.ascontiguousarray(x[b])
        m["mem"] = np.ascontiguousarray(mem[b])
        in_maps.append(m)
    res = run_bass_kernel_spmd(prog.nc, in_maps, core_ids=list(range(cores)))
    return prog, res


def kernel(**inputs):
    prog, res = run(inputs)
    return np.stack([np.asarray(r["out"], np.float32) for r in res.results], axis=0)
```

```python
import numpy as np
from contextlib import ExitStack
import concourse.bass as bass
import concourse.mybir as mybir
from concourse.bass_utils import run_bass_kernel_spmd

F32 = mybir.dt.float32
BF16 = mybir.dt.bfloat16
AF = mybir.ActivationFunctionType
ALU = mybir.AluOpType
AX = mybir.AxisListType

P = 128
T = 2048
D = 1024
KC = 8
NG = 4
GS = 512
NCH = 32
CH = 64
L = 4
MEM = 256
DFF = 4096
EPS = 1e-6
NCORES = 8

PF_GN = 0
PF_BIN = PF_GN + L * 32 + 8
PF_LBL = PF_BIN + L * 36
PF_HGN = PF_LBL + 32
PF_MLN = PF_HGN + L * 4
PF_CW = PF_MLN + L * 4
PF_CB = PF_CW + L * 40
NPF = PF_CB + L * 8


class Tok:
    __slots__ = ("w", "r")

    def __init__(self):
        self.w = None
        self.r = {}


class Eng:
    def __init__(self, k, eng, name, ordered=False):
        self.k = k
        self.eng = eng
        self.name = name
        self.ordered = ordered
        self.sem = None
        self.epoch = 0
        self.cnt = 0
        self.seen = {}
        self.pending = False


EPOCH_LIMIT = 30000


class K:
    def __init__(self, nc, es):
        self.nc = nc
        self.es = es
        self.es_stack = [es]
        self.pe = Eng(self, nc.tensor, "pe", ordered=True)
        self.act = Eng(self, nc.scalar, "act")
        self.dve = Eng(self, nc.vector, "dve")
        self.pool = Eng(self, nc.gpsimd, "pool")
        self.sp = Eng(self, nc.sync, "sp")
        self.engs = [self.pe, self.act, self.dve, self.pool, self.sp]
        self.nsem = 0
        for e in self.engs:
            e.sem = self.newsem(e.name)
        self.dmas = []
        self.dmas_hw = []
        self.dmas_sw = []
        for i in range(48):
            d = Eng(self, None, "dma%d" % i)
            d.sem = self.newsem(d.name)
            self.dmas.append(d)
            (self.dmas_hw if i < 24 else self.dmas_sw).append(d)
        self.dma_rr = 0
        self.dma_rr_hw = 0
        self.dma_rr_sw = 0
        self.uid = 0
        self.log = {e: [] for e in self.engs}

    def newsem(self, name):
        self.nsem += 1
        return self.es.enter_context(self.nc.semaphore("s_%s_%d" % (name, self.nsem)))

    def name(self, base):
        self.uid += 1
        return "%s_%d" % (base, self.uid)

    def sb(self, base, shape, dt):
        return self.es_stack[-1].enter_context(self.nc.sbuf_tensor(self.name(base), shape, dt))

    def scope(self):
        k = self

        class _S:
            def __enter__(s2):
                s2.es = ExitStack()
                k.es_stack.append(s2.es)
                return s2

            def __exit__(s2, *a):
                k.barrier()
                k.es_stack.pop()
                s2.es.close()
                return False
        return _S()

    def ps(self, base, shape, dt):
        return self.es.enter_context(self.nc.psum_tensor(self.name(base), shape, dt))

    def _deps(self, reads, writes):
        deps = {}

        def add(st):
            if st is None:
                return
            key = (st[0], st[1])
            if deps.get(key, 0) < st[2]:
                deps[key] = st[2]

        for t in reads:
            add(t.w)
        for t in writes:
            add(t.w)
            for st in t.r.values():
                add(st)
        return deps

    def _wait(self, e, deps):
        for (e2, ep), c in deps.items():
            if e2 is e and e.ordered:
                continue
            if e.seen.get((e2, ep), 0) >= c:
                continue
            sem = e2.sem if ep == e2.epoch else e2.old_sems[ep]
            e.eng.wait_ge(sem, c)
            e.seen[(e2, ep)] = c
            self.log[e].append(("w", e2, ep, c))

    def op(self, e, fn, reads=(), writes=(), inc=True):
        deps = self._deps(reads, writes)
        self._wait(e, deps)
        ins = fn()
        if inc:
            if e.cnt >= EPOCH_LIMIT:
                if not hasattr(e, "old_sems"):
                    e.old_sems = {}
                e.old_sems[e.epoch] = e.sem
                e.sem = self.newsem(e.name)
                e.epoch += 1
                e.cnt = 0
            ins.then_inc(e.sem, 1)
            e.cnt += 1
            self.log[e].append(("i", e, e.epoch, 1))
            st = (e, e.epoch, e.cnt)
            e.pending = False
        else:
            assert e.cnt + 1 < EPOCH_LIMIT
            st = (e, e.epoch, e.cnt + 1)
            e.pending = True
        for t in reads:
            t.r[e] = st
        for t in writes:
            t.w = st
            t.r = {}
        return ins

    def dma(self, q, out, in_, reads=(), writes=()):
        if q is self.pool:
            d = self.dmas_sw[self.dma_rr_sw % len(self.dmas_sw)]
            self.dma_rr_sw += 1
        else:
            d = self.dmas_hw[self.dma_rr_hw % len(self.dmas_hw)]
            self.dma_rr_hw += 1
        self.dma_rr += 1
        deps = self._deps(reads, writes)
        if d.cnt > 0:
            deps[(d, 0)] = max(deps.get((d, 0), 0), d.cnt)
        self._wait(q, deps)
        q.eng.dma_start(out=out, in_=in_).then_inc(d.sem, 16)
        d.cnt += 16
        self.log[q].append(("i", d, 0, 16))
        st = (d, 0, d.cnt)
        for t in reads:
            t.r[d] = st
        for t in writes:
            t.w = st
            t.r = {}

    def barrier(self):
        for e in self.engs:
            assert not e.pending
        for e in self.engs:
            deps = {}
            for e2 in self.engs + self.dmas:
                if e2 is e or e2.cnt == 0:
                    continue
                deps[(e2, e2.epoch)] = e2.cnt
            self._wait(e, deps)

    def simulate(self):
        val = {}
        pos = {e: 0 for e in self.engs}
        progress = True
        while progress:
            progress = False
            for e in self.engs:
                lg = self.log[e]
                while pos[e] < len(lg):
                    kind, e2, ep, c = lg[pos[e]]
                    if kind == "w":
                        if val.get((e2, ep), 0) < c:
                            break
                    else:
                        val[(e2, ep)] = val.get((e2, ep), 0) + c
                    pos[e] += 1
                    progress = True
        stuck = {e.name: (pos[e], len(self.log[e])) for e in self.engs if pos[e] < len(self.log[e])}
        for e in self.engs:
            if pos[e] < len(self.log[e]):
                kind, e2, ep, c = self.log[e][pos[e]]
                print("STUCK", e.name, "at", pos[e], "/", len(self.log[e]), "waiting", e2.name, ep, c, "have",
                      val.get((e2, ep), 0))
        return stuck

    def mm(self, out, lhsT, rhs, start, stop, reads, writes, inc=None, **kw):
        if inc is None:
            inc = stop
        return self.op(self.pe, lambda: self.nc.tensor.matmul(out, lhsT=lhsT, rhs=rhs, start=start, stop=stop, **kw),
                       reads, writes, inc=inc)

    def tr(self, out, in_, ident, reads, writes, inc=True):
        return self.op(self.pe, lambda: self.nc.tensor.transpose(out, in_, ident), reads, writes, inc=inc)

    def activation(self, out, in_, func, reads, writes, bias=None, scale=None, e=None):
        e = e or self.act
        kw = {}
        if bias is not None:
            kw["bias"] = bias
        if scale is not None:
            kw["scale"] = scale
        return self.op(e, lambda: self.nc.scalar.activation(out=out, in_=in_, func=func, **kw), reads, writes)

    def veng(self, e):
        return self.nc.vector if e is self.dve else self.nc.gpsimd

    def tt(self, out, in0, in1, op, reads, writes, e=None):
        e = e or self.dve
        return self.op(e, lambda: self.veng(e).tensor_tensor(out=out, in0=in0, in1=in1, op=op), reads, writes)

    def ts(self, out, in0, s1, s2, op0, op1, reads, writes, e=None):
        e = e or self.dve
        if op1 is None:
            return self.op(e, lambda: self.veng(e).tensor_single_scalar(out=out, in_=in0, scalar=s1, op=op0), reads, writes)
        return self.op(e, lambda: self.veng(e).tensor_scalar(out=out, in0=in0, scalar1=s1, scalar2=s2, op0=op0, op1=op1),
                       reads, writes)

    def stt(self, out, in0, scalar, in1, op0, op1, reads, writes):
        return self.op(self.dve, lambda: self.nc.vector.scalar_tensor_tensor(out=out, in0=in0, scalar=scalar, in1=in1,
                                                                             op0=op0, op1=op1), reads, writes)

    def copy(self, out, in_, reads, writes, e=None):
        e = e or self.dve
        if e is self.act:
            return self.op(e, lambda: self.nc.scalar.copy(out=out, in_=in_), reads, writes)
        return self.op(e, lambda: self.veng(e).tensor_copy(out=out, in_=in_), reads, writes)

    def memset(self, ap, val, writes, e=None):
        e = e or self.dve
        return self.op(e, lambda: self.veng(e).memset(ap, val), (), writes)


def toks(n):
    return [Tok() for _ in range(n)]


def toks2(n, m):
    return [[Tok() for _ in range(m)] for _ in range(n)]


class Prog:
    def __init__(self, n_layers=L, phases=("mix", "xattn", "mlp"), debug=(), nhg=4, nml=4, do_scan=True, do_post=True,
                 do_wout=True, do_prep=True, stop=99, hbar=0):
        self.hbar = hbar
        self.stop = stop
        self.nhg, self.nml, self.do_scan, self.do_post, self.do_wout, self.do_prep = nhg, nml, do_scan, do_post, do_wout, do_prep
        self.n_layers = n_layers
        self.phases = phases
        self.debug = set(debug)
        self.nc = bass.Bass("TRN2", target_bir_lowering=False)
        self.dbg_out = {}

    def declare(self):
        nc = self.nc

        def inp(name, shape):
            return nc.dram_tensor(name, list(shape), F32, kind="ExternalInput").ap()

        self.d_x = inp("x", [T, D])
        self.d_mem = inp("mem", [MEM, D])
        self.d_pfm = inp("pfm", [P, NPF])
        self.d_bin = inp("b_in", [L, 4624])
        self.d_whg = inp("w_hg", [L * 4 * P, KC * 512])
        self.d_whgg = inp("w_hgg", [L * 4 * P, KC * 128])
        self.d_wml = inp("w_ml", [L * 4 * P, KC * 384])
        self.d_wmlg = inp("w_mlg", [L * 4 * P, KC * 128])
        self.d_wgate = inp("w_gate", [L * P, KC * 16])
        self.d_wout = inp("w_out", [L * P, KC * D])
        self.d_wxq = inp("w_xq", [L * P, KC * D])
        self.d_wxo = inp("w_xo", [L * P, KC * D])
        self.d_wxkv = inp("w_xkv", [L * P, KC * 2 * D])
        self.d_wup = inp("w_up", [L * 8 * P, KC * 512])
        self.d_wdn = inp("w_down", [L * 8 * P, 32 * P])
        self.d_out = nc.dram_tensor("out", [T, D], F32, kind="ExternalOutput").ap()
        self.d_mix = nc.dram_tensor("mix_scr", [8 * P, T], BF16, kind="Internal").ap()

    def dbg(self, name, shape, dt=F32):
        ap = self.nc.dram_tensor("dbg_" + name, list(shape), dt, kind="ExternalOutput").ap()
        self.dbg_out[name] = "dbg_" + name
        return ap

    def build(self):
        self.declare()
        with ExitStack() as es:
            self.es = es
            self.k = K(self.nc, es)
            self.body()
            stuck = self.k.simulate()
            assert not stuck, stuck
            print("instr counts", {e.name: (e.epoch, e.cnt) for e in self.k.engs}, "dmas", self.k.dma_rr)
        return self.nc

    def body(self):
        k = self.k
        nc = self.nc
        self.xT = k.sb("xT", [P, KC * T], F32)
        self.xT_t = toks2(KC, NG)
        self.xn = k.sb("xn", [P, KC * T], BF16)
        self.xn_t = toks2(KC, NG)
        self.pfm = k.sb("pfm", [P, NPF], F32)
        self.pfm_t = Tok()
        self.mix_t = toks2(8, NG)
        self.setup_consts()
        self.pb = [k.ps("pb", [P, 512], F32) for _ in range(6)]
        self.pb_t = toks(6)
        self.pb_rr = 0
        self.ptr = [k.ps("ptr", [P, 1024], BF16) for _ in range(2)]
        self.ptr_t = toks(2)
        self.ptr_rr = 0

        self.load_x()
        for l in range(self.n_layers):
            self.layer(l)
        self.final_out()

    def big(self):
        i = self.pb_rr % 6
        self.pb_rr += 1
        return self.pb[i], self.pb_t[i]

    def trb(self):
        i = self.ptr_rr % 2
        self.ptr_rr += 1
        return self.ptr[i], self.ptr_t[i]

    def setup_consts(self):
        k = self.k
        nc = self.nc
        self.ct = Tok()
        self.identf = k.sb("identf", [P, P], F32)
        self.identb = k.sb("identb", [P, P], BF16)
        self.onesb = k.sb("onesb", [P, P], BF16)
        self.onesf = k.sb("onesf", [P, P], F32)
        self.maskF = k.sb("maskF", [CH, CH], F32)
        self.maskB = k.sb("maskB", [CH, CH], F32)
        self.scanm = k.sb("scanm", [P, GS], F32)
        ct = self.ct
        k.memset(self.onesf[:], 1.0, [ct], e=k.pool)
        k.op(k.pool, lambda: nc.gpsimd.affine_select(out=self.identf[:], in_=self.onesf[:], pattern=[[1, P]],
                                                     compare_op=ALU.is_equal, fill=0.0, base=0, channel_multiplier=-1),
             [ct], [ct])
        k.copy(self.identb[:], self.identf[:], [ct], [ct], e=k.pool)
        k.copy(self.onesb[:], self.onesf[:], [ct], [ct], e=k.pool)
        k.op(k.pool, lambda: nc.gpsimd.affine_select(out=self.maskF[:], in_=self.onesf[0:CH, 0:CH], pattern=[[1, CH]],
                                                     compare_op=ALU.is_ge, fill=0.0, base=0, channel_multiplier=-1),
             [ct], [ct])
        k.op(k.pool, lambda: nc.gpsimd.affine_select(out=self.maskB[:], in_=self.onesf[0:CH, 0:CH], pattern=[[-1, CH]],
                                                     compare_op=ALU.is_ge, fill=0.0, base=0, channel_multiplier=1),
             [ct], [ct])
        self.epsc = k.sb("epsc", [P, 1], F32)
        k.memset(self.epsc[:], EPS, [ct], e=k.pool)
        k.memset(self.scanm[:], 1.0, [ct], e=k.pool)
        k.memset(self.scanm[:].rearrange("p (c j) -> p c j", j=CH)[:, :, 0:1], 0.0, [ct], e=k.pool)
        k.dma(k.sp, self.pfm[:], self.d_pfm, [], [self.pfm_t])
        self.npfm = k.sb("npfm", [P, NPF], F32)
        k.ts(self.npfm[:], self.pfm[:], -1.0, None, ALU.mult, None, [self.pfm_t], [self.pfm_t])
        self.lb = k.sb("lb", [P, 32], F32)
        self.oml = k.sb("oml", [P, 32], F32)
        self.lb_t = Tok()
        ex = k.sb("lbex", [P, 32], F32)
        sm = k.sb("lbsm", [P, 8], F32)
        t = self.lb_t
        k.activation(ex[:], self.pfm[:, PF_LBL:PF_LBL + 32], AF.Exp, [self.pfm_t], [t])
        exv = ex[:].rearrange("p (d l h) -> p d l h", d=2, l=L)
        smv = sm[:].rearrange("p (d h) -> p d h", d=2)
        lbv = self.lb[:].rearrange("p (d l h) -> p d l h", d=2, l=L)
        k.tt(smv, exv[:, :, 0, :], exv[:, :, 1, :], ALU.add, [t], [t])
        k.tt(smv, smv, exv[:, :, 2, :], ALU.add, [t], [t])
        k.tt(smv, smv, exv[:, :, 3, :], ALU.add, [t], [t])
        k.op(k.dve, lambda: nc.vector.reciprocal(out=sm[:], in_=sm[:]), [t], [t])
        for l in range(L):
            k.tt(exv[:, :, l, :], exv[:, :, l, :], smv, ALU.mult, [t], [t])
        k.memset(lbv[:, :, 0, :], 0.0, [t])
        k.copy(lbv[:, :, 1, :], exv[:, :, 1, :], [t], [t])
        k.tt(lbv[:, :, 2, :], lbv[:, :, 1, :], exv[:, :, 2, :], ALU.add, [t], [t])
        k.tt(lbv[:, :, 3, :], lbv[:, :, 2, :], exv[:, :, 3, :], ALU.add, [t], [t])
        k.ts(self.oml[:], self.lb[:], -1.0, 1.0, ALU.mult, ALU.add, [t], [t])
        self.noml = k.sb("noml", [P, 32], F32)
        k.ts(self.noml[:], self.oml[:], -1.0, None, ALU.mult, None, [t], [t])

    def load_tm_to_fm(self, dram, ntile, dst, dst_cols, dst_tok_fn):
        k = self.k
        stg = [k.sb("ldstg", [P, D], F32) for _ in range(2)]
        stg_t = toks(2)
        for i in range(ntile):
            s, st = stg[i % 2], stg_t[i % 2]
            k.dma(k.sp, s[:], dram[i * P:(i + 1) * P, :], [], [st])
            for half in range(2):
                pb, pt = self.big()
                for cc in range(4):
                    c = half * 4 + cc
                    k.tr(pb[:, cc * P:(cc + 1) * P], s[:, c * P:(c + 1) * P], self.identf[:], [st, self.ct], [pt],
                         inc=(cc == 3))
                dv = dst[:].rearrange("p (c t) -> p c t", c=KC)[:, half * 4:(half + 1) * 4, i * P:(i + 1) * P]
                k.copy(dv, pb[:].rearrange("p (c t) -> p c t", c=4), [pt], dst_tok_fn(i, half),
                       e=(k.act if (i + half) % 2 else k.dve))

    def load_x(self):
        with self.k.scope():
            self._load_x()

    def _load_x(self):
        self.load_tm_to_fm(self.d_x, T // P, self.xT, T,
                           lambda i, half: [self.xT_t[c][i // 4] for c in range(half * 4, half * 4 + 4)])

    def rmsnorm_fm(self, src, src_t, ncols, gcol, dst, dst_t, out_dt_is_bf16=True):
        k = self.k
        nc = self.nc
        ng = (ncols + GS - 1) // GS
        gs = min(GS, ncols)
        self._nsq = [k.sb("nsq", [P, KC * GS], BF16) for _ in range(2)]
        self._nsq_t = toks(2)
        self._nrs = [k.sb("nrs", [P, GS], F32) for _ in range(2)]
        self._nrs_t = toks(2)
        self._n_rr = 0
        for g in range(ng):
            i = self._n_rr % 2
            self._n_rr += 1
            sq, sqt, rs, rst = self._nsq[i], self._nsq_t[i], self._nrs[i], self._nrs_t[i]
            srcv = src[:].rearrange("p (c t) -> p c t", c=KC)[:, :, g * gs:(g + 1) * gs]
            sqv = sq[:, 0:KC * gs].rearrange("p (c t) -> p c t", c=KC)
            k.activation(sqv, srcv, AF.Square, [src_t[c][g] for c in range(KC)], [sqt])
            pb, pt = self.big()
            for c in range(KC):
                k.mm(pb[:, 0:gs], self.onesb[:], sq[:, c * gs:(c + 1) * gs], c == 0, c == KC - 1, [sqt, self.ct], [pt])
            k.activation(rs[:, 0:gs], pb[:, 0:gs], AF.Ln, [pt, self.ct], [rst], bias=self.epsc[:, 0:1], scale=1.0 / D)
            k.activation(rs[:, 0:gs], rs[:, 0:gs], AF.Exp, [rst], [rst], scale=-0.5)
            for c in range(KC):
                k.stt(dst[:, c * ncols + g * gs: c * ncols + (g + 1) * gs],
                      src[:, c * ncols + g * gs: c * ncols + (g + 1) * gs],
                      self.pfm[:, gcol + c: gcol + c + 1], rs[:, 0:gs], ALU.mult, ALU.mult,
                      [src_t[c][g], rst, self.pfm_t], [dst_t[c][g]])

    def final_out(self):
        k = self.k
        with k.scope():
            self.rmsnorm_fm(self.xT, self.xT_t, T, PF_GN + L * 32, self.xT, self.xT_t)
        es2 = k.scope()
        es2.__enter__()
        stg = [k.sb("ostg", [P, D], F32) for _ in range(2)]
        stg_t = toks(2)
        xv = self.xT[:].rearrange("p (c t) -> p c t", c=KC)
        for i in range(T // P):
            s, st = stg[i % 2], stg_t[i % 2]
            for half in range(2):
                pb, pt = self.big()
                for cc in range(4):
                    c = half * 4 + cc
                    k.tr(pb[:, cc * P:(cc + 1) * P], xv[:, c, i * P:(i + 1) * P], self.identf[:],
                         [self.xT_t[c][i // 4], self.ct], [pt], inc=(cc == 3))
                k.copy(s[:, half * 512:(half + 1) * 512], pb[:], [pt], [st], e=(k.act if half else k.dve))
            k.dma(k.sp, self.d_out[i * P:(i + 1) * P, :], s[:], [st], [])
        es2.__exit__(None, None, None)

    def layer(self, l):
        k = self.k
        if "mix" in self.phases:
            with k.scope():
                self.mixer(l)
            if self.do_wout:
                with k.scope():
                    self.wout_phase(l)
        if "xattn" in self.phases:
            with k.scope():
                self.rmsnorm_fm(self.xT, self.xT_t, T, PF_GN + l * 32 + 8, self.xn, self.xn_t)
            with k.scope():
                self.xattn_phase(l)
        if "mlp" in self.phases:
            with k.scope():
                self.mlp_phase(l)

    def mixer(self, l):
        import types
        k = self.k
        m = self.m = types.SimpleNamespace()
        SW = self.SW = 130
        m.wh = k.sb("wh", [P, KC * 512], BF16)
        m.wh_t = Tok()
        m.whg = k.sb("whg", [P, KC * 128], BF16)
        m.whg_t = Tok()
        self.issue_main(l, 0)
        with k.scope():
            self.rmsnorm_fm(self.xT, self.xT_t, T, PF_GN + l * 32 + 0, self.xn, self.xn_t)
        m.stage = [k.sb("stage", [CH, NCH * SW], F32) for _ in range(2)]
        m.stage_t = [toks(NCH) for _ in range(2)]
        m.ktm = [k.sb("ktm", [CH, NCH * P], BF16) for _ in range(2)]
        m.ktm_t = [toks(4) for _ in range(2)]
        m.vcm = k.sb("vcm", [CH, NCH * SW], BF16)
        m.vcm_t = toks(8)
        m.QT = [k.sb("QT", [P, T], BF16)]
        m.QT_t = [toks(NG)]
        m.KT = [k.sb("KT", [P, T], BF16)]
        m.KT_t = [toks(NG)]
        m.R = [k.sb("R", [P, SW], F32) for _ in range(2)]
        m.R_t = toks(2)
        m.S = [[k.sb("S", [P, SW], BF16) for _ in range(3)] for _ in range(2)]
        m.S_t = toks2(2, 3)
        m.sc = [k.sb("sc", [CH, CH], BF16) for _ in range(4)]
        m.sc_t = toks(4)
        m.sc_rr = 0
        m.bv = k.sb("bv", [CH, P], F32)
        m.bv_t = Tok()
        m.gt = [k.sb("gateg", [P, GS], F32)] * 2
        m.gt_t = [Tok()] * 2
        m.mixh = [k.sb("mixh", [P, GS], BF16) for _ in range(2)]
        m.mixh_t = toks(2)
        m.ms = k.sb("ms", [CH, NCH], F32)
        m.small_t = Tok()
        m.tA = k.sb("tA", [P, GS], F32)
        m.tA_t = Tok()
        vv = m.vcm[:].rearrange("p (c e) -> p c e", e=SW)
        k.memset(vv[:, :, P:P + 1], 1.0, m.vcm_t)
        with k.scope():
            m.QT.append(k.sb("QTb", [P, T], BF16))
            m.QT_t.append(toks(NG))
            m.KT.append(k.sb("KTb", [P, T], BF16))
            m.KT_t.append(toks(NG))
            m.q2 = [k.sb("q32", [P, GS], F32) for _ in range(2)]
            m.q2_t = toks(2)
            m.tB2 = [k.sb("tB", [P, GS], F32) for _ in range(2)]
            m.tB2_t = toks(2)
            m.atab = [k.sb("atab", [P, NCH], F32) for _ in range(2)]
            m.atab_t = toks(2)
            for h in range(self.nhg):
                self.hgrn_head(l, h)
        m.QT = m.QT[:1]
        m.QT_t = m.QT_t[:1]
        m.KT = m.KT[:1]
        m.KT_t = m.KT_t[:1]
        with k.scope():
            m.xpad = k.sb("xpad", [P, T + 4], F32)
            m.xpad_t = toks(NG)
            m.G = k.sb("G", [CH, NCH * 16], F32)
            m.lfg = k.sb("lfg", [CH, 256], F32)
            m.cum = k.sb("cum", [CH, 256], F32)
            m.e1 = k.sb("e1", [CH, 256], F32)
            m.e2 = k.sb("e2", [CH, 256], F32)
            m.abc = k.sb("abc", [P, 256], F32)
            m.wg = k.sb("wg", [P, KC * 16], BF16)
            m.bg = k.sb("bg", [CH, 16], F32)
            m.lnk = k.sb("lnk", [CH, 1], F32)
            m.fac = k.sb("fac", [CH, 2 * NCH], F32)
            m.gates_t = Tok()
            m.tE = k.sb("tE", [P, GS], F32)
            m.tE_t = Tok()
            k.memset(m.xpad[:, 0:2], 0.0, m.xpad_t)
            k.memset(m.xpad[:, T + 2:T + 4], 0.0, m.xpad_t)
            if self.nml:
                self.mlstm_gates(l)
            for h in range(self.nml):
                self.mlstm_head(l, h)

    def sig_inplace(self, ap, tok):
        k = self.k
        np_ = ap.partition_size() if callable(getattr(ap, "partition_size", None)) else P
        k.activation(ap, ap, AF.Ln, [tok, self.ct], [tok], bias=self.onesf[0:np_, 0:1])
        k.activation(ap, ap, AF.Exp, [tok], [tok], scale=-1.0)

    def issue_main(self, l, idx):
        k = self.k
        m = self.m
        if idx < 4:
            if idx >= self.nhg:
                return
            k.dma(k.pool, m.wh[:, 0:KC * 512], self.d_whg[(l * 4 + idx) * P:(l * 4 + idx + 1) * P, :], [], [m.wh_t])
        elif idx < 8:
            if idx - 4 >= self.nml:
                return
            h = idx - 4
            k.dma(k.pool, m.wh[:, 0:KC * 384], self.d_wml[(l * 4 + h) * P:(l * 4 + h + 1) * P, :], [], [m.wh_t])

    def issue_gate(self, l, idx):
        k = self.k
        m = self.m
        src = self.d_whgg if idx < 4 else self.d_wmlg
        h = idx % 4
        k.dma(k.pool, m.whg[:], src[(l * 4 + h) * P:(l * 4 + h + 1) * P, :], [], [m.whg_t])

    @staticmethod
    def interleave(gens, window=3):
        active = []
        gens = list(gens)
        while gens or active:
            while gens and len(active) < window:
                active.append(gens.pop(0))
            nxt = []
            for gen in active:
                try:
                    next(gen)
                    nxt.append(gen)
                except StopIteration:
                    pass
            active = nxt

    def proj_group(self, blk, stride, g, gate=False):
        k = self.k
        m = self.m
        w, wt = (m.whg, m.whg_t) if gate else (m.wh, m.wh_t)
        pb, pt = self.big()
        for kc in range(KC):
            k.mm(pb[:, :], w[:, kc * stride + blk * P: kc * stride + (blk + 1) * P],
                 self.xn[:, kc * T + g * GS: kc * T + (g + 1) * GS], kc == 0, kc == KC - 1,
                 [wt, self.xn_t[kc][g]], [pt])
        return pb, pt

    def v_proj(self, l, vblk, stride, bias_col0):
        k = self.k
        m = self.m
        SW = self.SW
        k.dma(k.sp, m.bv[:], self.d_bin[l, bias_col0:bias_col0 + P].partition_broadcast(CH), [], [m.bv_t])
        vv = m.vcm[:].rearrange("p (c e) -> p c e", e=SW)
        for c4 in range(8):
            pb, pt = self.big()
            for cc in range(4):
                c = c4 * 4 + cc
                for kc in range(KC):
                    k.mm(pb[0:CH, cc * P:(cc + 1) * P], self.xn[:, kc * T + c * CH: kc * T + (c + 1) * CH],
                         m.wh[:, kc * stride + vblk * P: kc * stride + (vblk + 1) * P], kc == 0, kc == KC - 1,
                         [self.xn_t[kc][c // 8], m.wh_t], [pt])
            k.tt(vv[:, c4 * 4:(c4 + 1) * 4, 0:P], pb[0:CH, :].rearrange("p (c e) -> p c e", e=P),
                 m.bv[:].unsqueeze(1).to_broadcast([CH, 4, P]), ALU.add, [pt, m.bv_t], [m.vcm_t[c4]])

    def k_transposes(self, d_src, scale_fn):
        k = self.k
        m = self.m
        for c8 in range(4):
            ptb, ptt = self.trb()
            for cc in range(8):
                c = c8 * 8 + cc
                k.tr(ptb[0:CH, cc * P:(cc + 1) * P], m.KT[d_src][:, c * CH:(c + 1) * CH], self.identb[:],
                     [m.KT_t[d_src][c // 8], self.ct], [ptt], inc=(cc == 7))
            yield c8, ptb, ptt

    def hgrn_head(self, l, h):
        k = self.k
        nc = self.nc
        m = self.m
        stride = 512
        if self.hbar:
            k.barrier()
        self.issue_gate(l, h)
        if self.stop <= 1:
            return
        self.v_proj(l, 3, stride, 1536 + h * P)
        if self.stop <= 2:
            return
        bq = PF_BIN + l * 36 + h

        def q_chain(g):
            q, qtk = m.q2[g % 2], m.q2_t[g % 2]
            pb, pt = self.proj_group(0, stride, g)
            yield
            k.activation(q[:], pb[:, :], AF.Identity, [pt, self.pfm_t], [qtk], bias=self.pfm[:, bq:bq + 1])
            k.activation(pb[:, :], pb[:, :], AF.Exp, [pt, self.pfm_t], [pt], bias=self.npfm[:, bq:bq + 1], scale=-1.0)
            yield
            k.activation(pb[:, :], pb[:, :], AF.Ln, [pt, self.ct], [pt], bias=self.onesf[:, 0:1])
            yield
            k.activation(pb[:, :], pb[:, :], AF.Exp, [pt], [pt], scale=-1.0)
            yield
            k.tt(q[:], q[:], pb[:, :], ALU.mult, [qtk, pt], [qtk])
            yield

        def d_chain(g, d):
            q, qtk = m.q2[g % 2], m.q2_t[g % 2]
            bcol = PF_BIN + l * 36 + 4 + 4 * d + h
            idx = d * 16 + l * 4 + h
            tB, tBt = m.tB2[d], m.tB2_t[d]
            pa, pat = self.proj_group(1 + d, stride, g)
            yield
            k.activation(pa[:, :], pa[:, :], AF.Exp, [pat, self.pfm_t], [pat], bias=self.npfm[:, bcol:bcol + 1],
                         scale=-1.0)
            yield
            k.activation(pa[:, :], pa[:, :], AF.Ln, [pat, self.ct], [pat], bias=self.onesf[:, 0:1])
            yield
            k.activation(pa[:, :], pa[:, :], AF.Exp, [pat], [pat], scale=-1.0)
            yield
            k.activation(tB[:], pa[:, :], AF.Ln, [pat, self.lb_t], [tBt], scale=self.oml[:, idx:idx + 1],
                         bias=self.lb[:, idx:idx + 1])
            k.activation(pa[:, :], pa[:, :], AF.Identity, [pat, self.lb_t], [pat], scale=self.noml[:, idx:idx + 1],
                         bias=self.oml[:, idx:idx + 1])
            yield
            pc, pct = self.big()
            if d == 0:
                k.op(k.dve, lambda: nc.vector.tensor_tensor_scan(out=pc[:, :], data0=self.scanm[:], data1=tB[:],
                                                                 initial=0.0, op0=ALU.mult, op1=ALU.add),
                     [tBt, self.ct], [pct])
            else:
                k.op(k.dve, lambda: nc.vector.tensor_tensor_scan(out=pc[:, ::-1], data0=self.scanm[:],
                                                                 data1=tB[:, ::-1], initial=0.0, op0=ALU.mult,
                                                                 op1=ALU.add),
                     [tBt, self.ct], [pct])
            yield
            cv = pc[:, :].rearrange("p (c j) -> p c j", j=CH)
            edge = CH - 1 if d == 0 else 0
            k.activation(m.atab[d][:, g * 8:(g + 1) * 8], cv[:, :, edge], AF.Exp, [pct], [m.atab_t[d]])
            k.activation(tB[:], pc[:, :], AF.Exp, [pct], [tBt])
            yield
            k.tt(m.QT[d][:, g * GS:(g + 1) * GS], q[:], tB[:], ALU.mult, [qtk, tBt], [m.QT_t[d][g]])
            yield
            k.activation(tB[:], pc[:, :], AF.Exp, [pct], [tBt], scale=-1.0)
            yield
            k.tt(m.KT[d][:, g * GS:(g + 1) * GS], pa[:, :], tB[:], ALU.mult, [pat, tBt], [m.KT_t[d][g]])
            yield

        self.interleave([q_chain(0)], window=1)
        for g in range(NG):
            wave = [d_chain(g, 0), d_chain(g, 1)]
            if g + 1 < NG:
                wave.append(q_chain(g + 1))
            self.interleave(wave, window=3)
        self.issue_main(l, h + 1)
        if self.stop <= 3:
            return
        for d in range(2):
            for c8, ptb, ptt in self.k_transposes(d, None):
                k.copy(m.ktm[d][0:CH, c8 * 1024:(c8 + 1) * 1024], ptb[0:CH, :], [ptt], [m.ktm_t[d][c8]],
                       e=(k.act if c8 % 2 else k.dve))
        if self.do_scan:
            self.scan_head(P, False, lambda d, c: (m.atab[d][:, c:c + 1], m.atab_t[d]), None)
        if self.do_post:
            self.post_head(l, h, h, False, 0, 128, AF.Silu, PF_BIN + l * 36 + 16 + h, PF_HGN + l * 4 + h)

    def mlstm_gates(self, l):
        k = self.k
        nc = self.nc
        m = self.m
        gt = m.gates_t
        k.dma(k.pool, m.wg[:], self.d_wgate[l * P:(l + 1) * P, :], [], [gt])
        k.dma(k.sp, m.bg[:], self.d_bin[l, 4608:4624].partition_broadcast(CH), [], [gt])
        k.memset(m.lnk[:], -0.5 * float(np.log(128.0)), [gt])
        pb, pt = self.big()
        for c in range(NCH):
            for kc in range(KC):
                k.mm(pb[0:CH, c * 16:(c + 1) * 16], self.xn[:, kc * T + c * CH: kc * T + (c + 1) * CH],
                     m.wg[:, kc * 16:(kc + 1) * 16], kc == 0, kc == KC - 1, [gt, self.xn_t[kc][c // 8]], [pt])
        Gv = m.G[:].rearrange("p (c j) -> p c j", j=16)
        k.tt(Gv, pb[0:CH, :].rearrange("p (c j) -> p c j", j=16), m.bg[:].unsqueeze(1).to_broadcast([CH, NCH, 16]),
             ALU.add, [pt, gt], [gt])
        lv = m.lfg[:].rearrange("p (d c h) -> p d c h", d=2, h=4)
        for d in range(2):
            k.activation(lv[:, d], Gv[:, :, 8 + 4 * d:12 + 4 * d], AF.Exp, [gt], [gt], scale=-1.0)
        k.ts(m.lfg[:], m.lfg[:], 1.0, None, ALU.add, None, [gt], [gt])
        k.activation(m.lfg[:], m.lfg[:], AF.Ln, [gt], [gt])
        k.ts(m.lfg[:], m.lfg[:], -1.0, None, ALU.mult, None, [gt], [gt])
        pb, pt = self.big()
        k.mm(pb[0:CH, 0:128], self.maskF[:], m.lfg[:, 0:128], True, True, [gt, self.ct], [pt])
        k.mm(pb[0:CH, 128:256], self.maskB[:], m.lfg[:, 128:256], True, True, [gt, self.ct], [pt])
        k.copy(m.cum[:], pb[0:CH, 0:256], [pt], [gt])
        k.activation(m.e1[:], m.cum[:], AF.Exp, [gt], [gt])
        e2v = m.e2[:].rearrange("p (d c h) -> p d c h", d=2, h=4)
        cumv = m.cum[:].rearrange("p (d c h) -> p d c h", d=2, h=4)
        for d in range(2):
            k.tt(e2v[:, d], Gv[:, :, 4 * d:4 * d + 4], cumv[:, d], ALU.subtract, [gt], [gt])
        k.activation(m.e2[:], m.e2[:], AF.Exp, [gt], [gt], bias=m.lnk[:, 0:1])
        pb, pt = self.big()
        k.mm(pb[:, 0:256], self.onesf[0:CH, :], m.lfg[:], True, True, [gt, self.ct], [pt])
        k.activation(m.abc[:], pb[:, 0:256], AF.Exp, [pt], [gt])

    def conv_proj(self, l, blk, stride, cidx, bias_col, dst, dst_t):
        k = self.k
        m = self.m
        for g in range(NG):
            pb, pt = self.proj_group(blk, stride, g)
            k.activation(m.xpad[:, 2 + g * GS: 2 + (g + 1) * GS], pb[:, :], AF.Identity, [pt, self.pfm_t], [m.xpad_t[g]],
                         bias=self.pfm[:, bias_col:bias_col + 1])
        wcol = PF_CW + l * 40 + cidx * 5
        cb = PF_CB + l * 8 + cidx

        def chain(g):
            rd = [m.xpad_t[gg] for gg in range(max(0, g - 1), min(NG, g + 2))]
            pa, pat = self.big()
            k.ts(pa[:, :], m.xpad[:, g * GS: g * GS + GS], self.pfm[:, wcol:wcol + 1], self.pfm[:, cb:cb + 1], ALU.mult,
                 ALU.add, rd + [self.pfm_t], [pat])
            yield
            for j in range(1, 5):
                k.stt(pa[:, :], m.xpad[:, g * GS + j: g * GS + j + GS], self.pfm[:, wcol + j:wcol + j + 1], pa[:, :],
                      ALU.mult, ALU.add, rd + [pat, self.pfm_t], [pat])
                yield
            k.activation(m.tE[:], pa[:, :], AF.Exp, [pat], [m.tE_t], scale=-1.0)
            k.activation(m.tE[:], m.tE[:], AF.Ln, [m.tE_t, self.ct], [m.tE_t], bias=self.onesf[:, 0:1])
            k.activation(m.tE[:], m.tE[:], AF.Exp, [m.tE_t], [m.tE_t], scale=-1.0)
            k.tt(dst[:, g * GS:(g + 1) * GS], pa[:, :], m.tE[:], ALU.mult, [pat, m.tE_t], [dst_t[g]])
            yield

        self.interleave([chain(g) for g in range(NG)], window=2)

    def mlstm_head(self, l, h):
        k = self.k
        m = self.m
        stride = 384
        if self.hbar:
            k.barrier()
        self.issue_gate(l, 4 + h)
        self.v_proj(l, 2, stride, 3584 + h * P)
        self.conv_proj(l, 0, stride, h, PF_BIN + l * 36 + 20 + h, m.QT[0], m.QT_t[0])
        self.conv_proj(l, 1, stride, 4 + h, PF_BIN + l * 36 + 24 + h, m.KT[0], m.KT_t[0])
        self.issue_main(l, 4 + h + 1)
        for c8, ptb, ptt in self.k_transposes(0, None):
            for d in range(2):
                e2d = m.e2[:, d * 128:(d + 1) * 128].rearrange("p (c h) -> p c h", h=4)[:, c8 * 8:(c8 + 1) * 8, h:h + 1]
                k.tt(m.ktm[d][0:CH, c8 * 1024:(c8 + 1) * 1024].rearrange("p (c e) -> p c e", e=P),
                     ptb[0:CH, :].rearrange("p (c e) -> p c e", e=P), e2d.to_broadcast([CH, 8, P]), ALU.mult,
                     [ptt, m.gates_t], [m.ktm_t[d][c8]])
        if self.do_scan:
            self.scan_head(P + 1, True,
                           lambda d, c: (m.abc[:, d * 128 + c * 4 + h: d * 128 + c * 4 + h + 1], m.gates_t),
                           lambda d, c: m.e2[0:CH, d * 128 + c * 4 + h: d * 128 + c * 4 + h + 1])
        if self.do_post:
            self.post_head(l, h, 4 + h, True, 0, 128, AF.Sigmoid, PF_BIN + l * 36 + 32 + h, PF_MLN + l * 4 + h)

    def scan_head(self, E, mlstm, a_fn, e2_fn):
        k = self.k
        m = self.m
        SW = self.SW

        def front(i, d):
            dq = 0 if mlstm else d
            c = i if d == 0 else NCH - 1 - i
            g = c // 8
            vslot = m.vcm[0:CH, c * SW: c * SW + E]
            vt = m.vcm_t[c // 4]
            qsl = m.QT[dq][:, c * CH:(c + 1) * CH]
            qt = m.QT_t[dq][g]
            pbk, pst = self.big()
            psl = pbk[0:CH, 0:CH]
            k.mm(psl, m.KT[dq][:, c * CH:(c + 1) * CH], qsl, True, True, [m.KT_t[dq][g], qt], [pst])
            sc, sct = m.sc[m.sc_rr % 4], m.sc_t[m.sc_rr % 4]
            m.sc_rr += 1
            mask = self.maskF if d == 0 else self.maskB
            if mlstm:
                k.stt(sc[:], psl, e2_fn(d, c), mask[:], ALU.mult, ALU.mult, [pst, m.gates_t, self.ct], [sct])
            else:
                k.tt(sc[:], psl, mask[:], ALU.mult, [pst, self.ct], [sct])
            pbk, put = self.big()
            pul = pbk[:, 0:E]
            k.mm(pul, m.ktm[d][0:CH, c * P:(c + 1) * P], vslot, True, True, [m.ktm_t[d][c // 8], vt], [put])
            if i == 0:
                k.copy(m.R[d][:, 0:E], pul, [put], [m.R_t[d]])
            else:
                cprev = c - 1 if d == 0 else c + 1
                ap_prev, atok = a_fn(d, cprev)
                k.stt(m.R[d][:, 0:E], m.R[d][:, 0:E], ap_prev, pul, ALU.mult, ALU.add, [put, m.R_t[d], atok],
                      [m.R_t[d]])
            if i < NCH - 1:
                ap_c, atok = a_fn(d, c)
                k.activation(m.S[d][(i + 1) % 3][:, 0:E], m.R[d][:, 0:E], AF.Identity, [m.R_t[d], atok],
                             [m.S_t[d][(i + 1) % 3]], scale=ap_c)
            return (c, g, vslot, vt, qsl, qt, sc, sct)

        def back(i, d, st):
            c, g, vslot, vt, qsl, qt, sc, sct = st
            pbk, pot = self.big()
            pol = pbk[0:CH, 0:E]
            if i > 0:
                k.mm(pol, qsl, m.S[d][i % 3][:, 0:E], True, False, [qt, m.S_t[d][i % 3]], [pot])
            k.mm(pol, sc[:], vslot, i == 0, True, [sct, vt], [pot])
            dst = m.stage[d][0:CH, c * SW: c * SW + E]
            k.copy(dst, pol, [pot], [m.stage_t[d][c]], e=k.act)

        pending = []
        for i in range(NCH):
            for d in range(2):
                pending.append((i, d, front(i, d)))
                if len(pending) > 2:
                    back(*pending.pop(0))
        while pending:
            back(*pending.pop(0))

    def post_head(self, l, h, hidx, mlstm, gblk, stride, gfunc, gbias_col, gain_col):
        k = self.k
        nc = self.nc
        m = self.m
        SW = self.SW
        sA = m.stage[0][:].rearrange("p (c e) -> p c e", e=SW)
        sB = m.stage[1][:].rearrange("p (c e) -> p c e", e=SW)
        tA_all = m.stage_t[0]
        tB_all = m.stage_t[1]
        st = m.small_t
        if mlstm:
            for d in range(2):
                sv = sA if d == 0 else sB
                tv = tA_all if d == 0 else tB_all
                e1d = m.e1[:, d * 128:(d + 1) * 128].rearrange("p (c h) -> p c h", h=4)[:, :, h]
                t1 = m.fac[:, d * NCH:(d + 1) * NCH]
                k.tt(t1, sv[:, :, P], e1d, ALU.mult, tv + [m.gates_t], [st])
                k.activation(t1, t1, AF.Abs, [st], [st])
                k.ts(t1, t1, 1.0, None, ALU.max, None, [st], [st])
                k.op(k.dve, lambda: nc.vector.reciprocal(out=t1, in_=t1), [st], [st])
                k.tt(t1, t1, e1d, ALU.mult, [st, m.gates_t], [st])
                k.tt(sv[:, :, 0:P], sv[:, :, 0:P], t1.unsqueeze(2).to_broadcast([CH, NCH, P]), ALU.mult, tv + [st], tv)
        k.tt(sA[:, :, 0:P], sA[:, :, 0:P], sB[:, :, 0:P], ALU.add, tA_all + tB_all, tA_all)
        k.tt(sB[:, :, 0:P], sA[:, :, 0:P], sA[:, :, 0:P], ALU.mult, tA_all, tB_all)
        k.op(k.dve, lambda: nc.vector.tensor_reduce(out=m.ms[:], in_=sB[:, :, 0:P], axis=AX.X, op=ALU.add), tB_all, [st])
        k.activation(m.ms[:], m.ms[:], AF.Ln, [st, self.ct], [st], scale=1.0 / P, bias=self.epsc[0:CH, 0:1])
        k.activation(m.ms[:], m.ms[:], AF.Exp, [st], [st], scale=-0.5)
        hn = m.ktm[0]
        k.tt(hn[:].rearrange("p (c e) -> p c e", e=P), sA[:, :, 0:P], m.ms[:].unsqueeze(2).to_broadcast([CH, NCH, P]),
             ALU.mult, tA_all + [st], m.ktm_t[0])
        for g in range(NG):
            pb, pt = self.proj_group(gblk, stride, g, gate=True)
            gt, gtt = m.gt[g % 2], m.gt_t[g % 2]
            k.activation(gt[:], pb[:, :], AF.Exp, [pt, self.pfm_t], [gtt], bias=self.npfm[:, gbias_col:gbias_col + 1],
                         scale=-1.0)
            if gfunc == AF.Silu:
                k.activation(m.tA[:], pb[:, :], AF.Identity, [pt, self.pfm_t], [m.tA_t],
                             bias=self.pfm[:, gbias_col:gbias_col + 1])
            self.sig_inplace(gt[:], gtt)
            if gfunc == AF.Silu:
                k.tt(gt[:], gt[:], m.tA[:], ALU.mult, [gtt, m.tA_t], [gtt])
            ptb, ptt = self.trb()
            for cc in range(8):
                c = g * 8 + cc
                k.tr(ptb[:, cc * CH:(cc + 1) * CH], hn[0:CH, c * P:(c + 1) * P], self.identb[0:CH, 0:CH],
                     [m.ktm_t[0][c // 8], self.ct], [ptt], inc=(cc == 7))
            mh, mht = m.mixh[g % 2], m.mixh_t[g % 2]
            k.stt(mh[:], ptb[:, 0:GS], self.pfm[:, gain_col:gain_col + 1], gt[:], ALU.mult, ALU.mult,
                  [ptt, gtt, self.pfm_t], [mht])
            k.dma(k.sp, self.d_mix[hidx * P:(hidx + 1) * P, g * GS:(g + 1) * GS], mh[:], [mht], [self.mix_t[hidx][g]])

    def resid_add(self, pb, pt, n, g):
        k = self.k
        sl = self.xT[:, n * T + g * GS: n * T + (g + 1) * GS]
        k.tt(sl, pb[:, 0:GS], sl, ALU.add, [pt, self.xT_t[n][g]], [self.xT_t[n][g]])

    def mlp_phase(self, l):
        k = self.k
        wup = [k.sb("wup", [P, KC * 512], BF16) for _ in range(2)]
        wup_t = toks(2)
        wdn = [k.sb("wdn", [P, 32 * P], BF16) for _ in range(2)]
        wdn_t = toks(2)
        rt = [k.sb("rtmp", [P, GS], F32) for _ in range(2)]
        rt_t = toks(2)
        rr = 0
        k.dma(k.pool, wup[0][:], self.d_wup[(l * 8) * P:(l * 8 + 1) * P, :], [], [wup_t[0]])
        k.dma(k.pool, wup[1][:], self.d_wup[(l * 8 + 1) * P:(l * 8 + 2) * P, :], [], [wup_t[1]])
        with k.scope():
            self.rmsnorm_fm(self.xT, self.xT_t, T, PF_GN + l * 32 + 24, self.xn, self.xn_t)
        hT = k.sb("hT", [P, 32 * 1024], BF16)
        hT_t = toks2(32, 2)

        def ld_up(fg):
            k.dma(k.pool, wup[fg % 2][:], self.d_wup[(l * 8 + fg) * P:(l * 8 + fg + 1) * P, :], [], [wup_t[fg % 2]])

        def ld_dn(n):
            k.dma(k.pool, wdn[n % 2][:], self.d_wdn[(l * 8 + n) * P:(l * 8 + n + 1) * P, :], [], [wdn_t[n % 2]])

        for hf in range(2):
            if hf > 0:
                ld_up(0)
            for fg in range(8):
                if fg + 1 < 8 and not (hf == 0 and fg == 0):
                    ld_up(fg + 1)
                else:
                    ld_dn(0)
                wu, wut = wup[fg % 2], wup_t[fg % 2]
                for fc in range(4):
                    for g2 in range(2):
                        g = hf * 2 + g2
                        pb, pt = self.big()
                        for kc in range(KC):
                            k.mm(pb[:, :], wu[:, kc * 512 + fc * P: kc * 512 + (fc + 1) * P],
                                 self.xn[:, kc * T + g * GS: kc * T + (g + 1) * GS], kc == 0, kc == KC - 1,
                                 [wut, self.xn_t[kc][g]], [pt])
                        r, rtk = rt[rr % 2], rt_t[rr % 2]
                        rr += 1
                        k.activation(r[:], pb[:], AF.Relu, [pt], [rtk])
                        f = fg * 4 + fc
                        k.tt(hT[:, f * 1024 + g2 * GS: f * 1024 + (g2 + 1) * GS], r[:], r[:], ALU.mult, [rtk],
                             [hT_t[f][g2]])
            for n in range(8):
                if n + 1 < 8:
                    ld_dn(n + 1)
                wd, wdt = wdn[n % 2], wdn_t[n % 2]
                for g2 in range(2):
                    g = hf * 2 + g2
                    pb, pt = self.big()
                    for f in range(32):
                        k.mm(pb[:, :], wd[:, f * P:(f + 1) * P], hT[:, f * 1024 + g2 * GS: f * 1024 + (g2 + 1) * GS],
                             f == 0, f == 31, [wdt, hT_t[f][g2]], [pt])
                    self.resid_add(pb, pt, n, g)

    def wout_phase(self, l):
        k = self.k
        wo = k.sb("wout", [P, KC * D], BF16)
        wo_t = Tok()
        mb = [k.sb("mixb", [P, KC * GS], BF16) for _ in range(2)]
        mb_t = toks(2)
        k.dma(k.pool, wo[:], self.d_wout[l * P:(l + 1) * P, :], [], [wo_t])
        dmv = self.d_mix.rearrange("(c p) t -> p c t", p=P)

        act = list(range(self.nhg)) + [4 + i for i in range(self.nml)]
        full = len(act) == 8

        def ld(g):
            if full:
                k.dma(k.sp, mb[g % 2][:].rearrange("p (c t) -> p c t", c=KC), dmv[:, :, g * GS:(g + 1) * GS],
                      [self.mix_t[c][g] for c in range(8)], [mb_t[g % 2]])
            else:
                for c in act:
                    k.dma(k.sp, mb[g % 2][:, c * GS:(c + 1) * GS], self.d_mix[c * P:(c + 1) * P, g * GS:(g + 1) * GS],
                          [self.mix_t[c][g]], [mb_t[g % 2]])

        ld(0)
        for g in range(NG):
            if g + 1 < NG:
                ld(g + 1)
            for n in range(8):
                pb, pt = self.big()
                for kc in act:
                    k.mm(pb[:, :], wo[:, kc * D + n * P: kc * D + (n + 1) * P], mb[g % 2][:, kc * GS:(kc + 1) * GS],
                         kc == act[0], kc == act[-1], [wo_t, mb_t[g % 2]], [pt])
                self.resid_add(pb, pt, n, g)

    def xattn_phase(self, l):
        k = self.k
        nc = self.nc
        memn = k.sb("memn", [P, KC * MEM], BF16)
        memn_t = toks2(KC, 1)
        kT = k.sb("kT", [P, 8 * MEM], BF16)
        kT_t = Tok()
        vsb = k.sb("vsb", [P, 2 * D], BF16)
        vsb_t = Tok()
        with k.scope():
            wkv = k.sb("wkv", [P, KC * 2 * D], BF16)
            wkv_t = Tok()
            k.dma(k.pool, wkv[:], self.d_wxkv[l * P:(l + 1) * P, :], [], [wkv_t])
            memT = k.sb("memT", [P, KC * MEM], F32)
            memT_t = toks2(KC, 1)
            self.load_tm_to_fm(self.d_mem, MEM // P, memT, MEM,
                               lambda i, half: [memT_t[c][0] for c in range(half * 4, half * 4 + 4)])
            self.rmsnorm_fm(memT, memT_t, MEM, PF_GN + l * 32 + 16, memn, memn_t)
            for n in range(8):
                pb, pt = self.big()
                for kc in range(KC):
                    k.mm(pb[:, 0:MEM], wkv[:, kc * 2 * D + n * P: kc * 2 * D + (n + 1) * P],
                         memn[:, kc * MEM:(kc + 1) * MEM], kc == 0, kc == KC - 1, [wkv_t, memn_t[kc][0]], [pt])
                k.copy(kT[:, n * MEM:(n + 1) * MEM], pb[:, 0:MEM], [pt], [kT_t], e=k.act)
            for jc in range(2):
                for ng in range(2):
                    pb, pt = self.big()
                    for kc in range(KC):
                        k.mm(pb[:, :], memn[:, kc * MEM + jc * P: kc * MEM + (jc + 1) * P],
                             wkv[:, kc * 2 * D + D + ng * GS: kc * 2 * D + D + (ng + 1) * GS], kc == 0, kc == KC - 1,
                             [wkv_t, memn_t[kc][0]], [pt])
                    k.copy(vsb[:, jc * D + ng * GS: jc * D + (ng + 1) * GS], pb[:, :], [pt], [vsb_t], e=k.act)
        wq = k.sb("wq", [P, KC * D], BF16)
        wq_t = Tok()
        wo = k.sb("wxo", [P, KC * D], BF16)
        wo_t = Tok()
        k.dma(k.pool, wq[:], self.d_wxq[l * P:(l + 1) * P, :], [], [wq_t])
        k.dma(k.pool, wo[:], self.d_wxo[l * P:(l + 1) * P, :], [], [wo_t])
        qT = [k.sb("qT", [P, 8 * GS], BF16) for _ in range(2)]
        qT_t = toks(2)
        oT = [k.sb("oT", [P, 8 * GS], BF16) for _ in range(2)]
        oT_t = toks(2)
        pT = [k.sb("pT", [P, 2 * GS], BF16) for _ in range(2)]
        pT_t = toks(2)
        rc = [k.sb("rc", [P, GS], F32) for _ in range(2)]
        rc_t = toks(2)
        pr = 0
        for g in range(NG):
            q, qt = qT[g % 2], qT_t[g % 2]
            o, ot = oT[g % 2], oT_t[g % 2]
            for n in range(8):
                pb, pt = self.big()
                for kc in range(KC):
                    k.mm(pb[:, :], wq[:, kc * D + n * P: kc * D + (n + 1) * P],
                         self.xn[:, kc * T + g * GS: kc * T + (g + 1) * GS], kc == 0, kc == KC - 1,
                         [wq_t, self.xn_t[kc][g]], [pt])
                k.copy(q[:, n * GS:(n + 1) * GS], pb[:, :], [pt], [qt], e=k.act)
            for h in range(4):
                p_, ptk = pT[pr % 2], pT_t[pr % 2]
                r_, rtk = rc[pr % 2], rc_t[pr % 2]
                pr += 1
                for jc in range(2):
                    pb, pt = self.big()
                    for hc in range(2):
                        n = h * 2 + hc
                        k.mm(pb[:, :], kT[:, n * MEM + jc * P: n * MEM + (jc + 1) * P], q[:, n * GS:(n + 1) * GS],
                             hc == 0, hc == 1, [kT_t, qt], [pt])
                    k.activation(p_[:, jc * GS:(jc + 1) * GS], pb[:, :], AF.Exp, [pt], [ptk], scale=1.0 / 16.0)
                pb, pt = self.big()
                for jc in range(2):
                    k.mm(pb[:, :], self.onesb[:], p_[:, jc * GS:(jc + 1) * GS], jc == 0, jc == 1, [self.ct, ptk], [pt])
                k.activation(r_[:], pb[:, :], AF.Ln, [pt], [rtk])
                k.activation(r_[:], r_[:], AF.Exp, [rtk], [rtk], scale=-1.0)
                for hc in range(2):
                    n = h * 2 + hc
                    pb, pt = self.big()
                    for jc in range(2):
                        k.mm(pb[:, :], vsb[:, jc * D + n * P: jc * D + (n + 1) * P], p_[:, jc * GS:(jc + 1) * GS],
                             jc == 0, jc == 1, [vsb_t, ptk], [pt])
                    k.tt(o[:, n * GS:(n + 1) * GS], pb[:, :], r_[:], ALU.mult, [pt, rtk], [ot])
            for n in range(8):
                pb, pt = self.big()
                for kc in range(KC):
                    k.mm(pb[:, :], wo[:, kc * D + n * P: kc * D + (n + 1) * P], o[:, kc * GS:(kc + 1) * GS],
                         kc == 0, kc == KC - 1, [wo_t, ot], [pt])
                self.resid_add(pb, pt, n, g)


def fm(v):
    v = np.asarray(v, np.float32)
    n = v.shape[-1] // P
    lead = v.shape[:-1]
    return np.moveaxis(v.reshape(lead + (n, P)), -1, 0)


def wtile(w):
    kk, n = w.shape
    kc = kk // P
    return np.ascontiguousarray(w.reshape(kc, P, n).transpose(1, 0, 2).reshape(P, kc * n))


def prep_shared(inp):
    f32 = np.float32
    pfm = np.zeros((P, NPF), f32)
    norms = np.stack([inp["norm_mix"], inp["norm_xattn"], inp["norm_mem"], inp["norm_mlp"]], axis=1)
    pfm[:, PF_GN:PF_GN + L * 32] = fm(norms).reshape(P, L * 32)
    pfm[:, PF_GN + L * 32:PF_GN + L * 32 + 8] = fm(inp["norm_final"]).reshape(P, 8)
    pfm[:, PF_BIN:PF_BIN + L * 36] = fm(inp["b_in"][:, :36 * P]).reshape(P, L * 36)
    pfm[:, PF_LBL:PF_LBL + 32] = fm(inp["hgrn_lb_logits"]).reshape(P, 32)
    pfm[:, PF_HGN:PF_HGN + L * 4] = fm(inp["hgrn_norm"]).reshape(P, L * 4)
    pfm[:, PF_MLN:PF_MLN + L * 4] = fm(inp["mlstm_norm"]).reshape(P, L * 4)
    cw = fm(inp["mlstm_conv_w"])
    pfm[:, PF_CW:PF_CW + L * 40] = cw.transpose(0, 1, 3, 2).reshape(P, L * 40)
    pfm[:, PF_CB:PF_CB + L * 8] = fm(inp["mlstm_conv_b"]).reshape(P, L * 8)

    w_in = np.asarray(inp["w_in"], f32)
    whg = np.empty((L, 4, P, KC * 512), f32)
    whgg = np.empty((L, 4, P, KC * 128), f32)
    wml = np.empty((L, 4, P, KC * 384), f32)
    wmlg = np.empty((L, 4, P, KC * 128), f32)
    wgate = np.empty((L, P, KC * 16), f32)
    for l in range(L):
        for h in range(4):
            cols = np.concatenate([np.arange(b * 512 + h * P, b * 512 + (h + 1) * P) for b in (0, 1, 2, 3)])
            whg[l, h] = wtile(w_in[l][:, cols])
            whgg[l, h] = wtile(w_in[l][:, 4 * 512 + h * P:4 * 512 + (h + 1) * P])
            cols = np.concatenate([np.arange(2560 + b * 512 + h * P, 2560 + b * 512 + (h + 1) * P) for b in (0, 1, 2)])
            wml[l, h] = wtile(w_in[l][:, cols])
            wmlg[l, h] = wtile(w_in[l][:, 2560 + 3 * 512 + h * P:2560 + 3 * 512 + (h + 1) * P])
        wgate[l] = wtile(w_in[l][:, 4608:4624])
    w_up = np.asarray(inp["w_up"], f32)
    w_dn = np.asarray(inp["w_down"], f32)
    wup = np.empty((L, 8, P, KC * 512), f32)
    wdn = np.empty((L, 8, P, 32 * P), f32)
    for l in range(L):
        for g in range(8):
            wup[l, g] = wtile(w_up[l][:, g * 512:(g + 1) * 512])
            wdn[l, g] = wtile(w_dn[l][:, g * P:(g + 1) * P])
    sh = {
        "pfm": pfm,
        "b_in": np.ascontiguousarray(inp["b_in"], f32),
        "w_hg": whg.reshape(L * 4 * P, KC * 512),
        "w_hgg": whgg.reshape(L * 4 * P, KC * 128),
        "w_ml": wml.reshape(L * 4 * P, KC * 384),
        "w_mlg": wmlg.reshape(L * 4 * P, KC * 128),
        "w_gate": wgate.reshape(L * P, KC * 16),
        "w_out": np.stack([wtile(inp["w_out"][l]) for l in range(L)]).reshape(L * P, KC * D),
        "w_xq": np.stack([wtile(inp["w_xq"][l]) for l in range(L)]).reshape(L * P, KC * D),
        "w_xo": np.stack([wtile(inp["w_xo"][l]) for l in range(L)]).reshape(L * P, KC * D),
        "w_xkv": np.stack([wtile(inp["w_xkv"][l]) for l in range(L)]).reshape(L * P, KC * 2 * D),
        "w_up": wup.reshape(L * 8 * P, KC * 512),
        "w_down": wdn.reshape(L * 8 * P, 32 * P),
    }
    return sh


_CACHE = {}


def get_prog(**kw):
    key = repr(sorted(kw.items()))
    if key not in _CACHE:
        p = Prog(**kw)
        p.build()
        _CACHE[key] = p
    return _CACHE[key]


def run(inputs, cores=NCORES, **kw):
    prog = get_prog(**kw)
    sh = prep_shared(inputs)
    x = np.asarray(inputs["x"], np.float32)
    mem = np.asarray(inputs["mem"], np.float32)
    in_maps = []
    for b in range(cores):
        m = dict(sh)
        m["x"] = np.ascontiguousarray(x[b])
        m["mem"] = np.ascontiguousarray(mem[b])
        in_maps.append(m)
    res = run_bass_kernel_spmd(prog.nc, in_maps, core_ids=list(range(cores)))
    return prog, res


def kernel(**inputs):
    prog, res = run(inputs)
    return np.stack([np.asarray(r["out"], np.float32) for r in res.results], axis=0)
```

```python
import numpy as np
from contextlib import ExitStack
import concourse.bass as bass
import concourse.mybir as mybir
from concourse.bass_utils import run_bass_kernel_spmd

F32 = mybir.dt.float32
BF16 = mybir.dt.bfloat16
AF = mybir.ActivationFunctionType
ALU = mybir.AluOpType
AX = mybir.AxisListType

P = 128
T = 2048
D = 1024
KC = 8
NG = 4
GS = 512
NCH = 32
CH = 64
L = 4
MEM = 256
DFF = 4096
EPS = 1e-6
NCORES = 8

PF_GN = 0
PF_BIN = PF_GN + L * 32 + 8
PF_LBL = PF_BIN + L * 36
PF_HGN = PF_LBL + 32
PF_MLN = PF_HGN + L * 4
PF_CW = PF_MLN + L * 4
PF_CB = PF_CW + L * 40
NPF = PF_CB + L * 8


class Tok:
    __slots__ = ("w", "r")

    def __init__(self):
        self.w = None
        self.r = {}


class Eng:
    def __init__(self, k, eng, name, ordered=False):
        self.k = k
        self.eng = eng
        self.name = name
        self.ordered = ordered
        self.sem = None
        self.epoch = 0
        self.cnt = 0
        self.seen = {}
        self.pending = False


EPOCH_LIMIT = 30000


class K:
    def __init__(self, nc, es):
        self.nc = nc
        self.es = es
        self.es_stack = [es]
        self.pe = Eng(self, nc.tensor, "pe", ordered=True)
        self.act = Eng(self, nc.scalar, "act")
        self.dve = Eng(self, nc.vector, "dve")
        self.pool = Eng(self, nc.gpsimd, "pool")
        self.sp = Eng(self, nc.sync, "sp")
        self.engs = [self.pe, self.act, self.dve, self.pool, self.sp]
        self.nsem = 0
        for e in self.engs:
            e.sem = self.newsem(e.name)
        self.dmas = []
        self.dmas_hw = []
        self.dmas_sw = []
        for i in range(48):
            d = Eng(self, None, "dma%d" % i)
            d.sem = self.newsem(d.name)
            self.dmas.append(d)
            (self.dmas_hw if i < 24 else self.dmas_sw).append(d)
        self.dma_rr = 0
        self.dma_rr_hw = 0
        self.dma_rr_sw = 0
        self.uid = 0
        self.log = {e: [] for e in self.engs}

    def newsem(self, name):
        self.nsem += 1
        return self.es.enter_context(self.nc.semaphore("s_%s_%d" % (name, self.nsem)))

    def name(self, base):
        self.uid += 1
        return "%s_%d" % (base, self.uid)

    def sb(self, base, shape, dt):
        return self.es_stack[-1].enter_context(self.nc.sbuf_tensor(self.name(base), shape, dt))

    def scope(self):
        k = self

        class _S:
            def __enter__(s2):
                s2.es = ExitStack()
                k.es_stack.append(s2.es)
                return s2

            def __exit__(s2, *a):
                k.barrier()
                k.es_stack.pop()
                s2.es.close()
                return False
        return _S()

    def ps(self, base, shape, dt):
        return self.es.enter_context(self.nc.psum_tensor(self.name(base), shape, dt))

    def _deps(self, reads, writes):
        deps = {}

        def add(st):
            if st is None:
                return
            key = (st[0], st[1])
            if deps.get(key, 0) < st[2]:
                deps[key] = st[2]

        for t in reads:
            add(t.w)
        for t in writes:
            add(t.w)
            for st in t.r.values():
                add(st)
        return deps

    def _wait(self, e, deps):
        for (e2, ep), c in deps.items():
            if e2 is e and e.ordered:
                continue
            if e.seen.get((e2, ep), 0) >= c:
                continue
            sem = e2.sem if ep == e2.epoch else e2.old_sems[ep]
            e.eng.wait_ge(sem, c)
            e.seen[(e2, ep)] = c
            self.log[e].append(("w", e2, ep, c))

    def op(self, e, fn, reads=(), writes=(), inc=True):
        deps = self._deps(reads, writes)
        self._wait(e, deps)
        ins = fn()
        if inc:
            if e.cnt >= EPOCH_LIMIT:
                if not hasattr(e, "old_sems"):
                    e.old_sems = {}
                e.old_sems[e.epoch] = e.sem
                e.sem = self.newsem(e.name)
                e.epoch += 1
                e.cnt = 0
            ins.then_inc(e.sem, 1)
            e.cnt += 1
            self.log[e].append(("i", e, e.epoch, 1))
            st = (e, e.epoch, e.cnt)
            e.pending = False
        else:
            assert e.cnt + 1 < EPOCH_LIMIT
            st = (e, e.epoch, e.cnt + 1)
            e.pending = True
        for t in reads:
            t.r[e] = st
        for t in writes:
            t.w = st
            t.r = {}
        return ins

    def dma(self, q, out, in_, reads=(), writes=()):
        if q is self.pool:
            d = self.dmas_sw[self.dma_rr_sw % len(self.dmas_sw)]
            self.dma_rr_sw += 1
        else:
            d = self.dmas_hw[self.dma_rr_hw % len(self.dmas_hw)]
            self.dma_rr_hw += 1
        self.dma_rr += 1
        deps = self._deps(reads, writes)
        if d.cnt > 0:
            deps[(d, 0)] = max(deps.get((d, 0), 0), d.cnt)
        self._wait(q, deps)
        q.eng.dma_start(out=out, in_=in_).then_inc(d.sem, 16)
        d.cnt += 16
        self.log[q].append(("i", d, 0, 16))
        st = (d, 0, d.cnt)
        for t in reads:
            t.r[d] = st
        for t in writes:
            t.w = st
            t.r = {}

    def barrier(self):
        for e in self.engs:
            assert not e.pending
        for e in self.engs:
            deps = {}
            for e2 in self.engs + self.dmas:
                if e2 is e or e2.cnt == 0:
                    continue
                deps[(e2, e2.epoch)] = e2.cnt
            self._wait(e, deps)

    def simulate(self):
        val = {}
        pos = {e: 0 for e in self.engs}
        progress = True
        while progress:
            progress = False
            for e in self.engs:
                lg = self.log[e]
                while pos[e] < len(lg):
                    kind, e2, ep, c = lg[pos[e]]
                    if kind == "w":
                        if val.get((e2, ep), 0) < c:
                            break
                    else:
                        val[(e2, ep)] = val.get((e2, ep), 0) + c
                    pos[e] += 1
                    progress = True
        stuck = {e.name: (pos[e], len(self.log[e])) for e in self.engs if pos[e] < len(self.log[e])}
        for e in self.engs:
            if pos[e] < len(self.log[e]):
                kind, e2, ep, c = self.log[e][pos[e]]
                print("STUCK", e.name, "at", pos[e], "/", len(self.log[e]), "waiting", e2.name, ep, c, "have",
                      val.get((e2, ep), 0))
        return stuck

    def mm(self, out, lhsT, rhs, start, stop, reads, writes, inc=None, **kw):
        if inc is None:
            inc = stop
        return self.op(self.pe, lambda: self.nc.tensor.matmul(out, lhsT=lhsT, rhs=rhs, start=start, stop=stop, **kw),
                       reads, writes, inc=inc)

    def tr(self, out, in_, ident, reads, writes, inc=True):
        return self.op(self.pe, lambda: self.nc.tensor.transpose(out, in_, ident), reads, writes, inc=inc)

    def activation(self, out, in_, func, reads, writes, bias=None, scale=None, e=None):
        e = e or self.act
        kw = {}
        if bias is not None:
            kw["bias"] = bias
        if scale is not None:
            kw["scale"] = scale
        return self.op(e, lambda: self.nc.scalar.activation(out=out, in_=in_, func=func, **kw), reads, writes)

    def veng(self, e):
        return self.nc.vector if e is self.dve else self.nc.gpsimd

    def tt(self, out, in0, in1, op, reads, writes, e=None):
        e = e or self.dve
        return self.op(e, lambda: self.veng(e).tensor_tensor(out=out, in0=in0, in1=in1, op=op), reads, writes)

    def ts(self, out, in0, s1, s2, op0, op1, reads, writes, e=None):
        e = e or self.dve
        if op1 is None:
            return self.op(e, lambda: self.veng(e).tensor_single_scalar(out=out, in_=in0, scalar=s1, op=op0), reads, writes)
        return self.op(e, lambda: self.veng(e).tensor_scalar(out=out, in0=in0, scalar1=s1, scalar2=s2, op0=op0, op1=op1),
                       reads, writes)

    def stt(self, out, in0, scalar, in1, op0, op1, reads, writes):
        return self.op(self.dve, lambda: self.nc.vector.scalar_tensor_tensor(out=out, in0=in0, scalar=scalar, in1=in1,
                                                                             op0=op0, op1=op1), reads, writes)

    def copy(self, out, in_, reads, writes, e=None):
        e = e or self.dve
        if e is self.act:
            return self.op(e, lambda: self.nc.scalar.copy(out=out, in_=in_), reads, writes)
        return self.op(e, lambda: self.veng(e).tensor_copy(out=out, in_=in_), reads, writes)

    def memset(self, ap, val, writes, e=None):
        e = e or self.dve
        return self.op(e, lambda: self.veng(e).memset(ap, val), (), writes)


def toks(n):
    return [Tok() for _ in range(n)]


def toks2(n, m):
    return [[Tok() for _ in range(m)] for _ in range(n)]


class Prog:
    def __init__(self, n_layers=L, phases=("mix", "xattn", "mlp"), debug=(), nhg=4, nml=4, do_scan=True, do_post=True,
                 do_wout=True, do_prep=True, stop=99, hbar=0):
        self.hbar = hbar
        self.stop = stop
        self.nhg, self.nml, self.do_scan, self.do_post, self.do_wout, self.do_prep = nhg, nml, do_scan, do_post, do_wout, do_prep
        self.n_layers = n_layers
        self.phases = phases
        self.debug = set(debug)
        self.nc = bass.Bass("TRN2", target_bir_lowering=False)
        self.dbg_out = {}

    def declare(self):
        nc = self.nc

        def inp(name, shape):
            return nc.dram_tensor(name, list(shape), F32, kind="ExternalInput").ap()

        self.d_x = inp("x", [T, D])
        self.d_mem = inp("mem", [MEM, D])
        self.d_pfm = inp("pfm", [P, NPF])
        self.d_bin = inp("b_in", [L, 4624])
        self.d_whg = inp("w_hg", [L * 4 * P, KC * 512])
        self.d_whgg = inp("w_hgg", [L * 4 * P, KC * 128])
        self.d_wml = inp("w_ml", [L * 4 * P, KC * 384])
        self.d_wmlg = inp("w_mlg", [L * 4 * P, KC * 128])
        self.d_wgate = inp("w_gate", [L * P, KC * 16])
        self.d_wout = inp("w_out", [L * P, KC * D])
        self.d_wxq = inp("w_xq", [L * P, KC * D])
        self.d_wxo = inp("w_xo", [L * P, KC * D])
        self.d_wxkv = inp("w_xkv", [L * P, KC * 2 * D])
        self.d_wup = inp("w_up", [L * 8 * P, KC * 512])
        self.d_wdn = inp("w_down", [L * 8 * P, 32 * P])
        self.d_out = nc.dram_tensor("out", [T, D], F32, kind="ExternalOutput").ap()
        self.d_mix = nc.dram_tensor("mix_scr", [8 * P, T], BF16, kind="Internal").ap()

    def dbg(self, name, shape, dt=F32):
        ap = self.nc.dram_tensor("dbg_" + name, list(shape), dt, kind="ExternalOutput").ap()
        self.dbg_out[name] = "dbg_" + name
        return ap

    def build(self):
        self.declare()
        with ExitStack() as es:
            self.es = es
            self.k = K(self.nc, es)
            self.body()
            stuck = self.k.simulate()
            assert not stuck, stuck
            print("instr counts", {e.name: (e.epoch, e.cnt) for e in self.k.engs}, "dmas", self.k.dma_rr)
        return self.nc

    def body(self):
        k = self.k
        nc = self.nc
        self.xT = k.sb("xT", [P, KC * T], F32)
        self.xT_t = toks2(KC, NG)
        self.xn = k.sb("xn", [P, KC * T], BF16)
        self.xn_t = toks2(KC, NG)
        self.pfm = k.sb("pfm", [P, NPF], F32)
        self.pfm_t = Tok()
        self.mix_t = toks2(8, NG)
        self.setup_consts()
        self.pb = [k.ps("pb", [P, 512], F32) for _ in range(6)]
        self.pb_t = toks(6)
        self.pb_rr = 0
        self.ptr = [k.ps("ptr", [P, 1024], BF16) for _ in range(2)]
        self.ptr_t = toks(2)
        self.ptr_rr = 0

        self.load_x()
        for l in range(self.n_layers):
            self.layer(l)
        self.final_out()

    def big(self):
        i = self.pb_rr % 6
        self.pb_rr += 1
        return self.pb[i], self.pb_t[i]

    def trb(self):
        i = self.ptr_rr % 2
        self.ptr_rr += 1
        return self.ptr[i], self.ptr_t[i]

    def setup_consts(self):
        k = self.k
        nc = self.nc
        self.ct = Tok()
        self.identf = k.sb("identf", [P, P], F32)
        self.identb = k.sb("identb", [P, P], BF16)
        self.onesb = k.sb("onesb", [P, P], BF16)
        self.onesf = k.sb("onesf", [P, P], F32)
        self.maskF = k.sb("maskF", [CH, CH], F32)
        self.maskB = k.sb("maskB", [CH, CH], F32)
        self.scanm = k.sb("scanm", [P, GS], F32)
        ct = self.ct
        k.memset(self.onesf[:], 1.0, [ct], e=k.pool)
        k.op(k.pool, lambda: nc.gpsimd.affine_select(out=self.identf[:], in_=self.onesf[:], pattern=[[1, P]],
                                                     compare_op=ALU.is_equal, fill=0.0, base=0, channel_multiplier=-1),
             [ct], [ct])
        k.copy(self.identb[:], self.identf[:], [ct], [ct], e=k.pool)
        k.copy(self.onesb[:], self.onesf[:], [ct], [ct], e=k.pool)
        k.op(k.pool, lambda: nc.gpsimd.affine_select(out=self.maskF[:], in_=self.onesf[0:CH, 0:CH], pattern=[[1, CH]],
                                                     compare_op=ALU.is_ge, fill=0.0, base=0, channel_multiplier=-1),
             [ct], [ct])
        k.op(k.pool, lambda: nc.gpsimd.affine_select(out=self.maskB[:], in_=self.onesf[0:CH, 0:CH], pattern=[[-1, CH]],
                                                     compare_op=ALU.is_ge, fill=0.0, base=0, channel_multiplier=1),
             [ct], [ct])
        self.epsc = k.sb("epsc", [P, 1], F32)
        k.memset(self.epsc[:], EPS, [ct], e=k.pool)
        k.memset(self.scanm[:], 1.0, [ct], e=k.pool)
        k.memset(self.scanm[:].rearrange("p (c j) -> p c j", j=CH)[:, :, 0:1], 0.0, [ct], e=k.pool)
        k.dma(k.sp, self.pfm[:], self.d_pfm, [], [self.pfm_t])
        self.npfm = k.sb("npfm", [P, NPF], F32)
        k.ts(self.npfm[:], self.pfm[:], -1.0, None, ALU.mult, None, [self.pfm_t], [self.pfm_t])
        self.lb = k.sb("lb", [P, 32], F32)
        self.oml = k.sb("oml", [P, 32], F32)
        self.lb_t = Tok()
        ex = k.sb("lbex", [P, 32], F32)
        sm = k.sb("lbsm", [P, 8], F32)
        t = self.lb_t
        k.activation(ex[:], self.pfm[:, PF_LBL:PF_LBL + 32], AF.Exp, [self.pfm_t], [t])
        exv = ex[:].rearrange("p (d l h) -> p d l h", d=2, l=L)
        smv = sm[:].rearrange("p (d h) -> p d h", d=2)
        lbv = self.lb[:].rearrange("p (d l h) -> p d l h", d=2, l=L)
        k.tt(smv, exv[:, :, 0, :], exv[:, :, 1, :], ALU.add, [t], [t])
        k.tt(smv, smv, exv[:, :, 2, :], ALU.add, [t], [t])
        k.tt(smv, smv, exv[:, :, 3, :], ALU.add, [t], [t])
        k.op(k.dve, lambda: nc.vector.reciprocal(out=sm[:], in_=sm[:]), [t], [t])
        for l in range(L):
            k.tt(exv[:, :, l, :], exv[:, :, l, :], smv, ALU.mult, [t], [t])
        k.memset(lbv[:, :, 0, :], 0.0, [t])
        k.copy(lbv[:, :, 1, :], exv[:, :, 1, :], [t], [t])
        k.tt(lbv[:, :, 2, :], lbv[:, :, 1, :], exv[:, :, 2, :], ALU.add, [t], [t])
        k.tt(lbv[:, :, 3, :], lbv[:, :, 2, :], exv[:, :, 3, :], ALU.add, [t], [t])
        k.ts(self.oml[:], self.lb[:], -1.0, 1.0, ALU.mult, ALU.add, [t], [t])
        self.noml = k.sb("noml", [P, 32], F32)
        k.ts(self.noml[:], self.oml[:], -1.0, None, ALU.mult, None, [t], [t])

    def load_tm_to_fm(self, dram, ntile, dst, dst_cols, dst_tok_fn):
        k = self.k
        stg = [k.sb("ldstg", [P, D], F32) for _ in range(2)]
        stg_t = toks(2)
        for i in range(ntile):
            s, st = stg[i % 2], stg_t[i % 2]
            k.dma(k.sp, s[:], dram[i * P:(i + 1) * P, :], [], [st])
            for half in range(2):
                pb, pt = self.big()
                for cc in range(4):
                    c = half * 4 + cc
                    k.tr(pb[:, cc * P:(cc + 1) * P], s[:, c * P:(c + 1) * P], self.identf[:], [st, self.ct], [pt],
                         inc=(cc == 3))
                dv = dst[:].rearrange("p (c t) -> p c t", c=KC)[:, half * 4:(half + 1) * 4, i * P:(i + 1) * P]
                k.copy(dv, pb[:].rearrange("p (c t) -> p c t", c=4), [pt], dst_tok_fn(i, half),
                       e=(k.act if (i + half) % 2 else k.dve))

    def load_x(self):
        with self.k.scope():
            self._load_x()

    def _load_x(self):
        self.load_tm_to_fm(self.d_x, T // P, self.xT, T,
                           lambda i, half: [self.xT_t[c][i // 4] for c in range(half * 4, half * 4 + 4)])

    def rmsnorm_fm(self, src, src_t, ncols, gcol, dst, dst_t, out_dt_is_bf16=True):
        k = self.k
        nc = self.nc
        ng = (ncols + GS - 1) // GS
        gs = min(GS, ncols)
        self._nsq = [k.sb("nsq", [P, KC * GS], BF16) for _ in range(2)]
        self._nsq_t = toks(2)
        self._nrs = [k.sb("nrs", [P, GS], F32) for _ in range(2)]
        self._nrs_t = toks(2)
        self._n_rr = 0
        for g in range(ng):
            i = self._n_rr % 2
            self._n_rr += 1
            sq, sqt, rs, rst = self._nsq[i], self._nsq_t[i], self._nrs[i], self._nrs_t[i]
            srcv = src[:].rearrange("p (c t) -> p c t", c=KC)[:, :, g * gs:(g + 1) * gs]
            sqv = sq[:, 0:KC * gs].rearrange("p (c t) -> p c t", c=KC)
            k.activation(sqv, srcv, AF.Square, [src_t[c][g] for c in range(KC)], [sqt])
            pb, pt = self.big()
            for c in range(KC):
                k.mm(pb[:, 0:gs], self.onesb[:], sq[:, c * gs:(c + 1) * gs], c == 0, c == KC - 1, [sqt, self.ct], [pt])
            k.activation(rs[:, 0:gs], pb[:, 0:gs], AF.Ln, [pt, self.ct], [rst], bias=self.epsc[:, 0:1], scale=1.0 / D)
            k.activation(rs[:, 0:gs], rs[:, 0:gs], AF.Exp, [rst], [rst], scale=-0.5)
            for c in range(KC):
                k.stt(dst[:, c * ncols + g * gs: c * ncols + (g + 1) * gs],
                      src[:, c * ncols + g * gs: c * ncols + (g + 1) * gs],
                      self.pfm[:, gcol + c: gcol + c + 1], rs[:, 0:gs], ALU.mult, ALU.mult,
                      [src_t[c][g], rst, self.pfm_t], [dst_t[c][g]])

    def final_out(self):
        k = self.k
        with k.scope():
            self.rmsnorm_fm(self.xT, self.xT_t, T, PF_GN + L * 32, self.xT, self.xT_t)
        es2 = k.scope()
        es2.__enter__()
        stg = [k.sb("ostg", [P, D], F32) for _ in range(2)]
        stg_t = toks(2)
        xv = self.xT[:].rearrange("p (c t) -> p c t", c=KC)
        for i in range(T // P):
            s, st = stg[i % 2], stg_t[i % 2]
            for half in range(2):
                pb, pt = self.big()
                for cc in range(4):
                    c = half * 4 + cc
                    k.tr(pb[:, cc * P:(cc + 1) * P], xv[:, c, i * P:(i + 1) * P], self.identf[:],
                         [self.xT_t[c][i // 4], self.ct], [pt], inc=(cc == 3))
                k.copy(s[:, half * 512:(half + 1) * 512], pb[:], [pt], [st], e=(k.act if half else k.dve))
            k.dma(k.sp, self.d_out[i * P:(i + 1) * P, :], s[:], [st], [])
        es2.__exit__(None, None, None)

    def layer(self, l):
        k = self.k
        if "mix" in self.phases:
            with k.scope():
                self.mixer(l)
            if self.do_wout:
                with k.scope():
                    self.wout_phase(l)
        if "xattn" in self.phases:
            with k.scope():
                self.xattn_phase(l)
        if "mlp" in self.phases:
            with k.scope():
                self.mlp_phase(l)

    def mixer(self, l):
        import types
        k = self.k
        m = self.m = types.SimpleNamespace()
        SW = self.SW = 130
        m.wh = k.sb("wh", [P, KC * 512], BF16)
        m.wh_t = Tok()
        m.whg = k.sb("whg", [P, KC * 128], BF16)
        m.whg_t = Tok()
        self.issue_main(l, 0)
        with k.scope():
            self.rmsnorm_fm(self.xT, self.xT_t, T, PF_GN + l * 32 + 0, self.xn, self.xn_t)
        m.stage = [k.sb("stage", [CH, NCH * SW], F32) for _ in range(2)]
        m.stage_t = [toks(NCH) for _ in range(2)]
        m.ktm = [k.sb("ktm", [CH, NCH * P], BF16) for _ in range(2)]
        m.ktm_t = [toks(4) for _ in range(2)]
        m.vcm = k.sb("vcm", [CH, NCH * SW], BF16)
        m.vcm_t = toks(8)
        m.QT = [k.sb("QT", [P, T], BF16)]
        m.QT_t = [toks(NG)]
        m.KT = [k.sb("KT", [P, T], BF16)]
        m.KT_t = [toks(NG)]
        m.R = [k.sb("R", [P, SW], F32) for _ in range(2)]
        m.R_t = toks(2)
        m.S = [[k.sb("S", [P, SW], BF16) for _ in range(3)] for _ in range(2)]
        m.S_t = toks2(2, 3)
        m.sc = [k.sb("sc", [CH, CH], BF16) for _ in range(4)]
        m.sc_t = toks(4)
        m.sc_rr = 0
        m.bv = k.sb("bv", [CH, P], F32)
        m.bv_t = Tok()
        m.gt = [k.sb("gateg", [P, GS], F32)] * 2
        m.gt_t = [Tok()] * 2
        m.mixh = [k.sb("mixh", [P, GS], BF16) for _ in range(2)]
        m.mixh_t = toks(2)
        m.ms = k.sb("ms", [CH, NCH], F32)
        m.small_t = Tok()
        m.tA = k.sb("tA", [P, GS], F32)
        m.tA_t = Tok()
        vv = m.vcm[:].rearrange("p (c e) -> p c e", e=SW)
        k.memset(vv[:, :, P:P + 1], 1.0, m.vcm_t)
        with k.scope():
            m.QT.append(k.sb("QTb", [P, T], BF16))
            m.QT_t.append(toks(NG))
            m.KT.append(k.sb("KTb", [P, T], BF16))
            m.KT_t.append(toks(NG))
            m.q2 = [k.sb("q32", [P, GS], F32) for _ in range(2)]
            m.q2_t = toks(2)
            m.tB2 = [k.sb("tB", [P, GS], F32) for _ in range(2)]
            m.tB2_t = toks(2)
            m.atab = [k.sb("atab", [P, NCH], F32) for _ in range(2)]
            m.atab_t = toks(2)
            for h in range(self.nhg):
                self.hgrn_head(l, h)
        m.QT = m.QT[:1]
        m.QT_t = m.QT_t[:1]
        m.KT = m.KT[:1]
        m.KT_t = m.KT_t[:1]
        with k.scope():
            m.xpad = k.sb("xpad", [P, T + 4], F32)
            m.xpad_t = toks(NG)
            m.G = k.sb("G", [CH, NCH * 16], F32)
            m.lfg = k.sb("lfg", [CH, 256], F32)
            m.cum = k.sb("cum", [CH, 256], F32)
            m.e1 = k.sb("e1", [CH, 256], F32)
            m.e2 = k.sb("e2", [CH, 256], F32)
            m.abc = k.sb("abc", [P, 256], F32)
            m.wg = k.sb("wg", [P, KC * 16], BF16)
            m.bg = k.sb("bg", [CH, 16], F32)
            m.lnk = k.sb("lnk", [CH, 1], F32)
            m.fac = k.sb("fac", [CH, 2 * NCH], F32)
            m.gates_t = Tok()
            m.tE = k.sb("tE", [P, GS], F32)
            m.tE_t = Tok()
            k.memset(m.xpad[:, 0:2], 0.0, m.xpad_t)
            k.memset(m.xpad[:, T + 2:T + 4], 0.0, m.xpad_t)
            if self.nml:
                self.mlstm_gates(l)
            for h in range(self.nml):
                self.mlstm_head(l, h)

    def sig_inplace(self, ap, tok):
        k = self.k
        np_ = ap.partition_size() if callable(getattr(ap, "partition_size", None)) else P
        k.activation(ap, ap, AF.Ln, [tok, self.ct], [tok], bias=self.onesf[0:np_, 0:1])
        k.activation(ap, ap, AF.Exp, [tok], [tok], scale=-1.0)

    def issue_main(self, l, idx):
        k = self.k
        m = self.m
        if idx < 4:
            if idx >= self.nhg:
                return
            k.dma(k.pool, m.wh[:, 0:KC * 512], self.d_whg[(l * 4 + idx) * P:(l * 4 + idx + 1) * P, :], [], [m.wh_t])
        elif idx < 8:
            if idx - 4 >= self.nml:
                return
            h = idx - 4
            k.dma(k.pool, m.wh[:, 0:KC * 384], self.d_wml[(l * 4 + h) * P:(l * 4 + h + 1) * P, :], [], [m.wh_t])

    def issue_gate(self, l, idx):
        k = self.k
        m = self.m
        src = self.d_whgg if idx < 4 else self.d_wmlg
        h = idx % 4
        k.dma(k.pool, m.whg[:], src[(l * 4 + h) * P:(l * 4 + h + 1) * P, :], [], [m.whg_t])

    @staticmethod
    def interleave(gens, window=3):
        active = []
        gens = list(gens)
        while gens or active:
            while gens and len(active) < window:
                active.append(gens.pop(0))
            nxt = []
            for gen in active:
                try:
                    next(gen)
                    nxt.append(gen)
                except StopIteration:
                    pass
            active = nxt

    def proj_group(self, blk, stride, g, gate=False):
        k = self.k
        m = self.m
        w, wt = (m.whg, m.whg_t) if gate else (m.wh, m.wh_t)
        pb, pt = self.big()
        for kc in range(KC):
            k.mm(pb[:, :], w[:, kc * stride + blk * P: kc * stride + (blk + 1) * P],
                 self.xn[:, kc * T + g * GS: kc * T + (g + 1) * GS], kc == 0, kc == KC - 1,
                 [wt, self.xn_t[kc][g]], [pt])
        return pb, pt

    def v_proj(self, l, vblk, stride, bias_col0):
        k = self.k
        m = self.m
        SW = self.SW
        k.dma(k.sp, m.bv[:], self.d_bin[l, bias_col0:bias_col0 + P].partition_broadcast(CH), [], [m.bv_t])
        vv = m.vcm[:].rearrange("p (c e) -> p c e", e=SW)
        for c4 in range(8):
            pb, pt = self.big()
            for cc in range(4):
                c = c4 * 4 + cc
                for kc in range(KC):
                    k.mm(pb[0:CH, cc * P:(cc + 1) * P], self.xn[:, kc * T + c * CH: kc * T + (c + 1) * CH],
                         m.wh[:, kc * stride + vblk * P: kc * stride + (vblk + 1) * P], kc == 0, kc == KC - 1,
                         [self.xn_t[kc][c // 8], m.wh_t], [pt])
            k.tt(vv[:, c4 * 4:(c4 + 1) * 4, 0:P], pb[0:CH, :].rearrange("p (c e) -> p c e", e=P),
                 m.bv[:].unsqueeze(1).to_broadcast([CH, 4, P]), ALU.add, [pt, m.bv_t], [m.vcm_t[c4]])

    def k_transposes(self, d_src, scale_fn):
        k = self.k
        m = self.m
        for c8 in range(4):
            ptb, ptt = self.trb()
            for cc in range(8):
                c = c8 * 8 + cc
                k.tr(ptb[0:CH, cc * P:(cc + 1) * P], m.KT[d_src][:, c * CH:(c + 1) * CH], self.identb[:],
                     [m.KT_t[d_src][c // 8], self.ct], [ptt], inc=(cc == 7))
            yield c8, ptb, ptt

    def hgrn_head(self, l, h):
        k = self.k
        nc = self.nc
        m = self.m
        stride = 512
        if self.hbar:
            k.barrier()
        self.issue_gate(l, h)
        if self.stop <= 1:
            return
        self.v_proj(l, 3, stride, 1536 + h * P)
        if self.stop <= 2:
            return
        bq = PF_BIN + l * 36 + h

        def q_chain(g):
            q, qtk = m.q2[g % 2], m.q2_t[g % 2]
            pb, pt = self.proj_group(0, stride, g)
            yield
            k.activation(q[:], pb[:, :], AF.Identity, [pt, self.pfm_t], [qtk], bias=self.pfm[:, bq:bq + 1])
            k.activation(pb[:, :], pb[:, :], AF.Exp, [pt, self.pfm_t], [pt], bias=self.npfm[:, bq:bq + 1], scale=-1.0)
            yield
            k.activation(pb[:, :], pb[:, :], AF.Ln, [pt, self.ct], [pt], bias=self.onesf[:, 0:1])
            yield
            k.activation(pb[:, :], pb[:, :], AF.Exp, [pt], [pt], scale=-1.0)
            yield
            k.tt(q[:], q[:], pb[:, :], ALU.mult, [qtk, pt], [qtk])
            yield

        def d_chain(g, d):
            q, qtk = m.q2[g % 2], m.q2_t[g % 2]
            bcol = PF_BIN + l * 36 + 4 + 4 * d + h
            idx = d * 16 + l * 4 + h
            tB, tBt = m.tB2[d], m.tB2_t[d]
            pa, pat = self.proj_group(1 + d, stride, g)
            yield
            k.activation(pa[:, :], pa[:, :], AF.Exp, [pat, self.pfm_t], [pat], bias=self.npfm[:, bcol:bcol + 1],
                         scale=-1.0)
            yield
            k.activation(pa[:, :], pa[:, :], AF.Ln, [pat, self.ct], [pat], bias=self.onesf[:, 0:1])
            yield
            k.activation(pa[:, :], pa[:, :], AF.Exp, [pat], [pat], scale=-1.0)
            yield
            k.activation(tB[:], pa[:, :], AF.Ln, [pat, self.lb_t], [tBt], scale=self.oml[:, idx:idx + 1],
                         bias=self.lb[:, idx:idx + 1])
            k.activation(pa[:, :], pa[:, :], AF.Identity, [pat, self.lb_t], [pat], scale=self.noml[:, idx:idx + 1],
                         bias=self.oml[:, idx:idx + 1])
            yield
            pc, pct = self.big()
            if d == 0:
                k.op(k.dve, lambda: nc.vector.tensor_tensor_scan(out=pc[:, :], data0=self.scanm[:], data1=tB[:],
                                                                 initial=0.0, op0=ALU.mult, op1=ALU.add),
                     [tBt, self.ct], [pct])
            else:
                k.op(k.dve, lambda: nc.vector.tensor_tensor_scan(out=pc[:, ::-1], data0=self.scanm[:],
                                                                 data1=tB[:, ::-1], initial=0.0, op0=ALU.mult,
                                                                 op1=ALU.add),
                     [tBt, self.ct], [pct])
            yield
            cv = pc[:, :].rearrange("p (c j) -> p c j", j=CH)
            edge = CH - 1 if d == 0 else 0
            k.activation(m.atab[d][:, g * 8:(g + 1) * 8], cv[:, :, edge], AF.Exp, [pct], [m.atab_t[d]])
            k.activation(tB[:], pc[:, :], AF.Exp, [pct], [tBt])
            yield
            k.tt(m.QT[d][:, g * GS:(g + 1) * GS], q[:], tB[:], ALU.mult, [qtk, tBt], [m.QT_t[d][g]])
            yield
            k.activation(tB[:], pc[:, :], AF.Exp, [pct], [tBt], scale=-1.0)
            yield
            k.tt(m.KT[d][:, g * GS:(g + 1) * GS], pa[:, :], tB[:], ALU.mult, [pat, tBt], [m.KT_t[d][g]])
            yield

        self.interleave([q_chain(0)], window=1)
        for g in range(NG):
            wave = [d_chain(g, 0), d_chain(g, 1)]
            if g + 1 < NG:
                wave.append(q_chain(g + 1))
            self.interleave(wave, window=3)
        self.issue_main(l, h + 1)
        if self.stop <= 3:
            return
        for d in range(2):
            for c8, ptb, ptt in self.k_transposes(d, None):
                k.copy(m.ktm[d][0:CH, c8 * 1024:(c8 + 1) * 1024], ptb[0:CH, :], [ptt], [m.ktm_t[d][c8]],
                       e=(k.act if c8 % 2 else k.dve))
        if self.do_scan:
            self.scan_head(P, False, lambda d, c: (m.atab[d][:, c:c + 1], m.atab_t[d]), None)
        if self.do_post:
            self.post_head(l, h, h, False, 0, 128, AF.Silu, PF_BIN + l * 36 + 16 + h, PF_HGN + l * 4 + h)

    def mlstm_gates(self, l):
        k = self.k
        nc = self.nc
        m = self.m
        gt = m.gates_t
        k.dma(k.pool, m.wg[:], self.d_wgate[l * P:(l + 1) * P, :], [], [gt])
        k.dma(k.sp, m.bg[:], self.d_bin[l, 4608:4624].partition_broadcast(CH), [], [gt])
        k.memset(m.lnk[:], -0.5 * float(np.log(128.0)), [gt])
        pb, pt = self.big()
        for c in range(NCH):
            for kc in range(KC):
                k.mm(pb[0:CH, c * 16:(c + 1) * 16], self.xn[:, kc * T + c * CH: kc * T + (c + 1) * CH],
                     m.wg[:, kc * 16:(kc + 1) * 16], kc == 0, kc == KC - 1, [gt, self.xn_t[kc][c // 8]], [pt])
        Gv = m.G[:].rearrange("p (c j) -> p c j", j=16)
        k.tt(Gv, pb[0:CH, :].rearrange("p (c j) -> p c j", j=16), m.bg[:].unsqueeze(1).to_broadcast([CH, NCH, 16]),
             ALU.add, [pt, gt], [gt])
        lv = m.lfg[:].rearrange("p (d c h) -> p d c h", d=2, h=4)
        for d in range(2):
            k.activation(lv[:, d], Gv[:, :, 8 + 4 * d:12 + 4 * d], AF.Exp, [gt], [gt], scale=-1.0)
        k.ts(m.lfg[:], m.lfg[:], 1.0, None, ALU.add, None, [gt], [gt])
        k.activation(m.lfg[:], m.lfg[:], AF.Ln, [gt], [gt])
        k.ts(m.lfg[:], m.lfg[:], -1.0, None, ALU.mult, None, [gt], [gt])
        pb, pt = self.big()
        k.mm(pb[0:CH, 0:128], self.maskF[:], m.lfg[:, 0:128], True, True, [gt, self.ct], [pt])
        k.mm(pb[0:CH, 128:256], self.maskB[:], m.lfg[:, 128:256], True, True, [gt, self.ct], [pt])
        k.copy(m.cum[:], pb[0:CH, 0:256], [pt], [gt])
        k.activation(m.e1[:], m.cum[:], AF.Exp, [gt], [gt])
        e2v = m.e2[:].rearrange("p (d c h) -> p d c h", d=2, h=4)
        cumv = m.cum[:].rearrange("p (d c h) -> p d c h", d=2, h=4)
        for d in range(2):
            k.tt(e2v[:, d], Gv[:, :, 4 * d:4 * d + 4], cumv[:, d], ALU.subtract, [gt], [gt])
        k.activation(m.e2[:], m.e2[:], AF.Exp, [gt], [gt], bias=m.lnk[:, 0:1])
        pb, pt = self.big()
        k.mm(pb[:, 0:256], self.onesf[0:CH, :], m.lfg[:], True, True, [gt, self.ct], [pt])
        k.activation(m.abc[:], pb[:, 0:256], AF.Exp, [pt], [gt])

    def conv_proj(self, l, blk, stride, cidx, bias_col, dst, dst_t):
        k = self.k
        m = self.m
        for g in range(NG):
            pb, pt = self.proj_group(blk, stride, g)
            k.activation(m.xpad[:, 2 + g * GS: 2 + (g + 1) * GS], pb[:, :], AF.Identity, [pt, self.pfm_t], [m.xpad_t[g]],
                         bias=self.pfm[:, bias_col:bias_col + 1])
        wcol = PF_CW + l * 40 + cidx * 5
        cb = PF_CB + l * 8 + cidx

        def chain(g):
            rd = [m.xpad_t[gg] for gg in range(max(0, g - 1), min(NG, g + 2))]
            pa, pat = self.big()
            k.ts(pa[:, :], m.xpad[:, g * GS: g * GS + GS], self.pfm[:, wcol:wcol + 1], self.pfm[:, cb:cb + 1], ALU.mult,
                 ALU.add, rd + [self.pfm_t], [pat])
            yield
            for j in range(1, 5):
                k.stt(pa[:, :], m.xpad[:, g * GS + j: g * GS + j + GS], self.pfm[:, wcol + j:wcol + j + 1], pa[:, :],
                      ALU.mult, ALU.add, rd + [pat, self.pfm_t], [pat])
                yield
            k.activation(m.tE[:], pa[:, :], AF.Exp, [pat], [m.tE_t], scale=-1.0)
            k.activation(m.tE[:], m.tE[:], AF.Ln, [m.tE_t, self.ct], [m.tE_t], bias=self.onesf[:, 0:1])
            k.activation(m.tE[:], m.tE[:], AF.Exp, [m.tE_t], [m.tE_t], scale=-1.0)
            k.tt(dst[:, g * GS:(g + 1) * GS], pa[:, :], m.tE[:], ALU.mult, [pat, m.tE_t], [dst_t[g]])
            yield

        self.interleave([chain(g) for g in range(NG)], window=2)

    def mlstm_head(self, l, h):
        k = self.k
        m = self.m
        stride = 384
        if self.hbar:
            k.barrier()
        self.issue_gate(l, 4 + h)
        self.v_proj(l, 2, stride, 3584 + h * P)
        self.conv_proj(l, 0, stride, h, PF_BIN + l * 36 + 20 + h, m.QT[0], m.QT_t[0])
        self.conv_proj(l, 1, stride, 4 + h, PF_BIN + l * 36 + 24 + h, m.KT[0], m.KT_t[0])
        self.issue_main(l, 4 + h + 1)
        for c8, ptb, ptt in self.k_transposes(0, None):
            for d in range(2):
                e2d = m.e2[:, d * 128:(d + 1) * 128].rearrange("p (c h) -> p c h", h=4)[:, c8 * 8:(c8 + 1) * 8, h:h + 1]
                k.tt(m.ktm[d][0:CH, c8 * 1024:(c8 + 1) * 1024].rearrange("p (c e) -> p c e", e=P),
                     ptb[0:CH, :].rearrange("p (c e) -> p c e", e=P), e2d.to_broadcast([CH, 8, P]), ALU.mult,
                     [ptt, m.gates_t], [m.ktm_t[d][c8]])
        if self.do_scan:
            self.scan_head(P + 1, True,
                           lambda d, c: (m.abc[:, d * 128 + c * 4 + h: d * 128 + c * 4 + h + 1], m.gates_t),
                           lambda d, c: m.e2[0:CH, d * 128 + c * 4 + h: d * 128 + c * 4 + h + 1])
        if self.do_post:
            self.post_head(l, h, 4 + h, True, 0, 128, AF.Sigmoid, PF_BIN + l * 36 + 32 + h, PF_MLN + l * 4 + h)

    def scan_head(self, E, mlstm, a_fn, e2_fn):
        k = self.k
        m = self.m
        SW = self.SW

        def front(i, d):
            dq = 0 if mlstm else d
            c = i if d == 0 else NCH - 1 - i
            g = c // 8
            vslot = m.vcm[0:CH, c * SW: c * SW + E]
            vt = m.vcm_t[c // 4]
            qsl = m.QT[dq][:, c * CH:(c + 1) * CH]
            qt = m.QT_t[dq][g]
            pbk, pst = self.big()
            psl = pbk[0:CH, 0:CH]
            k.mm(psl, m.KT[dq][:, c * CH:(c + 1) * CH], qsl, True, True, [m.KT_t[dq][g], qt], [pst])
            sc, sct = m.sc[m.sc_rr % 4], m.sc_t[m.sc_rr % 4]
            m.sc_rr += 1
            mask = self.maskF if d == 0 else self.maskB
            if mlstm:
                k.stt(sc[:], psl, e2_fn(d, c), mask[:], ALU.mult, ALU.mult, [pst, m.gates_t, self.ct], [sct])
            else:
                k.tt(sc[:], psl, mask[:], ALU.mult, [pst, self.ct], [sct])
            pbk, put = self.big()
            pul = pbk[:, 0:E]
            k.mm(pul, m.ktm[d][0:CH, c * P:(c + 1) * P], vslot, True, True, [m.ktm_t[d][c // 8], vt], [put])
            if i == 0:
                k.copy(m.R[d][:, 0:E], pul, [put], [m.R_t[d]])
            else:
                cprev = c - 1 if d == 0 else c + 1
                ap_prev, atok = a_fn(d, cprev)
                k.stt(m.R[d][:, 0:E], m.R[d][:, 0:E], ap_prev, pul, ALU.mult, ALU.add, [put, m.R_t[d], atok],
                      [m.R_t[d]])
            if i < NCH - 1:
                ap_c, atok = a_fn(d, c)
                k.activation(m.S[d][(i + 1) % 3][:, 0:E], m.R[d][:, 0:E], AF.Identity, [m.R_t[d], atok],
                             [m.S_t[d][(i + 1) % 3]], scale=ap_c)
            return (c, g, vslot, vt, qsl, qt, sc, sct)

        def back(i, d, st):
            c, g, vslot, vt, qsl, qt, sc, sct = st
            pbk, pot = self.big()
            pol = pbk[0:CH, 0:E]
            if i > 0:
                k.mm(pol, qsl, m.S[d][i % 3][:, 0:E], True, False, [qt, m.S_t[d][i % 3]], [pot])
            k.mm(pol, sc[:], vslot, i == 0, True, [sct, vt], [pot])
            dst = m.stage[d][0:CH, c * SW: c * SW + E]
            k.copy(dst, pol, [pot], [m.stage_t[d][c]], e=k.act)

        pending = []
        for i in range(NCH):
            for d in range(2):
                pending.append((i, d, front(i, d)))
                if len(pending) > 2:
                    back(*pending.pop(0))
        while pending:
            back(*pending.pop(0))

    def post_head(self, l, h, hidx, mlstm, gblk, stride, gfunc, gbias_col, gain_col):
        k = self.k
        nc = self.nc
        m = self.m
        SW = self.SW
        sA = m.stage[0][:].rearrange("p (c e) -> p c e", e=SW)
        sB = m.stage[1][:].rearrange("p (c e) -> p c e", e=SW)
        tA_all = m.stage_t[0]
        tB_all = m.stage_t[1]
        st = m.small_t
        if mlstm:
            for d in range(2):
                sv = sA if d == 0 else sB
                tv = tA_all if d == 0 else tB_all
                e1d = m.e1[:, d * 128:(d + 1) * 128].rearrange("p (c h) -> p c h", h=4)[:, :, h]
                t1 = m.fac[:, d * NCH:(d + 1) * NCH]
                k.tt(t1, sv[:, :, P], e1d, ALU.mult, tv + [m.gates_t], [st])
                k.activation(t1, t1, AF.Abs, [st], [st])
                k.ts(t1, t1, 1.0, None, ALU.max, None, [st], [st])
                k.op(k.dve, lambda: nc.vector.reciprocal(out=t1, in_=t1), [st], [st])
                k.tt(t1, t1, e1d, ALU.mult, [st, m.gates_t], [st])
                k.tt(sv[:, :, 0:P], sv[:, :, 0:P], t1.unsqueeze(2).to_broadcast([CH, NCH, P]), ALU.mult, tv + [st], tv)
        k.tt(sA[:, :, 0:P], sA[:, :, 0:P], sB[:, :, 0:P], ALU.add, tA_all + tB_all, tA_all)
        k.tt(sB[:, :, 0:P], sA[:, :, 0:P], sA[:, :, 0:P], ALU.mult, tA_all, tB_all)
        k.op(k.dve, lambda: nc.vector.tensor_reduce(out=m.ms[:], in_=sB[:, :, 0:P], axis=AX.X, op=ALU.add), tB_all, [st])
        k.activation(m.ms[:], m.ms[:], AF.Ln, [st, self.ct], [st], scale=1.0 / P, bias=self.epsc[0:CH, 0:1])
        k.activation(m.ms[:], m.ms[:], AF.Exp, [st], [st], scale=-0.5)
        hn = m.ktm[0]
        k.tt(hn[:].rearrange("p (c e) -> p c e", e=P), sA[:, :, 0:P], m.ms[:].unsqueeze(2).to_broadcast([CH, NCH, P]),
             ALU.mult, tA_all + [st], m.ktm_t[0])
        for g in range(NG):
            pb, pt = self.proj_group(gblk, stride, g, gate=True)
            gt, gtt = m.gt[g % 2], m.gt_t[g % 2]
            k.activation(gt[:], pb[:, :], AF.Exp, [pt, self.pfm_t], [gtt], bias=self.npfm[:, gbias_col:gbias_col + 1],
                         scale=-1.0)
            if gfunc == AF.Silu:
                k.activation(m.tA[:], pb[:, :], AF.Identity, [pt, self.pfm_t], [m.tA_t],
                             bias=self.pfm[:, gbias_col:gbias_col + 1])
            self.sig_inplace(gt[:], gtt)
            if gfunc == AF.Silu:
                k.tt(gt[:], gt[:], m.tA[:], ALU.mult, [gtt, m.tA_t], [gtt])
            ptb, ptt = self.trb()
            for cc in range(8):
                c = g * 8 + cc
                k.tr(ptb[:, cc * CH:(cc + 1) * CH], hn[0:CH, c * P:(c + 1) * P], self.identb[0:CH, 0:CH],
                     [m.ktm_t[0][c // 8], self.ct], [ptt], inc=(cc == 7))
            mh, mht = m.mixh[g % 2], m.mixh_t[g % 2]
            k.stt(mh[:], ptb[:, 0:GS], self.pfm[:, gain_col:gain_col + 1], gt[:], ALU.mult, ALU.mult,
                  [ptt, gtt, self.pfm_t], [mht])
            k.dma(k.sp, self.d_mix[hidx * P:(hidx + 1) * P, g * GS:(g + 1) * GS], mh[:], [mht], [self.mix_t[hidx][g]])

    def resid_add(self, pb, pt, n, g):
        k = self.k
        sl = self.xT[:, n * T + g * GS: n * T + (g + 1) * GS]
        k.tt(sl, pb[:, 0:GS], sl, ALU.add, [pt, self.xT_t[n][g]], [self.xT_t[n][g]])

    def mlp_phase(self, l):
        k = self.k
        wup = [k.sb("wup", [P, KC * 512], BF16) for _ in range(2)]
        wup_t = toks(2)
        wdn = [k.sb("wdn", [P, 32 * P], BF16) for _ in range(2)]
        wdn_t = toks(2)
        rt = [k.sb("rtmp", [P, GS], F32) for _ in range(2)]
        rt_t = toks(2)
        rr = 0
        k.dma(k.pool, wup[0][:], self.d_wup[(l * 8) * P:(l * 8 + 1) * P, :], [], [wup_t[0]])
        k.dma(k.pool, wup[1][:], self.d_wup[(l * 8 + 1) * P:(l * 8 + 2) * P, :], [], [wup_t[1]])
        with k.scope():
            self.rmsnorm_fm(self.xT, self.xT_t, T, PF_GN + l * 32 + 24, self.xn, self.xn_t)
        hT = k.sb("hT", [P, 32 * 1024], BF16)
        hT_t = toks2(32, 2)

        def ld_up(fg):
            k.dma(k.pool, wup[fg % 2][:], self.d_wup[(l * 8 + fg) * P:(l * 8 + fg + 1) * P, :], [], [wup_t[fg % 2]])

        def ld_dn(n):
            k.dma(k.pool, wdn[n % 2][:], self.d_wdn[(l * 8 + n) * P:(l * 8 + n + 1) * P, :], [], [wdn_t[n % 2]])

        for hf in range(2):
            if hf > 0:
                ld_up(0)
            for fg in range(8):
                if fg + 1 < 8 and not (hf == 0 and fg == 0):
                    ld_up(fg + 1)
                else:
                    ld_dn(0)
                wu, wut = wup[fg % 2], wup_t[fg % 2]
                for fc in range(4):
                    for g2 in range(2):
                        g = hf * 2 + g2
                        pb, pt = self.big()
                        for kc in range(KC):
                            k.mm(pb[:, :], wu[:, kc * 512 + fc * P: kc * 512 + (fc + 1) * P],
                                 self.xn[:, kc * T + g * GS: kc * T + (g + 1) * GS], kc == 0, kc == KC - 1,
                                 [wut, self.xn_t[kc][g]], [pt])
                        r, rtk = rt[rr % 2], rt_t[rr % 2]
                        rr += 1
                        k.activation(r[:], pb[:], AF.Relu, [pt], [rtk])
                        f = fg * 4 + fc
                        k.tt(hT[:, f * 1024 + g2 * GS: f * 1024 + (g2 + 1) * GS], r[:], r[:], ALU.mult, [rtk],
                             [hT_t[f][g2]])
            for n in range(8):
                if n + 1 < 8:
                    ld_dn(n + 1)
                wd, wdt = wdn[n % 2], wdn_t[n % 2]
                for g2 in range(2):
                    g = hf * 2 + g2
                    pb, pt = self.big()
                    for f in range(32):
                        k.mm(pb[:, :], wd[:, f * P:(f + 1) * P], hT[:, f * 1024 + g2 * GS: f * 1024 + (g2 + 1) * GS],
                             f == 0, f == 31, [wdt, hT_t[f][g2]], [pt])
                    self.resid_add(pb, pt, n, g)

    def wout_phase(self, l):
        k = self.k
        wo = k.sb("wout", [P, KC * D], BF16)
        wo_t = Tok()
        mb = [k.sb("mixb", [P, KC * GS], BF16) for _ in range(2)]
        mb_t = toks(2)
        k.dma(k.pool, wo[:], self.d_wout[l * P:(l + 1) * P, :], [], [wo_t])
        dmv = self.d_mix.rearrange("(c p) t -> p c t", p=P)

        act = list(range(self.nhg)) + [4 + i for i in range(self.nml)]
        full = len(act) == 8

        def ld(g):
            if full:
                k.dma(k.sp, mb[g % 2][:].rearrange("p (c t) -> p c t", c=KC), dmv[:, :, g * GS:(g + 1) * GS],
                      [self.mix_t[c][g] for c in range(8)], [mb_t[g % 2]])
            else:
                for c in act:
                    k.dma(k.sp, mb[g % 2][:, c * GS:(c + 1) * GS], self.d_mix[c * P:(c + 1) * P, g * GS:(g + 1) * GS],
                          [self.mix_t[c][g]], [mb_t[g % 2]])

        ld(0)
        for g in range(NG):
            if g + 1 < NG:
                ld(g + 1)
            for n in range(8):
                pb, pt = self.big()
                for kc in act:
                    k.mm(pb[:, :], wo[:, kc * D + n * P: kc * D + (n + 1) * P], mb[g % 2][:, kc * GS:(kc + 1) * GS],
                         kc == act[0], kc == act[-1], [wo_t, mb_t[g % 2]], [pt])
                self.resid_add(pb, pt, n, g)

    def xattn_phase(self, l):
        k = self.k
        nc = self.nc
        memn = k.sb("memn", [P, KC * MEM], BF16)
        memn_t = toks2(KC, 1)
        kT = k.sb("kT", [P, 8 * MEM], BF16)
        kT_t = Tok()
        vsb = k.sb("vsb", [P, 2 * D], BF16)
        vsb_t = Tok()
        wq = k.sb("wq", [P, KC * D], BF16)
        wq_t = Tok()
        with k.scope():
            wkv = k.sb("wkv", [P, KC * 2 * D], BF16)
            wkv_t = Tok()
            k.dma(k.pool, wkv[:], self.d_wxkv[l * P:(l + 1) * P, :], [], [wkv_t])
            k.dma(k.pool, wq[:], self.d_wxq[l * P:(l + 1) * P, :], [], [wq_t])
            with k.scope():
                self.rmsnorm_fm(self.xT, self.xT_t, T, PF_GN + l * 32 + 8, self.xn, self.xn_t)
            memT = k.sb("memT", [P, KC * MEM], F32)
            memT_t = toks2(KC, 1)
            self.load_tm_to_fm(self.d_mem, MEM // P, memT, MEM,
                               lambda i, half: [memT_t[c][0] for c in range(half * 4, half * 4 + 4)])
            self.rmsnorm_fm(memT, memT_t, MEM, PF_GN + l * 32 + 16, memn, memn_t)
            for n in range(8):
                pb, pt = self.big()
                for kc in range(KC):
                    k.mm(pb[:, 0:MEM], wkv[:, kc * 2 * D + n * P: kc * 2 * D + (n + 1) * P],
                         memn[:, kc * MEM:(kc + 1) * MEM], kc == 0, kc == KC - 1, [wkv_t, memn_t[kc][0]], [pt])
                k.copy(kT[:, n * MEM:(n + 1) * MEM], pb[:, 0:MEM], [pt], [kT_t], e=k.act)
            for jc in range(2):
                for ng in range(2):
                    pb, pt = self.big()
                    for kc in range(KC):
                        k.mm(pb[:, :], memn[:, kc * MEM + jc * P: kc * MEM + (jc + 1) * P],
                             wkv[:, kc * 2 * D + D + ng * GS: kc * 2 * D + D + (ng + 1) * GS], kc == 0, kc == KC - 1,
                             [wkv_t, memn_t[kc][0]], [pt])
                    k.copy(vsb[:, jc * D + ng * GS: jc * D + (ng + 1) * GS], pb[:, :], [pt], [vsb_t], e=k.act)
        wo = k.sb("wxo", [P, KC * D], BF16)
        wo_t = Tok()
        k.dma(k.pool, wo[:], self.d_wxo[l * P:(l + 1) * P, :], [], [wo_t])
        qT = [k.sb("qT", [P, 8 * GS], BF16) for _ in range(2)]
        qT_t = toks(2)
        oT = [k.sb("oT", [P, 8 * GS], BF16) for _ in range(2)]
        oT_t = toks(2)
        pT = [k.sb("pT", [P, 2 * GS], BF16) for _ in range(2)]
        pT_t = toks(2)
        rc = [k.sb("rc", [P, GS], F32) for _ in range(2)]
        rc_t = toks(2)
        pr = 0
        for g in range(NG):
            q, qt = qT[g % 2], qT_t[g % 2]
            o, ot = oT[g % 2], oT_t[g % 2]
            for n in range(8):
                pb, pt = self.big()
                for kc in range(KC):
                    k.mm(pb[:, :], wq[:, kc * D + n * P: kc * D + (n + 1) * P],
                         self.xn[:, kc * T + g * GS: kc * T + (g + 1) * GS], kc == 0, kc == KC - 1,
                         [wq_t, self.xn_t[kc][g]], [pt])
                k.copy(q[:, n * GS:(n + 1) * GS], pb[:, :], [pt], [qt], e=k.act)
            for h in range(4):
                p_, ptk = pT[pr % 2], pT_t[pr % 2]
                r_, rtk = rc[pr % 2], rc_t[pr % 2]
                pr += 1
                for jc in range(2):
                    pb, pt = self.big()
                    for hc in range(2):
                        n = h * 2 + hc
                        k.mm(pb[:, :], kT[:, n * MEM + jc * P: n * MEM + (jc + 1) * P], q[:, n * GS:(n + 1) * GS],
                             hc == 0, hc == 1, [kT_t, qt], [pt])
                    k.activation(p_[:, jc * GS:(jc + 1) * GS], pb[:, :], AF.Exp, [pt], [ptk], scale=1.0 / 16.0)
                pb, pt = self.big()
                for jc in range(2):
                    k.mm(pb[:, :], self.onesb[:], p_[:, jc * GS:(jc + 1) * GS], jc == 0, jc == 1, [self.ct, ptk], [pt])
                k.activation(r_[:], pb[:, :], AF.Ln, [pt], [rtk])
                k.activation(r_[:], r_[:], AF.Exp, [rtk], [rtk], scale=-1.0)
                for hc in range(2):
                    n = h * 2 + hc
                    pb, pt = self.big()
                    for jc in range(2):
                        k.mm(pb[:, :], vsb[:, jc * D + n * P: jc * D + (n + 1) * P], p_[:, jc * GS:(jc + 1) * GS],
                             jc == 0, jc == 1, [vsb_t, ptk], [pt])
                    k.tt(o[:, n * GS:(n + 1) * GS], pb[:, :], r_[:], ALU.mult, [pt, rtk], [ot])
            for n in range(8):
                pb, pt = self.big()
                for kc in range(KC):
                    k.mm(pb[:, :], wo[:, kc * D + n * P: kc * D + (n + 1) * P], o[:, kc * GS:(kc + 1) * GS],
                         kc == 0, kc == KC - 1, [wo_t, ot], [pt])
                self.resid_add(pb, pt, n, g)


def fm(v):
    v = np.asarray(v, np.float32)
    n = v.shape[-1] // P
    lead = v.shape[:-1]
    return np.moveaxis(v.reshape(lead + (n, P)), -1, 0)


def wtile(w):
    kk, n = w.shape
    kc = kk // P
    return np.ascontiguousarray(w.reshape(kc, P, n).transpose(1, 0, 2).reshape(P, kc * n))


def prep_shared(inp):
    f32 = np.float32
    pfm = np.zeros((P, NPF), f32)
    norms = np.stack([inp["norm_mix"], inp["norm_xattn"], inp["norm_mem"], inp["norm_mlp"]], axis=1)
    pfm[:, PF_GN:PF_GN + L * 32] = fm(norms).reshape(P, L * 32)
    pfm[:, PF_GN + L * 32:PF_GN + L * 32 + 8] = fm(inp["norm_final"]).reshape(P, 8)
    pfm[:, PF_BIN:PF_BIN + L * 36] = fm(inp["b_in"][:, :36 * P]).reshape(P, L * 36)
    pfm[:, PF_LBL:PF_LBL + 32] = fm(inp["hgrn_lb_logits"]).reshape(P, 32)
    pfm[:, PF_HGN:PF_HGN + L * 4] = fm(inp["hgrn_norm"]).reshape(P, L * 4)
    pfm[:, PF_MLN:PF_MLN + L * 4] = fm(inp["mlstm_norm"]).reshape(P, L * 4)
    cw = fm(inp["mlstm_conv_w"])
    pfm[:, PF_CW:PF_CW + L * 40] = cw.transpose(0, 1, 3, 2).reshape(P, L * 40)
    pfm[:, PF_CB:PF_CB + L * 8] = fm(inp["mlstm_conv_b"]).reshape(P, L * 8)

    w_in = np.asarray(inp["w_in"], f32)
    whg = np.empty((L, 4, P, KC * 512), f32)
    whgg = np.empty((L, 4, P, KC * 128), f32)
    wml = np.empty((L, 4, P, KC * 384), f32)
    wmlg = np.empty((L, 4, P, KC * 128), f32)
    wgate = np.empty((L, P, KC * 16), f32)
    for l in range(L):
        for h in range(4):
            cols = np.concatenate([np.arange(b * 512 + h * P, b * 512 + (h + 1) * P) for b in (0, 1, 2, 3)])
            whg[l, h] = wtile(w_in[l][:, cols])
            whgg[l, h] = wtile(w_in[l][:, 4 * 512 + h * P:4 * 512 + (h + 1) * P])
            cols = np.concatenate([np.arange(2560 + b * 512 + h * P, 2560 + b * 512 + (h + 1) * P) for b in (0, 1, 2)])
            wml[l, h] = wtile(w_in[l][:, cols])
            wmlg[l, h] = wtile(w_in[l][:, 2560 + 3 * 512 + h * P:2560 + 3 * 512 + (h + 1) * P])
        wgate[l] = wtile(w_in[l][:, 4608:4624])
    w_up = np.asarray(inp["w_up"], f32)
    w_dn = np.asarray(inp["w_down"], f32)
    wup = np.empty((L, 8, P, KC * 512), f32)
    wdn = np.empty((L, 8, P, 32 * P), f32)
    for l in range(L):
        for g in range(8):
            wup[l, g] = wtile(w_up[l][:, g * 512:(g + 1) * 512])
            wdn[l, g] = wtile(w_dn[l][:, g * P:(g + 1) * P])
    sh = {
        "pfm": pfm,
        "b_in": np.ascontiguousarray(inp["b_in"], f32),
        "w_hg": whg.reshape(L * 4 * P, KC * 512),
        "w_hgg": whgg.reshape(L * 4 * P, KC * 128),
        "w_ml": wml.reshape(L * 4 * P, KC * 384),
        "w_mlg": wmlg.reshape(L * 4 * P, KC * 128),
        "w_gate": wgate.reshape(L * P, KC * 16),
        "w_out": np.stack([wtile(inp["w_out"][l]) for l in range(L)]).reshape(L * P, KC * D),
        "w_xq": np.stack([wtile(inp["w_xq"][l]) for l in range(L)]).reshape(L * P, KC * D),
        "w_xo": np.stack([wtile(inp["w_xo"][l]) for l in range(L)]).reshape(L * P, KC * D),
        "w_xkv": np.stack([wtile(inp["w_xkv"][l]) for l in range(L)]).reshape(L * P, KC * 2 * D),
        "w_up": wup.reshape(L * 8 * P, KC * 512),
        "w_down": wdn.reshape(L * 8 * P, 32 * P),
    }
    return sh


_CACHE = {}


def get_prog(**kw):
    key = repr(sorted(kw.items()))
    if key not in _CACHE:
        p = Prog(**kw)
        p.build()
        _CACHE[key] = p
    return _CACHE[key]


def run(inputs, cores=NCORES, **kw):
    prog = get_prog(**kw)
    sh = prep_shared(inputs)
    x = np.asarray(inputs["x"], np.float32)
    mem = np.asarray(inputs["mem"], np.float32)
    in_maps = []
    for b in range(cores):
        m = dict(sh)
        m["x"] = np.ascontiguousarray(x[b])
        m["mem"] = np.ascontiguousarray(mem[b])
        in_maps.append(m)
    res = run_bass_kernel_spmd(prog.nc, in_maps, core_ids=list(range(cores)))
    return prog, res


def kernel(**inputs):
    prog, res = run(inputs)
    return np.stack([np.asarray(r["out"], np.float32) for r in res.results], axis=0)
```
